# Optimizing a Trainium2 kernel written in Bass

```python
import jax, jax.numpy as jnp
from jax import lax
import numpy as np

D_MODEL = 1024
BATCH = 4
SEQ = 8192
DEPTH = 1

RWKV_HEAD_DIM = 64
RWKV_WIDTH = D_MODEL // 2
RWKV_HEADS = RWKV_WIDTH // RWKV_HEAD_DIM
DECAY_LORA = 64
ICLR_LORA = 64
GATE_LORA = 160
SB_HEAD_DIM = 64
SB_WIDTH = D_MODEL // 2
SB_HEADS = SB_WIDTH // SB_HEAD_DIM
Q_BLOCK = 128
D_FF = 2688
CONV_WIDTH = 3
N_BRANCHES = 2
RMS_EPS = 1e-6
GN_EPS = 64e-5
L2_EPS = 1e-12

RWKV_IN = 3 * RWKV_WIDTH + DECAY_LORA + ICLR_LORA + GATE_LORA
SB_IN = 3 * SB_WIDTH
GATE_IN = N_BRANCHES * D_MODEL
IN_WIDTH = RWKV_IN + SB_IN + GATE_IN
N_MOD = 6

kernel_name = "hybrid_rwkv7_stickbreak_convffn_block"


def split_last(t, sizes):
    idx = [int(i) for i in np.cumsum(sizes)[:-1]]
    return jnp.split(t, idx, axis=-1)


def rms_norm(t, g):
    tf = t.astype(jnp.float32)
    y = tf * lax.rsqrt(jnp.mean(tf * tf, axis=-1, keepdims=True) + RMS_EPS)
    return (y * g.astype(jnp.float32)).astype(t.dtype)


def token_shift(f):
    return jnp.pad(f, ((0, 0), (1, 0), (0, 0)))[:, :-1]


def rwkv7_step(state, inp):
    r_t, w_t, k_t, v_t, kk_t, b_t = inp
    sa = jnp.einsum('bhvk,bhk->bhv', state, -kk_t)
    state = (state * w_t[:, :, None, :]
             + sa[..., None] * b_t[:, :, None, :]
             + v_t[..., None] * k_t[:, :, None, :])
    y_t = jnp.einsum('bhvk,bhk->bhv', state, r_t)
    return state, y_t


def rwkv7_time_mix(f, mu, w0, w2, a0, a2, g2, k_k, k_a, r_k, lnx_g, lnx_b):
    B, S, _ = f.shape
    H, N = RWKV_HEADS, RWKV_HEAD_DIM
    f32 = jnp.float32
    f = f + (token_shift(f) - f) * mu
    r, k, v, wd, ad, gd = split_last(
        f, (RWKV_WIDTH, RWKV_WIDTH, RWKV_WIDTH, DECAY_LORA, ICLR_LORA, GATE_LORA))
    w_log = -jax.nn.softplus(-(w0 + jnp.tanh(wd) @ w2)) - 0.5
    decay = jnp.exp(-jnp.exp(w_log.astype(f32)))
    a = jax.nn.sigmoid(a0 + ad @ a2)
    g = jax.nn.sigmoid(gd) @ g2
    kk = (k * k_k).reshape(B, S, H, N).astype(f32)
    kk = kk / jnp.maximum(jnp.sqrt(jnp.sum(kk * kk, axis=-1, keepdims=True)), L2_EPS)
    k = k * (1 + (a - 1) * k_a)

    def heads(t):
        return t.reshape(B, S, H, N).astype(f32)

    r_h, k_h, v_h, a_h, w_h = heads(r), heads(k), heads(v), heads(a), heads(decay)
    b_h = kk * a_h
    xs = tuple(jnp.moveaxis(t, 1, 0) for t in (r_h, w_h, k_h, v_h, kk, b_h))
    state0 = jnp.zeros((B, H, N, N), f32)
    _, y = lax.scan(rwkv7_step, state0, xs)
    y = jnp.moveaxis(y, 0, 1)
    mean = jnp.mean(y, axis=-1, keepdims=True)
    var = jnp.mean(jnp.square(y - mean), axis=-1, keepdims=True)
    y = ((y - mean) * lax.rsqrt(var + GN_EPS)).reshape(B, S, RWKV_WIDTH)
    y = y * lnx_g.astype(f32) + lnx_b.astype(f32)
    bonus = jnp.sum(r_h * k_h * r_k.astype(f32), axis=-1, keepdims=True) * v_h
    y = (y + bonus.reshape(B, S, RWKV_WIDTH)) * g.astype(f32)
    return y.astype(f.dtype)


def stick_breaking_attention(q, k, v):
    B, S, H, Dh = q.shape
    nb = S // Q_BLOCK
    scale = Dh ** -0.5
    qb = q.reshape(B, nb, Q_BLOCK, H, Dh).transpose(1, 0, 3, 2, 4)
    kf = k.transpose(0, 2, 1, 3)
    vf = v.transpose(0, 2, 1, 3)
    key_pos = jnp.arange(S)

    def block(args):
        q_blk, blk = args
        q_pos = blk * Q_BLOCK + jnp.arange(Q_BLOCK)
        z = jnp.einsum('bhqd,bhkd->bhqk', q_blk, kf).astype(jnp.float32) * scale
        causal = key_pos[None, :] < q_pos[:, None]
        log_beta = jax.nn.log_sigmoid(z)
        log_1mb = jnp.where(causal, jax.nn.log_sigmoid(-z), 0.0)
        survive = lax.cumsum(log_1mb, axis=3, reverse=True) - log_1mb
        A = jnp.where(causal, jnp.exp(log_beta + survive), 0.0)
        return jnp.einsum('bhqk,bhkd->bhqd', A.astype(vf.dtype), vf)

    out = lax.map(block, (qb, jnp.arange(nb)))
    return out.transpose(1, 0, 3, 2, 4).reshape(B, S, H * Dh)


def causal_depthwise_conv(u, w, b):
    C = u.shape[-1]
    y = lax.conv_general_dilated(
        u, w[:, None, :].astype(u.dtype), window_strides=(1,),
        padding=((CONV_WIDTH - 1, 0),),
        dimension_numbers=('NWC', 'WIO', 'NWC'),
        feature_group_count=C)
    return y + b


def setup_inputs(seed: int = 0) -> dict:
    key = jax.random.key(seed)
    ks = jax.random.split(key, 32)
    L, D = DEPTH, D_MODEL

    def nrm(k, shape, s):
        return jax.random.normal(k, shape, jnp.float32) * s

    return {
        "x": nrm(ks[0], (BATCH, SEQ, D), 1.0),
        "c": nrm(ks[1], (BATCH, D), 1.0),
        "w_ada": nrm(ks[2], (L, D, N_MOD * D), 0.5 * D ** -0.5),
        "b_ada": nrm(ks[3], (L, N_MOD * D), 0.01),
        "norm1_g": 1.0 + nrm(ks[4], (L, D), 0.02),
        "w_in": nrm(ks[5], (L, D, IN_WIDTH), D ** -0.5),
        "b_gate": nrm(ks[6], (L, GATE_IN), 0.01),
        "rwkv_mu": jax.random.uniform(ks[7], (L, RWKV_IN), jnp.float32),
        "rwkv_w0": jax.random.uniform(ks[8], (L, RWKV_WIDTH), jnp.float32, -6.0, 0.0),
        "rwkv_w2": nrm(ks[9], (L, DECAY_LORA, RWKV_WIDTH), 0.1 * DECAY_LORA ** -0.5),
        "rwkv_a0": nrm(ks[10], (L, RWKV_WIDTH), 0.1),
        "rwkv_a2": nrm(ks[11], (L, ICLR_LORA, RWKV_WIDTH), 0.1 * ICLR_LORA ** -0.5),
        "rwkv_g2": nrm(ks[12], (L, GATE_LORA, RWKV_WIDTH), GATE_LORA ** -0.5),
        "rwkv_k_k": 0.85 + nrm(ks[13], (L, RWKV_WIDTH), 0.02),
        "rwkv_k_a": 1.0 + nrm(ks[14], (L, RWKV_WIDTH), 0.02),
        "rwkv_r_k": -0.04 + nrm(ks[15], (L, RWKV_HEADS, RWKV_HEAD_DIM), 0.02),
        "rwkv_lnx_g": 1.0 + nrm(ks[16], (L, RWKV_WIDTH), 0.02),
        "rwkv_lnx_b": nrm(ks[17], (L, RWKV_WIDTH), 0.01),
        "sb_q_g": 1.0 + nrm(ks[18], (L, SB_HEAD_DIM), 0.02),
        "sb_k_g": 1.0 + nrm(ks[19], (L, SB_HEAD_DIM), 0.02),
        "w_o_rwkv": nrm(ks[20], (L, RWKV_WIDTH, D), RWKV_WIDTH ** -0.5),
        "w_o_sb": nrm(ks[21], (L, SB_WIDTH, D), SB_WIDTH ** -0.5),
        "w_out": nrm(ks[22], (L, D, D), D ** -0.5),
        "norm2_g": 1.0 + nrm(ks[23], (L, D), 0.02),
        "w_up": nrm(ks[24], (L, D, 2 * D_FF), D ** -0.5),
        "conv_w": nrm(ks[25], (L, CONV_WIDTH, 2 * D_FF), CONV_WIDTH ** -0.5),
        "conv_b": nrm(ks[26], (L, 2 * D_FF), 0.01),
        "w_down": nrm(ks[27], (L, D_FF, D), D_FF ** -0.5),
    }


def reference(x, c, w_ada, b_ada, norm1_g, w_in, b_gate, rwkv_mu, rwkv_w0, rwkv_w2,
              rwkv_a0, rwkv_a2, rwkv_g2, rwkv_k_k, rwkv_k_a, rwkv_r_k, rwkv_lnx_g,
              rwkv_lnx_b, sb_q_g, sb_k_g, w_o_rwkv, w_o_sb, w_out, norm2_g, w_up,
              conv_w, conv_b, w_down):
    B, S, D = x.shape
    for l in range(DEPTH):
        mod = jax.nn.silu(c) @ w_ada[l] + b_ada[l]
        shift1, scale1, gate1, shift2, scale2, gate2 = jnp.split(mod[:, None, :], N_MOD, axis=-1)

        h = rms_norm(x, norm1_g[l]) * (1 + scale1) + shift1
        proj = h @ w_in[l]
        f_rwkv, f_sb, f_gate = split_last(proj, (RWKV_IN, SB_IN, GATE_IN))

        y_rwkv = rwkv7_time_mix(f_rwkv, rwkv_mu[l], rwkv_w0[l], rwkv_w2[l], rwkv_a0[l],
                                rwkv_a2[l], rwkv_g2[l], rwkv_k_k[l], rwkv_k_a[l],
                                rwkv_r_k[l], rwkv_lnx_g[l], rwkv_lnx_b[l])

        q, k, v = split_last(f_sb, (SB_WIDTH, SB_WIDTH, SB_WIDTH))
        q = rms_norm(q.reshape(B, S, SB_HEADS, SB_HEAD_DIM), sb_q_g[l])
        k = rms_norm(k.reshape(B, S, SB_HEADS, SB_HEAD_DIM), sb_k_g[l])
        v = v.reshape(B, S, SB_HEADS, SB_HEAD_DIM)
        y_sb = stick_breaking_attention(q, k, v)

        gate_rwkv, gate_sb = jnp.split(jax.nn.sigmoid(f_gate + b_gate[l]), N_BRANCHES, axis=-1)
        merged = gate_rwkv * (y_rwkv @ w_o_rwkv[l]) + gate_sb * (y_sb @ w_o_sb[l])
        x = x + gate1 * (merged @ w_out[l])

        h2 = rms_norm(x, norm2_g[l]) * (1 + scale2) + shift2
        up = causal_depthwise_conv(h2 @ w_up[l], conv_w[l], conv_b[l])
        val, gt = jnp.split(up, 2, axis=-1)
        x = x + gate2 * ((jax.nn.silu(gt) * val) @ w_down[l])
    return x
```

```python
import math
import os
from contextlib import ExitStack
import numpy as np
import concourse.bass as bass
import concourse.mybir as mybir
from concourse.bass_utils import run_bass_kernel_spmd

F32 = mybir.dt.float32
BF16 = mybir.dt.bfloat16
AF = mybir.ActivationFunctionType
ALU = mybir.AluOpType

ENGS = ("pe", "act", "dve", "pool", "sp")
D = 1024
NIN1 = 1824
NLERP = 1056
DFF = 2688
CDEC = math.exp(-0.5)
C = 64
TB = 512


class T:
    __slots__ = ("name", "w", "r", "dsem", "dcnt", "psum")

    def __init__(self, name=""):
        self.name = name
        self.psum = False
        self.w = None
        self.r = []
        self.dsem = None
        self.dcnt = 0


class Op:
    __slots__ = ("eng", "idx", "fn", "deps", "dma", "sem_tile", "dval", "sig", "cnt", "rt")

    def __init__(self, eng, idx, fn):
        self.eng = eng
        self.idx = idx
        self.fn = fn
        self.deps = []
        self.dma = False
        self.sem_tile = None
        self.dval = 0
        self.sig = False
        self.cnt = 0
        self.rt = None


class Prog:
    def __init__(self, nc):
        self.nc = nc
        self.ops = {e: [] for e in ENGS}
        self.dma_tiles = []
        self.last_dma = {}
        self.mute = False

    def op(self, eng, fn, reads=(), writes=(), dma=False, sem_tile=None, acc=False, extra=(), rt=None):
        if self.mute:
            return None
        o = Op(eng, len(self.ops[eng]), fn)
        o.rt = rt
        deps = {}
        force = set()
        if eng == "pe":
            for t in writes:
                if t.w is not None and t.w.eng == "pe" and t.w.rt != rt:
                    deps[id(t.w)] = (t.w, True)
                    force.add(id(t.w))
        for d in extra:
            deps[id(d)] = (d, True)
        for t in reads:
            if t.w is not None:
                deps[id(t.w)] = (t.w, True)
            if t.psum:
                for r in t.r:
                    if r.eng != eng:
                        deps[id(r)] = (r, True)
        for t in writes:
            w = t.w
            if w is not None:
                if id(w) in deps:
                    pass
                elif acc and w.eng == "pe" and eng == "pe":
                    pass
                elif dma and w.dma and w.sem_tile is (sem_tile or t) and w.eng == eng:
                    pass
                elif id(w) not in deps:
                    deps[id(w)] = (w, False)
            for r in t.r:
                if id(r) not in deps:
                    deps[id(r)] = (r, False)
        dl = []
        for d, israw in deps.values():
            if d is o:
                continue
            if d.eng == eng and not d.dma and id(d) not in force:
                if eng == "pe" or not israw:
                    continue
            dl.append(d)
        o.deps = dl
        if dma:
            o.dma = True
            st = sem_tile if sem_tile is not None else writes[0]
            o.sem_tile = st
            st.dcnt += 16
            o.dval = st.dcnt
            if st not in self.dma_tiles:
                self.dma_tiles.append(st)
        for d in dl:
            d.sig = True
        for t in reads:
            t.r.append(o)
        for t in writes:
            t.w = o
            t.r = []
        self.ops[eng].append(o)
        if dma:
            self.last_dma[id(o.sem_tile)] = o
        return o

    def barrier(self):
        lasts = [self.ops[e][-1] for e in ENGS if self.ops[e]]
        lasts = [o for o in lasts if o.fn is not None and not o.dma]
        dmas = list(self.last_dma.values())
        self.last_dma = {}
        for e in ENGS:
            self.op(e, None, extra=[o for o in lasts if o.eng != e] + dmas)

    def act(self, out, in_, func, R, W, **kw):
        return self.op("act", lambda e: e.activation(out=out, in_=in_, func=func, **kw), R, W)

    def mm(self, out, lhsT, rhs, R, W, start=True, stop=True, acc=False):
        rt = lhsT.base_partition() if lhsT.partition_size() <= 64 else None
        return self.op("pe", lambda e: e.matmul(out, lhsT=lhsT, rhs=rhs, start=start, stop=stop,
                                                skip_group_check=True), R, W, acc=acc, rt=rt)

    def tr(self, out, in_, ident, R, W, acc=False):
        return self.op("pe", lambda e: e.transpose(out=out, in_=in_, identity=ident), R, W, acc=acc)

    def tt(self, out, in0, in1, op, R, W, eng="dve"):
        return self.op(eng, lambda e: e.tensor_tensor(out=out, in0=in0, in1=in1, op=op), R, W)

    def ts(self, out, in0, s1, s2, op0, op1, R, W):
        return self.op("dve", lambda e: e.tensor_scalar(out=out, in0=in0, scalar1=s1, scalar2=s2,
                                                         op0=op0, op1=op1), R, W)

    def stt(self, out, in0, scalar, in1, op0, op1, R, W):
        return self.op("dve", lambda e: e.scalar_tensor_tensor(out=out, in0=in0, scalar=scalar, in1=in1,
                                                                op0=op0, op1=op1), R, W)

    def cp(self, out, in_, R, W, eng="dve"):
        return self.op(eng, lambda e: e.tensor_copy(out=out, in_=in_), R, W)

    def ms(self, ap, val, W, eng="pool"):
        return self.op(eng, lambda e: e.memset(ap, val), (), W)

    def dma(self, out, in_, R, W, eng="sp", sem_tile=None):
        return self.op(eng, lambda e: e.dma_start(out=out, in_=in_), R, W, dma=True, sem_tile=sem_tile)

    def emit(self, final_eng, final_tiles, sem_cap=30000):
        nc = self.nc
        if os.environ.get("KVERBOSE"):
            print("DMA semaphores:", len(self.dma_tiles), " ops:", {e: len(v) for e, v in self.ops.items()})
        self.op(final_eng, None, reads=final_tiles)
        with ExitStack() as es:
            esems = {}
            for e in ENGS:
                n = sum(1 for o in self.ops[e] if o.sig and not o.dma)
                k = max(1, (n + sem_cap - 1) // sem_cap)
                esems[e] = [es.enter_context(nc.semaphore(f"s_{e}{i}")) for i in range(k)]
                c = 0
                for o in self.ops[e]:
                    if o.sig and not o.dma:
                        c += 1
                        o.cnt = c
            for i, t in enumerate(self.dma_tiles):
                t.dsem = es.enter_context(nc.semaphore(f"d{i}"))

            def sigof(d):
                if d.dma:
                    return d.sem_tile.dsem, d.dval
                k = (d.cnt - 1) // sem_cap
                return esems[d.eng][k], d.cnt - k * sem_cap

            block = es.enter_context(nc.Block())

            def run(e, engobj):
                seen = {}
                for o in self.ops[e]:
                    for d in o.deps:
                        s, v = sigof(d)
                        if seen.get(id(s), 0) >= v:
                            continue
                        seen[id(s)] = v
                        engobj.wait_ge(s, v)
                    if o.fn is None:
                        continue
                    ins = o.fn(engobj)
                    if o.dma:
                        ins.then_inc(o.sem_tile.dsem, 16)
                    elif o.sig:
                        s, _ = sigof(o)
                        ins.then_inc(s, 1)

            @block.tensor
            def _(eng):
                run("pe", eng)

            @block.scalar
            def _(eng):
                run("act", eng)

            @block.vector
            def _(eng):
                run("dve", eng)

            @block.gpsimd
            def _(eng):
                run("pool", eng)

            @block.sync
            def _(eng):
                run("sp", eng)


class B:
    def __init__(self, h, name):
        self.h = h
        self.t = T(name)

    def __getitem__(self, k):
        return self.h[k]


def build(S, dbg=False, phases=("R", "A", "2a", "2b")):
    nc = bass.Bass("TRN2", target_bir_lowering=False)
    NB = S // TB
    NKB = S // 128
    NCH = TB // C

    def din(name, shape, dt=F32):
        return nc.dram_tensor(name, list(shape), dt, kind="ExternalInput").ap()

    x_d = din("x", [S, D])
    cT_d = din("cT", [128, 8])
    wada_d = din("w_ada", [D, 6 * D])
    badaT_d = din("badaT", [128, 48])
    bada_g_d = din("bada_g", [2, D])
    n1g_d = din("n1g", [128, 8])
    n2g_d = din("n2g", [128, 8])
    w1_d = din("w1", [2, D, NIN1])
    mu_d = din("mu", [2, NLERP])
    colp_d = din("colp", [2, 128, 16])
    w2_d = din("w2h", [2, 64, 256])
    a2_d = din("a2h", [2, 64, 256])
    g2_d = din("g2h", [2, 160, 256])
    wg_d = din("wg", [D, 2 * D])
    bgT_d = din("bgT", [128, 16])
    woa_d = din("woa", [512, D])
    wob_d = din("wob", [512, D])
    wout_d = din("wout", [D, D])
    wup_d = din("wup", [D, 2 * DFF])
    convT_d = din("convT", [128, 4, 42])
    wdn_d = din("wdn", [DFF, D])
    SH = S // 2
    xown_d = din("x_own", [SH, D])
    xhalo_d = din("x_halo", [128, D])
    flags_d = din("flags", [128, 2])
    out_d = nc.dram_tensor("out", [SH, D], F32, kind="ExternalOutput").ap()
    ykind = "ExternalOutput" if dbg else "Internal"
    ysc_d = nc.dram_tensor("ysc", [2, 512, S], BF16, kind=ykind).ap()
    x1_d = nc.dram_tensor("x1sc", [SH, D], F32, kind=ykind).ap()
    x1h_d = nc.dram_tensor("x1halo", [128, D], F32, kind="Internal").ap()

    dbgY_d = nc.dram_tensor("dbgY", [128, TB], F32, kind="ExternalOutput").ap() if dbg else None
    dbgT = T("dbgY")
    P = Prog(nc)
    top = ExitStack()
    RC = int(os.environ.get("RCUT", "99"))
    dumped = set()

    def dump(name, b, ap, shape, dt, cond=True):
        if not dbg or not cond or name in dumped or not os.environ.get("KDUMPS"):
            return
        dumped.add(name)
        d = nc.dram_tensor("dbg_" + name, list(shape), dt, kind="ExternalOutput").ap()
        P.dma(d, ap, [b.t], [T("dd" + name)], eng="sp", sem_tile=T("ds" + name))

    def cut(k):
        if RC == k:
            P.mute = True

    uniq = {"n": 0}

    def sbuf(es, name, shape, dt):
        uniq["n"] += 1
        nm = f"{name}_{uniq['n']}"
        return B(es.enter_context(nc.sbuf_tensor(nm, list(shape), dt)), nm)

    psall = top.enter_context(nc.psum_tensor("psall", [128, 8 * 512], F32))
    banks = [B(psall[:, i * 512:(i + 1) * 512], f"bank{i}") for i in range(8)]
    for b_ in banks:
        b_.t.psum = True
    rot = {"i": 0, "l": banks[3:]}

    def nb():
        b = rot["l"][rot["i"] % len(rot["l"])]
        rot["i"] += 1
        return b

    ident = sbuf(top, "ident", [128, 128], BF16)
    P.ms(ident[:], 1.0, [ident.t])
    P.op("pool", lambda e: e.affine_select(out=ident[:], in_=ident[:], pattern=[[1, 128]],
                                           compare_op=ALU.is_equal, fill=0.0, base=0, channel_multiplier=-1),
         [ident.t], [ident.t])
    blk1 = sbuf(top, "blk1", [128, 128], BF16)
    blk64 = sbuf(top, "blk64", [128, 128], BF16)
    for bt, val in ((blk1, 1.0), (blk64, 1.0 / 64)):
        P.ms(bt[:], 0.0, [bt.t])
        P.ms(bt[0:64, 0:64], val, [bt.t])
        P.ms(bt[64:128, 64:128], val, [bt.t])
    modT = sbuf(top, "modT", [128, 48], F32)
    grow = sbuf(top, "grow", [128, 2 * D], F32)
    G1 = sbuf(top, "G1", [128, 8], F32)
    fl = sbuf(top, "fl", [128, 2], F32)
    P.dma(fl[:], flags_d, [], [fl.t])
    G2 = sbuf(top, "G2", [128, 8], F32)

    with ExitStack() as es:
        cT = sbuf(es, "cTs", [128, 8], F32)
        sc = sbuf(es, "sc", [128, 8], F32)
        screp = sbuf(es, "screp", [128, 8, 128], F32)
        badaT = sbuf(es, "badaTs", [128, 48], F32)
        ng = sbuf(es, "ng", [128, 16], F32)
        wab = [sbuf(es, f"wab{i}", [128, 8, 512], F32) for i in range(2)]
        P.dma(cT[:], cT_d, [], [cT.t])
        P.dma(badaT[:], badaT_d, [], [badaT.t])
        P.dma(ng[:, 0:8], n1g_d, [], [ng.t])
        P.dma(ng[:, 8:16], n2g_d, [], [ng.t])
        P.dma(grow[:, 0:D], bada_g_d[0:1, :].partition_broadcast(128), [], [grow.t])
        P.dma(grow[:, D:2 * D], bada_g_d[1:2, :].partition_broadcast(128), [], [grow.t])
        P.act(sc[:], cT[:], AF.Silu, [cT.t], [sc.t])
        P.cp(screp[:], sc[:].unsqueeze(2).to_broadcast([128, 8, 128]), [sc.t], [screp.t])
        pm = banks[0]
        wv = wada_d.rearrange("(kc p) n -> p kc n", p=128)
        for blk in range(12):
            wb = wab[blk % 2]
            P.dma(wb[:], wv[:, :, blk * 512:(blk + 1) * 512], [], [wb.t])
            for j in range(4):
                oc = blk * 4 + j
                for kc in range(8):
                    P.mm(pm[:, oc:oc + 1], wb[:, kc, j * 128:(j + 1) * 128], sc[:, kc:kc + 1],
                         [wb.t, sc.t], [pm.t], start=(kc == 0), stop=(kc == 7), acc=(oc > 0 or kc > 0))
            gi = {4: 0, 5: 1, 10: 2, 11: 3}.get(blk)
            if gi is not None:
                pr = nb()
                for kc in range(8):
                    P.mm(pr[:, :], screp[:, kc, :], wb[:, kc, :], [wb.t, screp.t], [pr.t],
                         start=(kc == 0), stop=(kc == 7), acc=(kc > 0))
                P.tt(grow[:, gi * 512:(gi + 1) * 512], pr[:, :], grow[:, gi * 512:(gi + 1) * 512], ALU.add,
                     [pr.t, grow.t], [grow.t])
        P.tt(modT[:], pm[:, 0:48], badaT[:], ALU.add, [pm.t, badaT.t], [modT.t])
        P.stt(G1[:], modT[:, 8:16], 1.0, ng[:, 0:8], ALU.add, ALU.mult, [modT.t, ng.t], [G1.t])
        P.stt(G2[:], modT[:, 32:40], 1.0, ng[:, 8:16], ALU.add, ALU.mult, [modT.t, ng.t], [G2.t])
    SH1 = modT[:, 0:8]
    SH2 = modT[:, 24:32]
    P.barrier()

    def norm_T(es_bufs, xt, hT, col0, Gm, SHm):
        st, xn, junk = es_bufs
        P.act(junk[:], xt[:], AF.Square, [xt.t], [junk.t, st.t], accum_out=st[:, 0:1])
        P.act(st[:, 1:2], st[:, 0:1], AF.Ln, [st.t], [st.t], scale=1.0 / D, bias=1e-6)
        P.act(st[:, 2:3], st[:, 1:2], AF.Exp, [st.t], [st.t], scale=-0.5)
        P.act(xn[:], xt[:], AF.Copy, [xt.t, st.t], [xn.t], scale=st[:, 2:3])
        pb = nb()
        pv = pb[:].bitcast(BF16)
        for kc in range(8):
            P.tr(pv[:, kc * 128:(kc + 1) * 128], xn[:, kc * 128:(kc + 1) * 128], ident[:],
                 [xn.t, ident.t], [pb.t], acc=(kc > 0))
        dst = hT[:, :, col0:col0 + 128]
        P.tt(dst, pv.rearrange("p (k t) -> p k t", t=128), Gm.unsqueeze(2).to_broadcast([128, 8, 128]),
             ALU.mult, [pb.t, G1.t, G2.t, modT.t], [hT.t])
        P.tt(dst, dst, SHm.unsqueeze(2).to_broadcast([128, 8, 128]), ALU.add, [hT.t, modT.t], [hT.t])

    yT = T("ysc")

    def load_x_block(blk, xts, nbufs, hT, col_off):
        t0 = blk * TB
        for ti in range(4):
            xt = xts[ti % 2]
            P.dma(xt[:], x_d[t0 + ti * 128:t0 + (ti + 1) * 128, :], [], [xt.t])
            norm_T(nbufs, xt, hT, col_off + ti * 128, G1[:, :], SH1)

    for hh in range(2):
      if "R" in phases:
        with ExitStack() as es:
            Wb = sbuf(es, "Wb", [128, 8, NLERP], BF16)
            Wmu = sbuf(es, "Wmu", [128, 8, NLERP], BF16)
            colp = sbuf(es, "colp", [128, 24], F32)
            w2b = sbuf(es, "w2b", [64, 256], BF16)
            a2b = sbuf(es, "a2b", [64, 256], BF16)
            g2b0 = sbuf(es, "g2b0", [128, 256], BF16)
            g2b1 = sbuf(es, "g2b1", [32, 256], BF16)
            w1v = w1_d[hh].rearrange("(kc p) n -> p kc n", p=128)
            for kc in range(8):
                P.dma(Wb[:, kc, :], w1v[:, kc, 0:NLERP], [], [Wb.t], eng="pool")
            P.dma(colp[:, 0:16], colp_d[hh], [], [colp.t])
            P.dma(w2b[:], w2_d[hh], [], [w2b.t], eng="pool")
            P.dma(a2b[:], a2_d[hh], [], [a2b.t], eng="pool")
            P.dma(g2b0[:], g2_d[hh, 0:128, :], [], [g2b0.t], eng="pool")
            P.dma(g2b1[:], g2_d[hh, 128:160, :], [], [g2b1.t], eng="pool")
            MU = sbuf(es, "MU", [128, NLERP], F32)
            P.dma(MU[:], mu_d[hh:hh + 1, :].partition_broadcast(128), [], [MU.t])
            for kc in range(8):
                P.tt(Wmu[:, kc, :], Wb[:, kc, :], MU[:], ALU.mult, [Wb.t, MU.t], [Wmu.t])
                P.tt(Wb[:, kc, :], Wb[:, kc, :], Wmu[:, kc, :], ALU.subtract, [Wb.t, Wmu.t], [Wb.t])
            for g in range(2):
                P.ts(colp[:, 16 + g:17 + g], colp[:, 7 * g + 3:7 * g + 4], -1.0, 1.0, ALU.mult, ALU.add,
                     [colp.t], [colp.t])

            def cpar(g, j):
                return colp[:, 7 * g + j:7 * g + j + 1]

            msk = {}
            for nm_, cmp_, cm_, st_ in (("SU", ALU.is_gt, -1, 1), ("IU", ALU.is_ge, -1, 1),
                                        ("SL", ALU.is_gt, 1, -1), ("ID", ALU.is_equal, -1, 1)):
                mt_ = sbuf(es, "m" + nm_, [64, NCH, C], F32)
                P.ms(mt_[:], 1.0, [mt_.t])
                P.op("pool", (lambda mt_=mt_, cmp_=cmp_, cm_=cm_, st_=st_:
                              lambda e: e.affine_select(out=mt_[:], in_=mt_[:], pattern=[[0, NCH], [st_, C]],
                                                        compare_op=cmp_, fill=0.0, base=0,
                                                        channel_multiplier=cm_))(),
                     [mt_.t], [mt_.t])
                msk[nm_] = mt_
            mSU, mIU, mSL, idf = msk["SU"], msk["IU"], msk["SL"], msk["ID"]
            smask = sbuf(es, "smask", [128, TB], F32)
            P.ms(smask[:], 1.0, [smask.t])
            P.ms(smask[:].rearrange("p (c t) -> p c t", t=C)[:, :, 0:1], 0.0, [smask.t])

            hTs = [sbuf(es, f"hT{i}", [128, 8, TB + 1], BF16) for i in range(2)]
            hcur = [hTs[0]]
            P.ms(hTs[0][:, :, 0:1], 0.0, [hTs[0].t])
            xts = [sbuf(es, f"xt{i}", [128, D], F32) for i in range(2)]
            nbufs = (sbuf(es, "nst", [128, 4], F32), sbuf(es, "xn", [128, D], BF16), sbuf(es, "junk", [128, D], BF16))
            ST32 = [sbuf(es, f"ST32_{g}", [128, C], F32) for g in range(2)]
            STb = [sbuf(es, f"STb_{g}", [128, C], BF16) for g in range(2)]
            for g in range(2):
                P.ms(ST32[g][:], 0.0, [ST32[g].t])
                P.ms(STb[g][:], 0.0, [STb[g].t])
            tanhwd = sbuf(es, "tanhwd", [64, TB], BF16)
            adsb = sbuf(es, "adsb", [64, TB], BF16)
            sgd0 = sbuf(es, "sgd0", [128, TB], BF16)
            sgd1 = sbuf(es, "sgd1", [32, TB], BF16)

            def f32b(name):
                return sbuf(es, name, [128, TB], F32)

            def b16b(name):
                return sbuf(es, name, [128, TB], BF16)

            r_sb, k_sb, v_sb, sg, a_sb, gg, rn, kk, k2, bb, Lsg, Eneg, Epos, tmpa = [
                f32b(n) for n in ("r_sb", "k_sb", "v_sb", "sg", "a_sb", "gg", "rn", "kk", "k2", "bb", "Lsg",
                                  "Eneg", "Epos", "tmpa")]
            Lx, Eprev, Egc = tmpa, rn, a_sb
            ysb, dd, m2, var, bon = Lsg, tmpa, Eneg, rn, a_sb
            sq, Bt, Kt, BG, KG, vb = [b16b(n) for n in ("sq", "Bt", "Kt", "BG", "KG", "vb")]
            yb, ysqb, rkb = Bt, Kt, BG
            AR = sbuf(es, "AR", [128, NCH, 2, C], BF16)
            BGt, KGt, Vt, Att = [sbuf(es, n, [64, NCH, 128], BF16) for n in ("BGt", "KGt", "Vt", "Att")]
            Pm = [[sbuf(es, f"Pm{h}_{i}", [64, NCH, C], F32) for i in range(2)] for h in range(2)]
            PmT = [[sbuf(es, f"PmT{h}_{i}", [64, NCH, C], F32) for i in range(2)] for h in range(2)]
            Rm = [[sbuf(es, f"Rm{h}_{i}", [64, NCH, C], F32) for i in range(2)] for h in range(2)]
            rot["l"] = [banks[0]] + banks[3:]
            TTb = [sbuf(es, f"TTb{h}", [64, NCH, C], BF16) for h in range(2)]
            MakT = [sbuf(es, f"MakT{h}", [64, NCH, C], BF16) for h in range(2)]
            MrbT = [sbuf(es, f"MrbT{h}", [64, NCH, C], BF16) for h in range(2)]
            MrkT = [sbuf(es, f"MrkT{h}", [64, NCH, C], BF16) for h in range(2)]
            Xs = sbuf(es, "Xs", [64, NCH, 128], BF16)
            Ut = sbuf(es, "Ut", [64, NCH, 128], F32)
            WtT = sbuf(es, "WtT", [128, NCH, C], BF16)
            Ub = [sbuf(es, f"Ub{i}", [64, 128], BF16) for i in range(2)]
            yout = [b16b(f"yout{i}") for i in range(2)]

            def proj(c0, M):
                pb = nb()
                n = 0
                for kc in range(8):
                    hT = hcur[0]
                    P.mm(pb[0:M, :], Wb[:, kc, c0:c0 + M], hT[:, kc, 1:TB + 1], [Wb.t, hT.t], [pb.t],
                         start=(n == 0), stop=False, acc=(n > 0))
                    n += 1
                    P.mm(pb[0:M, :], Wmu[:, kc, c0:c0 + M], hT[:, kc, 0:TB], [Wmu.t, hT.t], [pb.t],
                         start=False, stop=(n == 15), acc=True)
                    n += 1
                return pb

            v3 = lambda bf: bf[:].rearrange("p (c t) -> p c t", t=C)
            pv3 = lambda pb: pb[0:64, :].rearrange("p (c t) -> p c t", t=C)

            OWN0 = SH // TB
            HB = OWN0 - 1
            for blk in range(NB):
                t0 = blk * TB
                full = blk >= HB
                hcur[0] = hTs[blk % 2]

                def prefetch_next(blk=blk):
                    nb_ = blk + 1
                    if nb_ >= NB:
                        return
                    hc, hn = hTs[blk % 2], hTs[nb_ % 2]
                    if nb_ == OWN0:
                        P.ts(hn[:, :, 0:1], hc[:, :, TB:TB + 1], fl[:, 1:2], None, ALU.mult, ALU.bypass, [hc.t, fl.t], [hn.t])
                    else:
                        P.cp(hn[:, :, 0:1], hc[:, :, TB:TB + 1], [hc.t], [hn.t])
                    load_x_block(nb_, xts, nbufs, hn, 1)
                if blk == 0:
                    load_x_block(0, xts, nbufs, hTs[0], 1)
                pb = proj(768, 64)
                P.act(tanhwd[:], pb[0:64, :], AF.Tanh, [pb.t], [tanhwd.t])
                pb = proj(832, 64)
                P.act(adsb[:], pb[0:64, :], AF.Copy, [pb.t], [adsb.t])
                if full:
                    pb = proj(896, 128)
                    P.act(sgd0[:], pb[:, :], AF.Sigmoid, [pb.t], [sgd0.t])
                    pb = proj(1024, 32)
                    P.act(sgd1[:], pb[0:32, :], AF.Sigmoid, [pb.t], [sgd1.t])
                cut(1)
                for g in range(2):
                    ch = slice(g * 128, (g + 1) * 128)
                    if full:
                        pb = proj(0 + g * 128, 128)
                        P.act(r_sb[:], pb[:, :], AF.Copy, [pb.t], [r_sb.t])
                    pb = proj(256 + g * 128, 128)
                    P.act(k_sb[:], pb[:, :], AF.Copy, [pb.t], [k_sb.t])
                    pb = proj(512 + g * 128, 128)
                    P.act(v_sb[:], pb[:, :], AF.Copy, [pb.t], [v_sb.t])
                    P.cp(vb[:], v_sb[:], [v_sb.t], [vb.t])
                    pb = nb()
                    P.mm(pb[:, :], w2b[:, ch], tanhwd[:], [w2b.t, tanhwd.t], [pb.t])
                    P.act(sg[:], pb[:, :], AF.Sigmoid, [pb.t, colp.t], [sg.t], bias=cpar(g, 0))
                    pb = nb()
                    P.mm(pb[:, :], a2b[:, ch], adsb[:], [a2b.t, adsb.t], [pb.t])
                    P.act(a_sb[:], pb[:, :], AF.Sigmoid, [pb.t, colp.t], [a_sb.t], bias=cpar(g, 1))
                    if full:
                        pb = nb()
                        P.mm(pb[:, :], g2b0[:, ch], sgd0[:], [g2b0.t, sgd0.t], [pb.t], start=True, stop=False)
                        P.mm(pb[:, :], g2b1[:, ch], sgd1[:], [g2b1.t, sgd1.t], [pb.t], start=False, stop=True, acc=True)
                        P.act(gg[:], pb[:, :], AF.Copy, [pb.t], [gg.t])
                    P.act(sq[:], k_sb[:], AF.Square, [k_sb.t, colp.t], [sq.t], scale=cpar(g, 2))
                    pb = nb()
                    P.mm(pb[:, :], blk1[:], sq[:], [blk1.t, sq.t], [pb.t])
                    P.act(rn[:], pb[:, :], AF.Ln, [pb.t], [rn.t], bias=1e-24)
                    P.act(rn[:], rn[:], AF.Exp, [rn.t], [rn.t], scale=-0.5)
                    P.stt(kk[:], k_sb[:], cpar(g, 2), rn[:], ALU.mult, ALU.mult, [k_sb.t, colp.t, rn.t], [kk.t])
                    P.ts(tmpa[:], a_sb[:], cpar(g, 3), colp[:, 16 + g:17 + g], ALU.mult, ALU.add, [a_sb.t, colp.t], [tmpa.t])
                    P.tt(k2[:], k_sb[:], tmpa[:], ALU.mult, [k_sb.t, tmpa.t], [k2.t])
                    P.tt(bb[:], kk[:], a_sb[:], ALU.mult, [kk.t, a_sb.t], [bb.t])
                    P.op("dve", lambda e: e.tensor_tensor_scan(out=Lsg[:], data0=smask[:], data1=sg[:], initial=0.0,
                                                               op0=ALU.mult, op1=ALU.add), [smask.t, sg.t], [Lsg.t])
                    P.tt(Lx[:], Lsg[:], sg[:], ALU.subtract, [Lsg.t, sg.t], [Lx.t])
                    P.act(Eneg[:], Lsg[:], AF.Exp, [Lsg.t], [Eneg.t], scale=CDEC)
                    P.act(Epos[:], Lsg[:], AF.Exp, [Lsg.t], [Epos.t], scale=-CDEC)
                    P.act(Eprev[:], Lx[:], AF.Exp, [Lx.t], [Eprev.t], scale=-CDEC)
                    gamC = v3(Epos)[:, :, C - 1:C]
                    P.tt(v3(Egc), v3(Eneg), gamC.to_broadcast([128, NCH, C]), ALU.mult, [Eneg.t, Epos.t], [Egc.t])
                    P.tt(AR[:, :, 0, :], v3(kk), v3(Eprev), ALU.mult, [kk.t, Eprev.t], [AR.t])
                    if full:
                        P.tt(AR[:, :, 1, :], v3(r_sb), v3(Epos), ALU.mult, [r_sb.t, Epos.t], [AR.t])
                    P.tt(Bt[:], bb[:], Eneg[:], ALU.mult, [bb.t, Eneg.t], [Bt.t])
                    P.tt(Kt[:], k2[:], Eneg[:], ALU.mult, [k2.t, Eneg.t], [Kt.t])
                    P.tt(BG[:], bb[:], Egc[:], ALU.mult, [bb.t, Egc.t], [BG.t])
                    P.tt(KG[:], k2[:], Egc[:], ALU.mult, [k2.t, Egc.t], [KG.t])
                    D0 = (hh == 0 and blk == 0 and g == 0)
                    for nm_, b_, dt_ in (("r", r_sb, F32), ("k", k_sb, F32), ("v", v_sb, F32), ("sg", sg, F32),
                                         ("kk", kk, F32), ("k2", k2, F32), ("bb", bb, F32), ("Lsg", Lsg, F32),
                                         ("Eneg", Eneg, F32), ("Epos", Epos, F32), ("Eprev", Eprev, F32), ("Egc", Egc, F32),
                                         ("gg", gg, F32), ("Bt", Bt, BF16), ("Kt", Kt, BF16), ("BG", BG, BF16), ("KG", KG, BF16)):
                        dump(nm_, b_, b_[:], [128, TB], dt_, D0)
                    dump("AR", AR, AR[:].rearrange("p c s t -> p (c s t)"), [128, NCH * 2 * C], BF16, D0)
                    cut(2)
                    for src_ap, srct, dst in ((lambda c: BG[:, c * C:(c + 1) * C], BG.t, BGt),
                                              (lambda c: KG[:, c * C:(c + 1) * C], KG.t, KGt),
                                              (lambda c: vb[:, c * C:(c + 1) * C], vb.t, Vt),
                                              (lambda c: AR[:, c, 0, :], AR.t, Att)):
                        pb = nb()
                        pv = pb[:].bitcast(BF16)
                        for c in range(NCH):
                            P.tr(pv[0:64, c * 128:(c + 1) * 128], src_ap(c), ident[:], [srct, ident.t], [pb.t], acc=(c > 0))
                        P.cp(dst[:].rearrange("p c k -> p (c k)"), pv[0:64, 0:NCH * 128], [pb.t], [dst.t])
                    for nm_, b_ in (("BGt", BGt), ("KGt", KGt), ("Vt", Vt), ("Att", Att)):
                        dump(nm_, b_, b_[:].rearrange("p c k -> p (c k)"), [64, NCH * 128], BF16, D0)
                    cut(3)
                    if g == 0:
                        prefetch_next()
                    HP = [slice(0, 64), slice(64, 128)]

                    def mat(lhs_fn, rhs_fn, Rr):
                        pb = nb()
                        for c in range(NCH):
                            P.mm(pb[0:64, c * C:(c + 1) * C], lhs_fn(c), rhs_fn(c), Rr, [pb.t],
                                 start=(c == 0), stop=True, acc=(c > 0))
                        return pb
                    Btc = lambda h: (lambda c: Bt[HP[h], c * C:(c + 1) * C])
                    Ktc = lambda h: (lambda c: Kt[HP[h], c * C:(c + 1) * C])
                    Atc = lambda h: (lambda c: AR[HP[h], c, 0, :])
                    Rtc = lambda h: (lambda c: AR[HP[h], c, 1, :])
                    f3 = lambda b_: b_[:].rearrange("p c t -> p (c t)")
                    for h in range(2):
                        pb = mat(Btc(h), Atc(h), [Bt.t, AR.t])
                        P.stt(Pm[h][0][:], pv3(pb), -1.0, mSU[:], ALU.mult, ALU.mult, [pb.t, mSU.t], [Pm[h][0].t])
                        pb = mat(Atc(h), Btc(h), [Bt.t, AR.t])
                        P.stt(PmT[h][0][:], pv3(pb), -1.0, mSL[:], ALU.mult, ALU.mult, [pb.t, mSL.t], [PmT[h][0].t])
                    for h in range(2):
                        P.tt(Rm[h][0][:], Pm[h][0][:], idf[:], ALU.add, [Pm[h][0].t, idf.t], [Rm[h][0].t])
                    for h in range(2):
                        pb = mat(Ktc(h), Atc(h), [Kt.t, AR.t])
                        P.tt(MakT[h][:], pv3(pb), mSU[:], ALU.mult, [pb.t, mSU.t], [MakT[h].t])
                    cur = 0
                    rcur = 0
                    for lvl in range(1, 6):
                        pbs = []
                        for h in range(2):
                            Pc, PcT = Pm[h][cur], PmT[h][cur]
                            pb = mat(lambda c, PcT=PcT: PcT[:, c, :], lambda c, Pc=Pc: Pc[:, c, :], [Pc.t, PcT.t])
                            pb2 = mat(lambda c, Pc=Pc: Pc[:, c, :], lambda c, PcT=PcT: PcT[:, c, :], [Pc.t, PcT.t])
                            pbs.append((pb, pb2))
                        for h in range(2):
                            Pn, PnT = Pm[h][1 - cur], PmT[h][1 - cur]
                            P.cp(Pn[:], pv3(pbs[h][0]), [pbs[h][0].t], [Pn.t])
                            P.act(PnT[:], pv3(pbs[h][1]), AF.Copy, [pbs[h][1].t], [PnT.t])
                        pb3s = []
                        for h in range(2):
                            PnT = PmT[h][1 - cur]
                            Rc = Rm[h][rcur]
                            pb3s.append(mat(lambda c, PnT=PnT: PnT[:, c, :], lambda c, Rc=Rc: Rc[:, c, :], [PnT.t, Rc.t]))
                        for h in range(2):
                            Rc, Rn = Rm[h][rcur], Rm[h][1 - rcur]
                            P.tt(Rn[:], pv3(pb3s[h]), Rc[:], ALU.add, [pb3s[h].t, Rc.t], [Rn.t])
                        if lvl == 1 and full:
                            for h in range(2):
                                pb = mat(Btc(h), Rtc(h), [Bt.t, AR.t])
                                P.tt(MrbT[h][:], pv3(pb), mIU[:], ALU.mult, [pb.t, mIU.t], [MrbT[h].t])
                        if lvl == 2 and full:
                            for h in range(2):
                                pb = mat(Ktc(h), Rtc(h), [Kt.t, AR.t])
                                P.tt(MrkT[h][:], pv3(pb), mIU[:], ALU.mult, [pb.t, mIU.t], [MrkT[h].t])
                        if lvl == 3:
                            for h in range(2):
                                pb = mat(lambda c, h=h: MakT[h][:, c, :], lambda c, h=h: Vt[:, c, HP[h]], [MakT[h].t, Vt.t])
                                P.cp(Xs[:, :, HP[h]], pv3(pb), [pb.t], [Xs.t])
                        cur = 1 - cur
                        rcur = 1 - rcur
                    for h in range(2):
                        P.cp(TTb[h][:], Rm[h][rcur][:], [Rm[h][rcur].t], [TTb[h].t])
                    for h in range(2):
                        pb = mat(lambda c, h=h: TTb[h][:, c, :], lambda c, h=h: Xs[:, c, HP[h]], [TTb[h].t, Xs.t])
                        P.cp(Ut[:, :, HP[h]], pv3(pb), [pb.t], [Ut.t])
                        pb = nb()
                        for c in range(NCH):
                            P.mm(pb[HP[h], c * C:(c + 1) * C], Att[:, c, HP[h]], TTb[h][:, c, :], [Att.t, TTb[h].t], [pb.t],
                                 start=(c == 0), stop=True, acc=(c > 0))
                        P.cp(WtT[HP[h], :, :], pb[HP[h], :].rearrange("p (c t) -> p c t", t=C), [pb.t], [WtT.t])
                    dump("TT", TTb[0], f3(TTb[0]), [64, NCH * C], BF16, D0)
                    dump("MrbT", MrbT[0], f3(MrbT[0]), [64, NCH * C], BF16, D0)
                    dump("MakT", MakT[0], f3(MakT[0]), [64, NCH * C], BF16, D0)
                    dump("MrkT", MrkT[0], f3(MrkT[0]), [64, NCH * C], BF16, D0)
                    dump("Xs", Xs, Xs[:].rearrange("p c k -> p (c k)"), [64, NCH * 128], BF16, D0)
                    dump("Ut", Ut, Ut[:].rearrange("p c k -> p (c k)"), [64, NCH * 128], F32, D0)
                    dump("WtT", WtT, WtT[:].rearrange("p c t -> p (c t)"), [128, NCH * C], BF16, D0)
                    cut(6)
                    py = banks[1 + g]
                    for c in range(NCH):
                        U = Ub[c % 2]
                        pu = nb()
                        for h in range(2):
                            hp = slice(h * 64, (h + 1) * 64)
                            P.mm(pu[0:64, hp], WtT[hp, c, :], STb[g][hp, :], [WtT.t, STb[g].t], [pu.t],
                                 start=True, stop=True, acc=(h > 0))
                        P.stt(U[:], pu[0:64, 0:128], -1.0, Ut[:, c, :], ALU.mult, ALU.subtract, [pu.t, Ut.t], [U.t])
                        dump("U0", U, U[:], [64, 128], BF16, D0 and c == 0)
                        dump("U1", U, U[:], [64, 128], BF16, D0 and c == 1)
                        for h in (range(2) if full else ()):
                            hp = slice(h * 64, (h + 1) * 64)
                            oy = py[hp, c * C:(c + 1) * C]
                            P.mm(oy, Vt[:, c, hp], MrkT[h][:, c, :], [Vt.t, MrkT[h].t], [py.t], start=True, stop=False,
                                 acc=(c > 0 or h > 0))
                            P.mm(oy, STb[g][hp, :], AR[hp, c, 1, :], [STb[g].t, AR.t], [py.t], start=False, stop=False, acc=True)
                            P.mm(oy, U[:, hp], MrbT[h][:, c, :], [U.t, MrbT[h].t], [py.t], start=False, stop=True, acc=True)
                        psn = nb()
                        for h in range(2):
                            hp = slice(h * 64, (h + 1) * 64)
                            P.mm(psn[hp, 0:C], KGt[:, c, hp], Vt[:, c, hp], [KGt.t, Vt.t], [psn.t], start=True, stop=False,
                                 acc=(h > 0))
                            P.mm(psn[hp, 0:C], BGt[:, c, hp], U[:, hp], [BGt.t, U.t], [psn.t], start=False, stop=True, acc=True)
                        gcol = v3(Epos)[:, c, C - 1:C]
                        P.stt(STb[g][:], ST32[g][:], gcol, psn[:, 0:C], ALU.mult, ALU.add, [ST32[g].t, Epos.t, psn.t], [STb[g].t])
                        P.stt(ST32[g][:], ST32[g][:], gcol, psn[:, 0:C], ALU.mult, ALU.add, [ST32[g].t, Epos.t, psn.t], [ST32[g].t])
                    if blk == OWN0 - 1:
                        P.ts(STb[g][:], STb[g][:], fl[:, 1:2], None, ALU.mult, ALU.bypass, [STb[g].t, fl.t], [STb[g].t])
                        P.ts(ST32[g][:], ST32[g][:], fl[:, 1:2], None, ALU.mult, ALU.bypass, [ST32[g].t, fl.t], [ST32[g].t])
                    P.mute = not full
                    cut(7)
                    P.act(ysb[:], py[:, :], AF.Copy, [py.t], [ysb.t])
                    if dbg and os.environ.get("KDUMPS") and hh == 0 and blk == 0 and g == 0:
                        P.dma(dbgY_d, ysb[:], [ysb.t], [dbgT], eng="sp", sem_tile=ysb.t)
                    P.cp(yb[:], ysb[:], [ysb.t], [yb.t])
                    P.act(ysqb[:], ysb[:], AF.Square, [ysb.t], [ysqb.t])
                    cut(71)
                    pmean = nb()
                    P.mm(pmean[:, :], blk64[:], yb[:], [blk64.t, yb.t], [pmean.t])
                    pmsq = nb()
                    P.mm(pmsq[:, :], blk64[:], ysqb[:], [blk64.t, ysqb.t], [pmsq.t])
                    cut(72)
                    P.stt(dd[:], pmean[:, :], -1.0, ysb[:], ALU.mult, ALU.add, [ysb.t, pmean.t], [dd.t])
                    P.act(m2[:], pmean[:, :], AF.Square, [pmean.t], [m2.t])
                    P.tt(var[:], pmsq[:, :], m2[:], ALU.subtract, [pmsq.t, m2.t], [var.t])
                    cut(73)
                    P.act(var[:], var[:], AF.Ln, [var.t], [var.t], bias=64e-5)
                    P.act(var[:], var[:], AF.Exp, [var.t], [var.t], scale=-0.5)
                    cut(74)
                    P.tt(dd[:], dd[:], var[:], ALU.mult, [dd.t, var.t], [dd.t])
                    P.ts(dd[:], dd[:], cpar(g, 5), cpar(g, 6), ALU.mult, ALU.add, [dd.t, colp.t], [dd.t])
                    cut(8)
                    P.stt(rkb[:], r_sb[:], cpar(g, 4), k2[:], ALU.mult, ALU.mult, [r_sb.t, colp.t, k2.t], [rkb.t])
                    pbon = nb()
                    P.mm(pbon[:, :], blk1[:], rkb[:], [blk1.t, rkb.t], [pbon.t])
                    P.tt(bon[:], pbon[:, :], v_sb[:], ALU.mult, [pbon.t, v_sb.t], [bon.t])
                    P.tt(dd[:], dd[:], bon[:], ALU.add, [dd.t, bon.t], [dd.t])
                    yo_ = yout[g]
                    P.tt(yo_[:], dd[:], gg[:], ALU.mult, [dd.t, gg.t], [yo_.t])
                    cut(9)
                    P.dma(ysc_d[0, hh * 256 + g * 128:hh * 256 + (g + 1) * 128, t0:t0 + TB], yo_[:], [yo_.t], [yT],
                          eng="sp", sem_tile=yo_.t)
                    P.mute = False

        P.mute = False
        rot["l"] = banks[3:]
        P.barrier()
      if "A" in phases:
        with ExitStack() as es:
            NSB = 768
            Wb = sbuf(es, "WbA", [128, 8, NSB], BF16)
            colp = sbuf(es, "colpA", [128, 24], F32)
            w1v = w1_d[hh].rearrange("(kc p) n -> p kc n", p=128)
            for kc in range(8):
                P.dma(Wb[:, kc, :], w1v[:, kc, NLERP:NLERP + NSB], [], [Wb.t], eng="pool")
            P.dma(colp[:, 0:16], colp_d[hh], [], [colp.t])
            P.ts(colp[:, 18:19], colp[:, 14:15], 0.125, None, ALU.mult, ALU.bypass, [colp.t], [colp.t])
            Tm = sbuf(es, "Tm", [128, 128], BF16)
            P.ms(Tm[:], -1.0, [Tm.t])
            P.op("pool", lambda e: e.affine_select(out=Tm[:], in_=Tm[:], pattern=[[-1, 128]], compare_op=ALU.is_ge,
                                                   fill=0.0, base=0, channel_multiplier=1), [Tm.t], [Tm.t])
            NO = sbuf(es, "NO", [128, 128], BF16)
            P.ms(NO[:], -1.0, [NO.t])
            cmask = sbuf(es, "cmask", [128, 128], BF16)
            P.ms(cmask[:], 1.0, [cmask.t])
            P.op("pool", lambda e: e.affine_select(out=cmask[:], in_=cmask[:], pattern=[[1, 128]], compare_op=ALU.is_gt,
                                                   fill=0.0, base=0, channel_multiplier=-1), [cmask.t], [cmask.t])
            KT = [sbuf(es, f"KT{g}", [128, S], BF16) for g in range(2)]
            Vres = sbuf(es, "Vres", [128, NKB, 256], BF16)
            KT_t = [[T(f"kt{g}_{b}") for b in range(NB)] for g in range(2)]
            V_t = [T(f"v{b}") for b in range(NB)]
            hTsA = [sbuf(es, f"hTA{i}", [128, 8, TB], BF16) for i in range(2)]
            hcurA = [hTsA[0]]
            xts = [sbuf(es, f"xtA{i}", [128, D], F32) for i in range(2)]
            nbufs = (sbuf(es, "nstA", [128, 4], F32), sbuf(es, "xnA", [128, D], BF16), sbuf(es, "junkA", [128, D], BF16))
            rn = sbuf(es, "rnA", [128, TB], F32)
            sq = sbuf(es, "sqA", [128, TB], BF16)
            vb = sbuf(es, "vbA", [128, TB], BF16)
            QTs = [[sbuf(es, f"QT{k}_{g}", [128, TB], BF16) for g in range(2)] for k in range(2)]
            E2 = [sbuf(es, f"E2_{i}", [128, 2, TB], F32) for i in range(2)]
            L2 = [sbuf(es, f"L2_{i}", [128, 2, TB], BF16) for i in range(2)]
            Ls2 = sbuf(es, "Ls2", [128, 2, TB], BF16)
            A2 = [sbuf(es, f"A2_{i}", [128, 2, TB], BF16) for i in range(2)]
            qfA = sbuf(es, "qfA", [128, TB], F32)
            NSET = 3
            p1S = [[banks[1 + 2 * k + h] for h in range(2)] for k in range(NSET)]
            p1P = [psall[:, (1 + 2 * k) * 512:(3 + 2 * k) * 512].rearrange("p (s q) -> p s q", s=2) for k in range(NSET)]
            rot["l"] = [banks[7]]
            yBo = [sbuf(es, f"yBo{i}", [128, TB], BF16) for i in range(2)]

            def projA(c0):
                pb = nb()
                for kc in range(8):
                    hT = hcurA[0]
                    P.mm(pb[:, :], Wb[:, kc, c0:c0 + 128], hT[:, kc, :], [Wb.t, hT.t], [pb.t],
                         start=(kc == 0), stop=(kc == 7), acc=(kc > 0))
                return pb

            def prologue(b):
                fullb = b >= SH // TB - 1
                tb0 = b * TB
                hsave = hcurA[0]
                for g in range(2):
                    for which, c0 in (("q", 0), ("k", 256)):
                        if which == "q" and not fullb:
                            continue
                        hcurA[0] = hTsA[b % 2]
                        pb = projA(c0 + g * 128)
                        hcurA[0] = hsave
                        P.act(qfA[:], pb[:, :], AF.Copy, [pb.t], [qfA.t])
                        yield
                        P.act(sq[:], qfA[:], AF.Square, [qfA.t], [sq.t])
                        pn = nb()
                        P.mm(pn[:, :], blk64[:], sq[:], [blk64.t, sq.t], [pn.t])
                        P.act(rn[:], pn[:, :], AF.Ln, [pn.t], [rn.t], bias=1e-6)
                        P.act(rn[:], rn[:], AF.Exp, [rn.t], [rn.t], scale=-0.5)
                        if which == "q":
                            P.stt(QTs[b % 2][g][:], qfA[:], colp[:, 18:19], rn[:], ALU.mult, ALU.mult, [qfA.t, colp.t, rn.t], [QTs[b % 2][g].t])
                        else:
                            P.stt(KT[g][:, tb0:tb0 + TB], qfA[:], colp[:, 15:16], rn[:], ALU.mult, ALU.mult,
                                  [qfA.t, colp.t, rn.t], [KT_t[g][b]])
                        yield
                    hcurA[0] = hTsA[b % 2]
                    pb = projA(512 + g * 128)
                    hcurA[0] = hsave
                    P.act(vb[:], pb[:, :], AF.Copy, [pb.t], [vb.t])
                    yield
                    pb = nb()
                    pv = pb[:].bitcast(BF16)
                    for j in range(4):
                        P.tr(pv[:, j * 128:(j + 1) * 128], vb[:, j * 128:(j + 1) * 128], ident[:], [vb.t, ident.t], [pb.t], acc=(j > 0))
                    if b < SH // TB:
                        P.ts(Vres[:, b * 4:b * 4 + 4, g * 128:(g + 1) * 128], pv[:, 0:512].rearrange("p (j c) -> p j c", c=128),
                             fl[:, 1:2], None, ALU.mult, ALU.bypass, [pb.t, fl.t], [V_t[b]])
                    else:
                        P.cp(Vres[:, b * 4:b * 4 + 4, g * 128:(g + 1) * 128], pv[:, 0:512].rearrange("p (j c) -> p j c", c=128),
                             [pb.t], [V_t[b]])
                    yield

            for blk in range(NB):
                t0 = blk * TB
                full = blk >= SH // TB - 1
                hcurA[0] = hTsA[blk % 2]
                if blk == 0:
                    load_x_block(0, xts, nbufs, hTsA[0], 0)
                QT = QTs[blk % 2]
                if blk == 0:
                    for _ in prologue(0):
                        pass
                pro_next = None
                if blk + 1 < NB:
                    load_x_block(blk + 1, xts, nbufs, hTsA[(blk + 1) % 2], 0)
                    pro_next = prologue(blk + 1)
                if pro_next is not None and not full:
                    for _ in pro_next:
                        pass
                    pro_next = None
                P.mute = not full
                nkb_ = (blk + 1) * 4
                kbs = list(range(nkb_ - 1, -1, -1))
                nst_ = len(kbs)
                cm2 = cmask[:].unsqueeze(1).to_broadcast([128, 2, 128])
                po = banks[0]

                def geom(i):
                    kb = kbs[i]
                    o = kb - blk * 4
                    q0 = max(o, 0) * 128
                    return kb, o, q0, slice(q0, TB), kb // 4

                for g in range(2):
                    P.ms(Ls2[:], 0.0, [Ls2.t], eng="dve")

                    def qk(i, g=g):
                        kb, o, q0, qs, kblk = geom(i)
                        for h in range(2):
                            hp = slice(h * 64, (h + 1) * 64)
                            p1 = p1S[i % NSET][h]
                            P.mm(p1[:, qs], KT[g][hp, kb * 128:(kb + 1) * 128], QT[g][hp, qs], [KT_t[g][kblk], QT[g].t], [p1.t])

                    def front(i):
                        kb, o, q0, qs, kblk = geom(i)
                        k_ = i % NSET
                        pt = [p1S[k_][0].t, p1S[k_][1].t]
                        E_, L_ = E2[i % 2], L2[i % 2]
                        P.act(E_[:, :, qs], p1P[k_][:, :, qs], AF.Exp, pt, [E_.t])
                        P.act(L_[:, :, qs], E_[:, :, qs], AF.Ln, [E_.t], [L_.t], bias=1.0)
                        if o >= 0:
                            P.tt(L_[:, :, q0:q0 + 128], L_[:, :, q0:q0 + 128], cm2, ALU.mult, [L_.t, cmask.t], [L_.t])

                    qk(0)
                    if nst_ > 1:
                        qk(1)
                    front(0)
                    for i in range(nst_):
                        kb, o, q0, qs, kblk = geom(i)
                        k_ = i % NSET
                        L_, A_ = L2[i % 2], A2[i % 2]
                        for h in range(2):
                            p1 = p1S[k_][h]
                            P.mm(p1[:, qs], Tm[:], L_[:, h, qs], [Tm.t, L_.t], [p1.t], start=False, stop=False, acc=True)
                            P.mm(p1[:, qs], NO[:], Ls2[:, h, qs], [NO.t, Ls2.t], [p1.t], start=False, stop=True, acc=True)
                        if kb > 0:
                            P.tt(Ls2[:, :, qs], Ls2[:, :, qs], L_[:, :, qs], ALU.add, [Ls2.t, L_.t], [Ls2.t])
                        if i + 2 < nst_:
                            qk(i + 2)
                        if i + 1 < nst_:
                            front(i + 1)
                        if pro_next is not None and i % 2 == 1:
                            next(pro_next, None)
                        pt = [p1S[k_][0].t, p1S[k_][1].t]
                        P.act(A_[:, :, qs], p1P[k_][:, :, qs], AF.Exp, pt, [A_.t])
                        if o >= 0:
                            P.tt(A_[:, :, q0:q0 + 128], A_[:, :, q0:q0 + 128], cm2, ALU.mult, [A_.t, cmask.t], [A_.t])
                        for h in range(2):
                            hp = slice(h * 64, (h + 1) * 64)
                            P.mm(po[hp, qs], Vres[:, kb, (2 * g + h) * 64:(2 * g + h + 1) * 64], A_[:, h, qs], [V_t[kblk], A_.t], [po.t],
                                 start=(i == 0), stop=(kb == 0), acc=True)
                    yb_ = yBo[g]
                    P.act(yb_[:], po[:, :], AF.Copy, [po.t], [yb_.t])
                    P.dma(ysc_d[1, hh * 256 + g * 128:hh * 256 + (g + 1) * 128, t0:t0 + TB], yb_[:], [yb_.t], [yT],
                          eng="sp", sem_tile=yb_.t)
                P.mute = False
                if pro_next is not None:
                    for _ in pro_next:
                        pass

            rot["l"] = banks[3:]
        P.barrier()
    x1T = T("x1sc")
    with ExitStack() as es:
      if "2a" in phases:
          Wg = sbuf(es, "Wg", [128, 8, 2 * D], BF16)
          Woa = sbuf(es, "Woa", [128, 4, D], BF16)
          Wob = sbuf(es, "Wob", [128, 4, D], BF16)
          Wo = sbuf(es, "Wo", [128, 8, D], BF16)
          bgT = sbuf(es, "bgT", [128, 16], F32)
          for kc in range(8):
              P.dma(Wg[:, kc, :], wg_d.rearrange("(kc p) n -> p kc n", p=128)[:, kc, :], [], [Wg.t], eng="pool")
          P.dma(Woa[:], woa_d.rearrange("(kc p) n -> p kc n", p=128), [], [Woa.t], eng="pool")
          P.dma(Wob[:], wob_d.rearrange("(kc p) n -> p kc n", p=128), [], [Wob.t], eng="pool")
          for kc in range(8):
              P.dma(Wo[:, kc, :], wout_d.rearrange("(kc p) n -> p kc n", p=128)[:, kc, :], [], [Wo.t], eng="pool")
          P.dma(bgT[:], bgT_d, [], [bgT.t])
          hT2s = [sbuf(es, f"hT2_{i}", [128, 8, TB], BF16) for i in range(2)]
          xts2 = [[sbuf(es, f"xq{k}_{i}", [128, D], F32) for i in range(4)] for k in range(2)]
          nst = sbuf(es, "nst2", [128, 4], F32)
          xn = sbuf(es, "xn2", [128, D], BF16)
          junk = sbuf(es, "junk2", [128, D], BF16)
          gT = sbuf(es, "gT", [128, 16, TB], BF16)
          yA = sbuf(es, "yA", [128, 4, TB], BF16)
          yBb = sbuf(es, "yBb", [128, 4, TB], BF16)
          yA2 = sbuf(es, "yA2", [128, 4, TB], BF16)
          yB2 = sbuf(es, "yB2", [128, 4, TB], BF16)
          mT = sbuf(es, "mT", [128, 8, TB], BF16)
          tA = sbuf(es, "tA", [128, TB], F32)
          tB = sbuf(es, "tB", [128, TB], F32)
          x1s = [sbuf(es, f"x1s{i}", [128, D], F32) for i in range(2)]
          yv = [ysc_d[br].rearrange("(kc p) t -> p kc t", p=128) for br in range(2)]
          x1hT = T("x1halo")

          ysets = [(yA, yBb), (yA2, yB2)]

          def pre2a(k, nt, xsrc, ytok):
              NT = nt * 128
              for br in range(2):
                  yb_ = ysets[k][br]
                  P.dma(yb_[:, :, 0:NT], yv[br][:, :, ytok:ytok + NT], [yT], [yb_.t])
              for ti in range(nt):
                  xt = xts2[k][ti]
                  P.dma(xt[:], xsrc(ti), [], [xt.t])
                  norm_T((nst, xn, junk), xt, hT2s[k], ti * 128, G1[:, :], SH1)

          def blk2a(k, nt, x1dst, dstT, nxt):
              NT = nt * 128
              hT = hT2s[k]
              xts = xts2[k]
              yA_, yB_ = ysets[k]
              for mt in range(16):
                  pb = nb()
                  for kc in range(8):
                      P.mm(pb[:, 0:NT], Wg[:, kc, mt * 128:(mt + 1) * 128], hT[:, kc, 0:NT], [Wg.t, hT.t], [pb.t],
                           start=(kc == 0), stop=(kc == 7), acc=(kc > 0))
                  P.act(gT[:, mt, 0:NT], pb[:, 0:NT], AF.Sigmoid, [pb.t, bgT.t], [gT.t], bias=bgT[:, mt:mt + 1])
              if nxt is not None:
                  nxt()
              for mt in range(8):
                  pa = nb()
                  for kc in range(4):
                      P.mm(pa[:, 0:NT], Woa[:, kc, mt * 128:(mt + 1) * 128], yA_[:, kc, 0:NT], [Woa.t, yA_.t], [pa.t],
                           start=(kc == 0), stop=(kc == 3), acc=(kc > 0))
                  pbb = nb()
                  for kc in range(4):
                      P.mm(pbb[:, 0:NT], Wob[:, kc, mt * 128:(mt + 1) * 128], yB_[:, kc, 0:NT], [Wob.t, yB_.t], [pbb.t],
                           start=(kc == 0), stop=(kc == 3), acc=(kc > 0))
                  P.tt(tA[:, 0:NT], pa[:, 0:NT], gT[:, mt, 0:NT], ALU.mult, [pa.t, gT.t], [tA.t])
                  P.tt(tB[:, 0:NT], pbb[:, 0:NT], gT[:, 8 + mt, 0:NT], ALU.mult, [pbb.t, gT.t], [tB.t])
                  P.tt(mT[:, mt, 0:NT], tA[:, 0:NT], tB[:, 0:NT], ALU.add, [tA.t, tB.t], [mT.t])
              for ti in range(nt):
                  x1 = x1s[ti % 2]
                  for half in range(2):
                      pb = nb()
                      for kc in range(8):
                          P.mm(pb[:, :], mT[:, kc, ti * 128:(ti + 1) * 128], Wo[:, kc, half * 512:(half + 1) * 512],
                               [mT.t, Wo.t], [pb.t], start=(kc == 0), stop=(kc == 7), acc=(kc > 0))
                      cs = slice(half * 512, (half + 1) * 512)
                      P.tt(x1[:, cs], pb[:, :], grow[:, cs], ALU.mult, [pb.t, grow.t], [x1.t])
                      P.tt(x1[:, cs], x1[:, cs], xts[ti][:, cs], ALU.add, [x1.t, xts[ti].t], [x1.t])
                  P.dma(x1dst(ti), x1[:], [x1.t], [dstT], eng="sp", sem_tile=x1.t)

          items = [(1, (lambda ti: xhalo_d), SH - 128, (lambda ti: x1h_d), x1hT)]
          for blk in range(SH // TB):
              t0 = blk * TB
              items.append((4, (lambda ti, t0=t0: xown_d[t0 + ti * 128:t0 + (ti + 1) * 128, :]), SH + t0,
                            (lambda ti, t0=t0: x1_d[t0 + ti * 128:t0 + (ti + 1) * 128, :]), x1T))
          pre2a(0, items[0][0], items[0][1], items[0][2])
          for i_, (nt_, xsrc_, ytok_, x1dst_, dstT_) in enumerate(items):
              nxt_ = None
              if i_ + 1 < len(items):
                  n_ = items[i_ + 1]
                  nxt_ = (lambda k=(i_ + 1) % 2, n_=n_: pre2a(k, n_[0], n_[1], n_[2]))
              blk2a(i_ % 2, nt_, x1dst_, dstT_, nxt_)

    P.barrier()
    TB2 = 256
    NB2 = SH // TB2
    outT = T("out")
    finals = []
    with ExitStack() as es:
      if "2b" in phases:
          Wup = sbuf(es, "Wup", [128, 8, 2 * DFF], BF16)
          Wdn = sbuf(es, "Wdn", [128, 21, D], BF16)
          convT = sbuf(es, "convTs", [128, 4, 42], F32)
          for kc in range(8):
              for hf in range(2):
                  P.dma(Wup[:, kc, hf * DFF:(hf + 1) * DFF],
                        wup_d.rearrange("(kc p) n -> p kc n", p=128)[:, kc, hf * DFF:(hf + 1) * DFF], [], [Wup.t], eng="pool")
          for kc in range(21):
              P.dma(Wdn[:, kc, :], wdn_d.rearrange("(kc p) n -> p kc n", p=128)[:, kc, :], [], [Wdn.t], eng="pool")
          P.dma(convT[:], convT_d, [], [convT.t])
          h2Ts = [sbuf(es, f"h2T{i}", [128, 8, TB2], BF16) for i in range(2)]
          x1ts = [[sbuf(es, f"x1t{k}_{i}", [128, D], F32) for i in range(2)] for k in range(2)]
          h2T = h2Ts[1]
          x1t = x1ts[1]
          nst = sbuf(es, "nst3", [128, 4], F32)
          xn = sbuf(es, "xn3", [128, D], BF16)
          junk = sbuf(es, "junk3", [128, D], BF16)
          halo = sbuf(es, "halo", [128, 42, 2], F32)
          halo_t = [T(f"halo{m}") for m in range(42)]
          NRU = 4
          ups = [sbuf(es, f"ups{i}", [128, TB2 + 2], F32) for i in range(NRU)]
          c1 = [sbuf(es, f"c1_{i}", [128, TB2], F32) for i in range(NRU)]
          vals = sbuf(es, "vals", [128, 21, TB2], BF16)
          vals_t = [T(f"vals{m}") for m in range(21)]
          actT = sbuf(es, "actT", [128, 21, TB2], BF16)
          sgt = [sbuf(es, f"sgt{i}", [128, TB2], F32) for i in range(2)]
          ost = [sbuf(es, f"ost{i}", [128, D], F32) for i in range(2)]
          P.dma(x1t[0][:], x1h_d, [x1hT], [x1t[0].t])
          norm_T((nst, xn, junk), x1t[0], h2T, 0, G2[:, :], SH2)
          ph = nb()
          for mt in range(42):
              for kc in range(8):
                  P.mm(ph[:, 2 * mt:2 * mt + 2], Wup[:, kc, mt * 128:(mt + 1) * 128], h2T[:, kc, 126:128], [Wup.t, h2T.t], [ph.t],
                       start=(kc == 0), stop=(kc == 7), acc=(mt > 0 or kc > 0))
          P.ts(halo[:].rearrange("p m c -> p (m c)"), ph[:, 0:84], fl[:, 1:2], None, ALU.mult, ALU.bypass,
               [ph.t, fl.t], halo_t)
          def pre2b(blk):
              for ti in range(2):
                  xt = x1ts[blk % 2][ti]
                  P.dma(xt[:], x1_d[blk * TB2 + ti * 128:blk * TB2 + (ti + 1) * 128, :], [x1T], [xt.t])
                  norm_T((nst, xn, junk), xt, h2Ts[blk % 2], ti * 128, G2[:, :], SH2)

          pre2b(0)
          for blk in range(NB2):
              t0 = blk * TB2
              h2T = h2Ts[blk % 2]
              x1t = x1ts[blk % 2]
              for mt in range(42):
                  if mt == 21 and blk + 1 < NB2:
                      pre2b(blk + 1)
                  pb = nb()
                  for kc in range(8):
                      P.mm(pb[:, 0:TB2], Wup[:, kc, mt * 128:(mt + 1) * 128], h2T[:, kc, :], [Wup.t, h2T.t], [pb.t],
                           start=(kc == 0), stop=(kc == 7), acc=(kc > 0))
                  u = ups[mt % NRU]
                  cc = c1[mt % NRU]
                  P.act(u[:, 2:TB2 + 2], pb[:, 0:TB2], AF.Copy, [pb.t], [u.t])
                  P.act(cc[:], pb[:, 0:TB2], AF.Identity, [pb.t, convT.t], [cc.t],
                        scale=convT[:, 2, mt:mt + 1], bias=convT[:, 3, mt:mt + 1])
                  P.cp(u[:, 0:2], halo[:, mt, :], [halo_t[mt]], [u.t], eng="pool")
                  P.cp(halo[:, mt, :], u[:, TB2:TB2 + 2], [u.t], [halo_t[mt]], eng="pool")
                  P.stt(cc[:], u[:, 1:TB2 + 1], convT[:, 1, mt:mt + 1], cc[:], ALU.mult, ALU.add, [u.t, convT.t, cc.t], [cc.t])
                  if mt < 21:
                      P.stt(vals[:, mt, :], u[:, 0:TB2], convT[:, 0, mt:mt + 1], cc[:], ALU.mult, ALU.add,
                            [u.t, convT.t, cc.t], [vals_t[mt]])
                  else:
                      sg_ = sgt[mt % 2]
                      P.stt(cc[:], u[:, 0:TB2], convT[:, 0, mt:mt + 1], cc[:], ALU.mult, ALU.add, [u.t, convT.t, cc.t], [cc.t])
                      P.act(sg_[:], cc[:], AF.Silu, [cc.t], [sg_.t])
                      P.tt(actT[:, mt - 21, :], sg_[:], vals[:, mt - 21, :], ALU.mult, [sg_.t, vals_t[mt - 21]], [actT.t])
              for ti in range(2):
                  o_ = ost[ti]
                  for half in range(2):
                      pb = nb()
                      for kc in range(21):
                          P.mm(pb[:, :], actT[:, kc, ti * 128:(ti + 1) * 128], Wdn[:, kc, half * 512:(half + 1) * 512],
                               [actT.t, Wdn.t], [pb.t], start=(kc == 0), stop=(kc == 20), acc=(kc > 0))
                      cs = slice(half * 512, (half + 1) * 512)
                      P.tt(o_[:, cs], pb[:, :], grow[:, D + half * 512:D + (half + 1) * 512], ALU.mult, [pb.t, grow.t], [o_.t])
                      P.tt(o_[:, cs], o_[:, cs], x1t[ti][:, cs], ALU.add, [o_.t, x1t[ti].t], [o_.t])
                  P.dma(out_d[t0 + ti * 128:t0 + (ti + 1) * 128, :], o_[:], [o_.t], [outT], eng="sp", sem_tile=o_.t)
                  finals.append(o_.t)
    P.emit("sp", list({id(t): t for t in P.dma_tiles}.values()))
    top.close()
    return nc


def _col(v):
    v = np.asarray(v, np.float32)
    return np.ascontiguousarray(v.reshape(-1, 128).T)


def prep_inputs(S, x_b, c_b, p, own=0):
    w_in = p["w_in"]
    RW = 512
    o_sb = 1824
    w1 = []
    mu = []
    colp = []
    w2h, a2h, g2h = [], [], []
    for hh in range(2):
        hs = slice(hh * 256, (hh + 1) * 256)
        cols = np.concatenate([np.arange(0, 512)[hs], 512 + np.arange(512)[hs], 1024 + np.arange(512)[hs],
                               np.arange(1536, 1824),
                               o_sb + np.arange(512)[hs], o_sb + 512 + np.arange(512)[hs], o_sb + 1024 + np.arange(512)[hs]])
        w1.append(w_in[:, cols])
        mu.append(p["rwkv_mu"][cols[:NLERP]])
        cp_ = np.zeros((128, 16), np.float32)
        for g in range(2):
            cs = slice(hh * 256 + g * 128, hh * 256 + (g + 1) * 128)
            for j, nm in enumerate(("rwkv_w0", "rwkv_a0", "rwkv_k_k", "rwkv_k_a", "rwkv_r_k", "rwkv_lnx_g", "rwkv_lnx_b")):
                cp_[:, 7 * g + j] = np.asarray(p[nm]).reshape(-1)[cs]
        cp_[:, 14] = np.tile(p["sb_q_g"], 2)
        cp_[:, 15] = np.tile(p["sb_k_g"], 2)
        colp.append(cp_)
        w2h.append(p["rwkv_w2"][:, hs])
        a2h.append(p["rwkv_a2"][:, hs])
        g2h.append(p["rwkv_g2"][:, hs])
    convT = np.stack([_col(p["conv_w"][0]), _col(p["conv_w"][1]), _col(p["conv_w"][2]), _col(p["conv_b"])], axis=1)
    f = lambda a: np.ascontiguousarray(np.asarray(a, np.float32))
    SHh = S // 2
    x_seq = x_b if own == 1 else np.concatenate([x_b[:SHh], x_b[:SHh]], axis=0)
    return {
        "x": f(x_seq), "x_own": f(x_seq[SHh:]), "x_halo": f(x_seq[SHh - 128:SHh]),
        "flags": f(np.tile(np.array([[1.0 - own, float(own)]], np.float32), (128, 1))), "cT": _col(c_b), "w_ada": f(p["w_ada"]), "badaT": _col(p["b_ada"]),
        "bada_g": f(np.stack([p["b_ada"][2 * D:3 * D], p["b_ada"][5 * D:6 * D]])),
        "n1g": _col(p["norm1_g"]), "n2g": _col(p["norm2_g"]),
        "w1": f(np.stack(w1)), "mu": f(np.stack(mu)), "colp": f(np.stack(colp)),
        "w2h": f(np.stack(w2h)), "a2h": f(np.stack(a2h)), "g2h": f(np.stack(g2h)),
        "wg": f(w_in[:, 3360:]), "bgT": _col(p["b_gate"]), "woa": f(p["w_o_rwkv"]), "wob": f(p["w_o_sb"]),
        "wout": f(p["w_out"]), "wup": f(p["w_up"]), "convT": f(convT), "wdn": f(p["w_down"]),
    }


_NC_CACHE = {}


def kernel(**inputs):
    x = np.asarray(inputs["x"], np.float32)
    Bn, S, _ = x.shape
    p = {k: np.asarray(v)[0] for k, v in inputs.items() if k not in ("x", "c")}
    c = np.asarray(inputs["c"], np.float32)
    if S not in _NC_CACHE:
        _NC_CACHE[S] = build(S)
    nc = _NC_CACHE[S]
    ncores = 8
    maps = []
    for core in range(ncores):
        b = (core // 2) % Bn
        maps.append(prep_inputs(S, x[b], c[b], p, core % 2))
    res = run_bass_kernel_spmd(nc, maps, core_ids=list(range(ncores)))
    out = np.stack([np.concatenate([res.results[2 * b]["out"], res.results[2 * b + 1]["out"]], axis=0)
                    for b in range(Bn)], axis=0)
    return out.astype(np.float32)
```

```python
import math
import os
from contextlib import ExitStack
import numpy as np
import concourse.bass as bass
import concourse.mybir as mybir
from concourse.bass_utils import run_bass_kernel_spmd

F32 = mybir.dt.float32
BF16 = mybir.dt.bfloat16
AF = mybir.ActivationFunctionType
ALU = mybir.AluOpType

ENGS = ("pe", "act", "dve", "pool", "sp")
D = 1024
NIN1 = 1824
NLERP = 1056
DFF = 2688
CDEC = math.exp(-0.5)
C = 64
TB = 512


class T:
    __slots__ = ("name", "w", "r", "dsem", "dcnt", "psum")

    def __init__(self, name=""):
        self.name = name
        self.psum = False
        self.w = None
        self.r = []
        self.dsem = None
        self.dcnt = 0


class Op:
    __slots__ = ("eng", "idx", "fn", "deps", "dma", "sem_tile", "dval", "sig", "cnt", "rt")

    def __init__(self, eng, idx, fn):
        self.eng = eng
        self.idx = idx
        self.fn = fn
        self.deps = []
        self.dma = False
        self.sem_tile = None
        self.dval = 0
        self.sig = False
        self.cnt = 0
        self.rt = None


class Prog:
    def __init__(self, nc):
        self.nc = nc
        self.ops = {e: [] for e in ENGS}
        self.dma_tiles = []
        self.last_dma = {}
        self.mute = False

    def op(self, eng, fn, reads=(), writes=(), dma=False, sem_tile=None, acc=False, extra=(), rt=None):
        if self.mute:
            return None
        o = Op(eng, len(self.ops[eng]), fn)
        o.rt = rt
        deps = {}
        force = set()
        if eng == "pe":
            for t in writes:
                if t.w is not None and t.w.eng == "pe" and t.w.rt != rt:
                    deps[id(t.w)] = (t.w, True)
                    force.add(id(t.w))
        for d in extra:
            deps[id(d)] = (d, True)
        for t in reads:
            if t.w is not None:
                deps[id(t.w)] = (t.w, True)
            if t.psum:
                for r in t.r:
                    if r.eng != eng:
                        deps[id(r)] = (r, True)
        for t in writes:
            w = t.w
            if w is not None:
                if id(w) in deps:
                    pass
                elif acc and w.eng == "pe" and eng == "pe":
                    pass
                elif dma and w.dma and w.sem_tile is (sem_tile or t) and w.eng == eng:
                    pass
                elif id(w) not in deps:
                    deps[id(w)] = (w, False)
            for r in t.r:
                if id(r) not in deps:
                    deps[id(r)] = (r, False)
        dl = []
        for d, israw in deps.values():
            if d is o:
                continue
            if d.eng == eng and not d.dma and id(d) not in force:
                if eng == "pe" or not israw:
                    continue
            dl.append(d)
        o.deps = dl
        if dma:
            o.dma = True
            st = sem_tile if sem_tile is not None else writes[0]
            o.sem_tile = st
            st.dcnt += 16
            o.dval = st.dcnt
            if st not in self.dma_tiles:
                self.dma_tiles.append(st)
        for d in dl:
            d.sig = True
        for t in reads:
            t.r.append(o)
        for t in writes:
            t.w = o
            t.r = []
        self.ops[eng].append(o)
        if dma:
            self.last_dma[id(o.sem_tile)] = o
        return o

    def barrier(self):
        lasts = [self.ops[e][-1] for e in ENGS if self.ops[e]]
        lasts = [o for o in lasts if o.fn is not None and not o.dma]
        dmas = list(self.last_dma.values())
        self.last_dma = {}
        for e in ENGS:
            self.op(e, None, extra=[o for o in lasts if o.eng != e] + dmas)

    def act(self, out, in_, func, R, W, **kw):
        return self.op("act", lambda e: e.activation(out=out, in_=in_, func=func, **kw), R, W)

    def mm(self, out, lhsT, rhs, R, W, start=True, stop=True, acc=False):
        rt = lhsT.base_partition() if lhsT.partition_size() <= 64 else None
        return self.op("pe", lambda e: e.matmul(out, lhsT=lhsT, rhs=rhs, start=start, stop=stop,
                                                skip_group_check=True), R, W, acc=acc, rt=rt)

    def tr(self, out, in_, ident, R, W, acc=False):
        return self.op("pe", lambda e: e.transpose(out=out, in_=in_, identity=ident), R, W, acc=acc)

    def tt(self, out, in0, in1, op, R, W, eng="dve"):
        return self.op(eng, lambda e: e.tensor_tensor(out=out, in0=in0, in1=in1, op=op), R, W)

    def ts(self, out, in0, s1, s2, op0, op1, R, W):
        return self.op("dve", lambda e: e.tensor_scalar(out=out, in0=in0, scalar1=s1, scalar2=s2,
                                                         op0=op0, op1=op1), R, W)

    def stt(self, out, in0, scalar, in1, op0, op1, R, W):
        return self.op("dve", lambda e: e.scalar_tensor_tensor(out=out, in0=in0, scalar=scalar, in1=in1,
                                                                op0=op0, op1=op1), R, W)

    def cp(self, out, in_, R, W, eng="dve"):
        return self.op(eng, lambda e: e.tensor_copy(out=out, in_=in_), R, W)

    def ms(self, ap, val, W, eng="pool"):
        return self.op(eng, lambda e: e.memset(ap, val), (), W)

    def dma(self, out, in_, R, W, eng="sp", sem_tile=None):
        return self.op(eng, lambda e: e.dma_start(out=out, in_=in_), R, W, dma=True, sem_tile=sem_tile)

    def emit(self, final_eng, final_tiles, sem_cap=30000):
        nc = self.nc
        if os.environ.get("KVERBOSE"):
            print("DMA semaphores:", len(self.dma_tiles), " ops:", {e: len(v) for e, v in self.ops.items()})
        self.op(final_eng, None, reads=final_tiles)
        with ExitStack() as es:
            esems = {}
            for e in ENGS:
                n = sum(1 for o in self.ops[e] if o.sig and not o.dma)
                k = max(1, (n + sem_cap - 1) // sem_cap)
                esems[e] = [es.enter_context(nc.semaphore(f"s_{e}{i}")) for i in range(k)]
                c = 0
                for o in self.ops[e]:
                    if o.sig and not o.dma:
                        c += 1
                        o.cnt = c
            for i, t in enumerate(self.dma_tiles):
                t.dsem = es.enter_context(nc.semaphore(f"d{i}"))

            def sigof(d):
                if d.dma:
                    return d.sem_tile.dsem, d.dval
                k = (d.cnt - 1) // sem_cap
                return esems[d.eng][k], d.cnt - k * sem_cap

            block = es.enter_context(nc.Block())

            def run(e, engobj):
                seen = {}
                for o in self.ops[e]:
                    for d in o.deps:
                        s, v = sigof(d)
                        if seen.get(id(s), 0) >= v:
                            continue
                        seen[id(s)] = v
                        engobj.wait_ge(s, v)
                    if o.fn is None:
                        continue
                    ins = o.fn(engobj)
                    if o.dma:
                        ins.then_inc(o.sem_tile.dsem, 16)
                    elif o.sig:
                        s, _ = sigof(o)
                        ins.then_inc(s, 1)

            @block.tensor
            def _(eng):
                run("pe", eng)

            @block.scalar
            def _(eng):
                run("act", eng)

            @block.vector
            def _(eng):
                run("dve", eng)

            @block.gpsimd
            def _(eng):
                run("pool", eng)

            @block.sync
            def _(eng):
                run("sp", eng)


class B:
    def __init__(self, h, name):
        self.h = h
        self.t = T(name)

    def __getitem__(self, k):
        return self.h[k]


def build(S, dbg=False, phases=("R", "A", "2a", "2b")):
    nc = bass.Bass("TRN2", target_bir_lowering=False)
    NB = S // TB
    NKB = S // 128
    NCH = TB // C

    def din(name, shape, dt=F32):
        return nc.dram_tensor(name, list(shape), dt, kind="ExternalInput").ap()

    x_d = din("x", [S, D])
    cT_d = din("cT", [128, 8])
    wada_d = din("w_ada", [D, 6 * D])
    badaT_d = din("badaT", [128, 48])
    bada_g_d = din("bada_g", [2, D])
    n1g_d = din("n1g", [128, 8])
    n2g_d = din("n2g", [128, 8])
    w1_d = din("w1", [2, D, NIN1])
    mu_d = din("mu", [2, NLERP])
    colp_d = din("colp", [2, 128, 16])
    w2_d = din("w2h", [2, 64, 256])
    a2_d = din("a2h", [2, 64, 256])
    g2_d = din("g2h", [2, 160, 256])
    wg_d = din("wg", [D, 2 * D])
    bgT_d = din("bgT", [128, 16])
    woa_d = din("woa", [512, D])
    wob_d = din("wob", [512, D])
    wout_d = din("wout", [D, D])
    wup_d = din("wup", [D, 2 * DFF])
    convT_d = din("convT", [128, 4, 42])
    wdn_d = din("wdn", [DFF, D])
    SH = S // 2
    xown_d = din("x_own", [SH, D])
    xhalo_d = din("x_halo", [128, D])
    flags_d = din("flags", [128, 2])
    out_d = nc.dram_tensor("out", [SH, D], F32, kind="ExternalOutput").ap()
    ykind = "ExternalOutput" if dbg else "Internal"
    ysc_d = nc.dram_tensor("ysc", [2, 512, S], BF16, kind=ykind).ap()
    x1_d = nc.dram_tensor("x1sc", [SH, D], F32, kind=ykind).ap()
    x1h_d = nc.dram_tensor("x1halo", [128, D], F32, kind="Internal").ap()

    dbgY_d = nc.dram_tensor("dbgY", [128, TB], F32, kind="ExternalOutput").ap() if dbg else None
    dbgT = T("dbgY")
    P = Prog(nc)
    top = ExitStack()
    RC = int(os.environ.get("RCUT", "99"))
    dumped = set()

    def dump(name, b, ap, shape, dt, cond=True):
        if not dbg or not cond or name in dumped or not os.environ.get("KDUMPS"):
            return
        dumped.add(name)
        d = nc.dram_tensor("dbg_" + name, list(shape), dt, kind="ExternalOutput").ap()
        P.dma(d, ap, [b.t], [T("dd" + name)], eng="sp", sem_tile=T("ds" + name))

    def cut(k):
        if RC == k:
            P.mute = True

    uniq = {"n": 0}

    def sbuf(es, name, shape, dt):
        uniq["n"] += 1
        nm = f"{name}_{uniq['n']}"
        return B(es.enter_context(nc.sbuf_tensor(nm, list(shape), dt)), nm)

    psall = top.enter_context(nc.psum_tensor("psall", [128, 8 * 512], F32))
    banks = [B(psall[:, i * 512:(i + 1) * 512], f"bank{i}") for i in range(8)]
    for b_ in banks:
        b_.t.psum = True
    rot = {"i": 0, "l": banks[3:]}

    def nb():
        b = rot["l"][rot["i"] % len(rot["l"])]
        rot["i"] += 1
        return b

    ident = sbuf(top, "ident", [128, 128], BF16)
    P.ms(ident[:], 1.0, [ident.t])
    P.op("pool", lambda e: e.affine_select(out=ident[:], in_=ident[:], pattern=[[1, 128]],
                                           compare_op=ALU.is_equal, fill=0.0, base=0, channel_multiplier=-1),
         [ident.t], [ident.t])
    blk1 = sbuf(top, "blk1", [128, 128], BF16)
    blk64 = sbuf(top, "blk64", [128, 128], BF16)
    for bt, val in ((blk1, 1.0), (blk64, 1.0 / 64)):
        P.ms(bt[:], 0.0, [bt.t])
        P.ms(bt[0:64, 0:64], val, [bt.t])
        P.ms(bt[64:128, 64:128], val, [bt.t])
    modT = sbuf(top, "modT", [128, 48], F32)
    grow = sbuf(top, "grow", [128, 2 * D], F32)
    G1 = sbuf(top, "G1", [128, 8], F32)
    fl = sbuf(top, "fl", [128, 2], F32)
    P.dma(fl[:], flags_d, [], [fl.t])
    G2 = sbuf(top, "G2", [128, 8], F32)

    with ExitStack() as es:
        cT = sbuf(es, "cTs", [128, 8], F32)
        sc = sbuf(es, "sc", [128, 8], F32)
        screp = sbuf(es, "screp", [128, 8, 128], F32)
        badaT = sbuf(es, "badaTs", [128, 48], F32)
        ng = sbuf(es, "ng", [128, 16], F32)
        wab = [sbuf(es, f"wab{i}", [128, 8, 512], F32) for i in range(2)]
        P.dma(cT[:], cT_d, [], [cT.t])
        P.dma(badaT[:], badaT_d, [], [badaT.t])
        P.dma(ng[:, 0:8], n1g_d, [], [ng.t])
        P.dma(ng[:, 8:16], n2g_d, [], [ng.t])
        P.dma(grow[:, 0:D], bada_g_d[0:1, :].partition_broadcast(128), [], [grow.t])
        P.dma(grow[:, D:2 * D], bada_g_d[1:2, :].partition_broadcast(128), [], [grow.t])
        P.act(sc[:], cT[:], AF.Silu, [cT.t], [sc.t])
        P.cp(screp[:], sc[:].unsqueeze(2).to_broadcast([128, 8, 128]), [sc.t], [screp.t])
        pm = banks[0]
        wv = wada_d.rearrange("(kc p) n -> p kc n", p=128)
        for blk in range(12):
            wb = wab[blk % 2]
            P.dma(wb[:], wv[:, :, blk * 512:(blk + 1) * 512], [], [wb.t])
            for j in range(4):
                oc = blk * 4 + j
                for kc in range(8):
                    P.mm(pm[:, oc:oc + 1], wb[:, kc, j * 128:(j + 1) * 128], sc[:, kc:kc + 1],
                         [wb.t, sc.t], [pm.t], start=(kc == 0), stop=(kc == 7), acc=(oc > 0 or kc > 0))
            gi = {4: 0, 5: 1, 10: 2, 11: 3}.get(blk)
            if gi is not None:
                pr = nb()
                for kc in range(8):
                    P.mm(pr[:, :], screp[:, kc, :], wb[:, kc, :], [wb.t, screp.t], [pr.t],
                         start=(kc == 0), stop=(kc == 7), acc=(kc > 0))
                P.tt(grow[:, gi * 512:(gi + 1) * 512], pr[:, :], grow[:, gi * 512:(gi + 1) * 512], ALU.add,
                     [pr.t, grow.t], [grow.t])
        P.tt(modT[:], pm[:, 0:48], badaT[:], ALU.add, [pm.t, badaT.t], [modT.t])
        P.stt(G1[:], modT[:, 8:16], 1.0, ng[:, 0:8], ALU.add, ALU.mult, [modT.t, ng.t], [G1.t])
        P.stt(G2[:], modT[:, 32:40], 1.0, ng[:, 8:16], ALU.add, ALU.mult, [modT.t, ng.t], [G2.t])
    SH1 = modT[:, 0:8]
    SH2 = modT[:, 24:32]
    P.barrier()

    def norm_T(es_bufs, xt, hT, col0, Gm, SHm):
        st, xn, junk = es_bufs
        P.act(junk[:], xt[:], AF.Square, [xt.t], [junk.t, st.t], accum_out=st[:, 0:1])
        P.act(st[:, 1:2], st[:, 0:1], AF.Ln, [st.t], [st.t], scale=1.0 / D, bias=1e-6)
        P.act(st[:, 2:3], st[:, 1:2], AF.Exp, [st.t], [st.t], scale=-0.5)
        P.act(xn[:], xt[:], AF.Copy, [xt.t, st.t], [xn.t], scale=st[:, 2:3])
        pb = nb()
        pv = pb[:].bitcast(BF16)
        for kc in range(8):
            P.tr(pv[:, kc * 128:(kc + 1) * 128], xn[:, kc * 128:(kc + 1) * 128], ident[:],
                 [xn.t, ident.t], [pb.t], acc=(kc > 0))
        dst = hT[:, :, col0:col0 + 128]
        P.tt(dst, pv.rearrange("p (k t) -> p k t", t=128), Gm.unsqueeze(2).to_broadcast([128, 8, 128]),
             ALU.mult, [pb.t, G1.t, G2.t, modT.t], [hT.t])
        P.tt(dst, dst, SHm.unsqueeze(2).to_broadcast([128, 8, 128]), ALU.add, [hT.t, modT.t], [hT.t])

    yT = T("ysc")

    def load_x_block(blk, xts, nbufs, hT, col_off):
        t0 = blk * TB
        for ti in range(4):
            xt = xts[ti % 2]
            P.dma(xt[:], x_d[t0 + ti * 128:t0 + (ti + 1) * 128, :], [], [xt.t])
            norm_T(nbufs, xt, hT, col_off + ti * 128, G1[:, :], SH1)

    for hh in range(2):
      if "R" in phases:
        with ExitStack() as es:
            Wb = sbuf(es, "Wb", [128, 8, NLERP], BF16)
            Wmu = sbuf(es, "Wmu", [128, 8, NLERP], BF16)
            colp = sbuf(es, "colp", [128, 24], F32)
            w2b = sbuf(es, "w2b", [64, 256], BF16)
            a2b = sbuf(es, "a2b", [64, 256], BF16)
            g2b0 = sbuf(es, "g2b0", [128, 256], BF16)
            g2b1 = sbuf(es, "g2b1", [32, 256], BF16)
            w1v = w1_d[hh].rearrange("(kc p) n -> p kc n", p=128)
            for kc in range(8):
                P.dma(Wb[:, kc, :], w1v[:, kc, 0:NLERP], [], [Wb.t], eng="pool")
            P.dma(colp[:, 0:16], colp_d[hh], [], [colp.t])
            P.dma(w2b[:], w2_d[hh], [], [w2b.t], eng="pool")
            P.dma(a2b[:], a2_d[hh], [], [a2b.t], eng="pool")
            P.dma(g2b0[:], g2_d[hh, 0:128, :], [], [g2b0.t], eng="pool")
            P.dma(g2b1[:], g2_d[hh, 128:160, :], [], [g2b1.t], eng="pool")
            MU = sbuf(es, "MU", [128, NLERP], F32)
            P.dma(MU[:], mu_d[hh:hh + 1, :].partition_broadcast(128), [], [MU.t])
            for kc in range(8):
                P.tt(Wmu[:, kc, :], Wb[:, kc, :], MU[:], ALU.mult, [Wb.t, MU.t], [Wmu.t])
                P.tt(Wb[:, kc, :], Wb[:, kc, :], Wmu[:, kc, :], ALU.subtract, [Wb.t, Wmu.t], [Wb.t])
            for g in range(2):
                P.ts(colp[:, 16 + g:17 + g], colp[:, 7 * g + 3:7 * g + 4], -1.0, 1.0, ALU.mult, ALU.add,
                     [colp.t], [colp.t])

            def cpar(g, j):
                return colp[:, 7 * g + j:7 * g + j + 1]

            msk = {}
            for nm_, cmp_, cm_, st_ in (("SU", ALU.is_gt, -1, 1), ("IU", ALU.is_ge, -1, 1),
                                        ("SL", ALU.is_gt, 1, -1), ("ID", ALU.is_equal, -1, 1)):
                mt_ = sbuf(es, "m" + nm_, [64, NCH, C], F32)
                P.ms(mt_[:], 1.0, [mt_.t])
                P.op("pool", (lambda mt_=mt_, cmp_=cmp_, cm_=cm_, st_=st_:
                              lambda e: e.affine_select(out=mt_[:], in_=mt_[:], pattern=[[0, NCH], [st_, C]],
                                                        compare_op=cmp_, fill=0.0, base=0,
                                                        channel_multiplier=cm_))(),
                     [mt_.t], [mt_.t])
                msk[nm_] = mt_
            mSU, mIU, mSL, idf = msk["SU"], msk["IU"], msk["SL"], msk["ID"]
            smask = sbuf(es, "smask", [128, TB], F32)
            P.ms(smask[:], 1.0, [smask.t])
            P.ms(smask[:].rearrange("p (c t) -> p c t", t=C)[:, :, 0:1], 0.0, [smask.t])

            hTs = [sbuf(es, f"hT{i}", [128, 8, TB + 1], BF16) for i in range(2)]
            hcur = [hTs[0]]
            P.ms(hTs[0][:, :, 0:1], 0.0, [hTs[0].t])
            xts = [sbuf(es, f"xt{i}", [128, D], F32) for i in range(2)]
            nbufs = (sbuf(es, "nst", [128, 4], F32), sbuf(es, "xn", [128, D], BF16), sbuf(es, "junk", [128, D], BF16))
            ST32 = [sbuf(es, f"ST32_{g}", [128, C], F32) for g in range(2)]
            STb = [sbuf(es, f"STb_{g}", [128, C], BF16) for g in range(2)]
            for g in range(2):
                P.ms(ST32[g][:], 0.0, [ST32[g].t])
                P.ms(STb[g][:], 0.0, [STb[g].t])
            tanhwd = sbuf(es, "tanhwd", [64, TB], BF16)
            adsb = sbuf(es, "adsb", [64, TB], BF16)
            sgd0 = sbuf(es, "sgd0", [128, TB], BF16)
            sgd1 = sbuf(es, "sgd1", [32, TB], BF16)

            def f32b(name):
                return sbuf(es, name, [128, TB], F32)

            def b16b(name):
                return sbuf(es, name, [128, TB], BF16)

            r_sb, k_sb, v_sb, sg, a_sb, gg, rn, kk, k2, bb, Lsg, Eneg, Epos, tmpa = [
                f32b(n) for n in ("r_sb", "k_sb", "v_sb", "sg", "a_sb", "gg", "rn", "kk", "k2", "bb", "Lsg",
                                  "Eneg", "Epos", "tmpa")]
            Lx, Eprev, Egc = tmpa, rn, a_sb
            ysb, dd, m2, var, bon = Lsg, tmpa, Eneg, rn, a_sb
            sq, Bt, Kt, BG, KG, vb = [b16b(n) for n in ("sq", "Bt", "Kt", "BG", "KG", "vb")]
            yb, ysqb, rkb = Bt, Kt, BG
            AR = sbuf(es, "AR", [128, NCH, 2, C], BF16)
            BGt, KGt, Vt, Att = [sbuf(es, n, [64, NCH, 128], BF16) for n in ("BGt", "KGt", "Vt", "Att")]
            Pm = [[sbuf(es, f"Pm{h}_{i}", [64, NCH, C], F32) for i in range(2)] for h in range(2)]
            PmT = [[sbuf(es, f"PmT{h}_{i}", [64, NCH, C], F32) for i in range(2)] for h in range(2)]
            Rm = [[sbuf(es, f"Rm{h}_{i}", [64, NCH, C], F32) for i in range(2)] for h in range(2)]
            rot["l"] = [banks[0]] + banks[3:]
            TTb = [sbuf(es, f"TTb{h}", [64, NCH, C], BF16) for h in range(2)]
            MakT = [sbuf(es, f"MakT{h}", [64, NCH, C], BF16) for h in range(2)]
            MrbT = [sbuf(es, f"MrbT{h}", [64, NCH, C], BF16) for h in range(2)]
            MrkT = [sbuf(es, f"MrkT{h}", [64, NCH, C], BF16) for h in range(2)]
            Xs = sbuf(es, "Xs", [64, NCH, 128], BF16)
            Ut = sbuf(es, "Ut", [64, NCH, 128], F32)
            WtT = sbuf(es, "WtT", [128, NCH, C], BF16)
            Ub = [sbuf(es, f"Ub{i}", [64, 128], BF16) for i in range(2)]
            yout = [b16b(f"yout{i}") for i in range(2)]

            def proj(c0, M):
                pb = nb()
                n = 0
                for kc in range(8):
                    hT = hcur[0]
                    P.mm(pb[0:M, :], Wb[:, kc, c0:c0 + M], hT[:, kc, 1:TB + 1], [Wb.t, hT.t], [pb.t],
                         start=(n == 0), stop=False, acc=(n > 0))
                    n += 1
                    P.mm(pb[0:M, :], Wmu[:, kc, c0:c0 + M], hT[:, kc, 0:TB], [Wmu.t, hT.t], [pb.t],
                         start=False, stop=(n == 15), acc=True)
                    n += 1
                return pb

            v3 = lambda bf: bf[:].rearrange("p (c t) -> p c t", t=C)
            pv3 = lambda pb: pb[0:64, :].rearrange("p (c t) -> p c t", t=C)

            OWN0 = SH // TB
            HB = OWN0 - 1
            for blk in range(NB):
                t0 = blk * TB
                full = blk >= HB
                hcur[0] = hTs[blk % 2]

                def prefetch_next(blk=blk):
                    nb_ = blk + 1
                    if nb_ >= NB:
                        return
                    hc, hn = hTs[blk % 2], hTs[nb_ % 2]
                    if nb_ == OWN0:
                        P.ts(hn[:, :, 0:1], hc[:, :, TB:TB + 1], fl[:, 1:2], None, ALU.mult, ALU.bypass, [hc.t, fl.t], [hn.t])
                    else:
                        P.cp(hn[:, :, 0:1], hc[:, :, TB:TB + 1], [hc.t], [hn.t])
                    load_x_block(nb_, xts, nbufs, hn, 1)
                if blk == 0:
                    load_x_block(0, xts, nbufs, hTs[0], 1)
                pb = proj(768, 64)
                P.act(tanhwd[:], pb[0:64, :], AF.Tanh, [pb.t], [tanhwd.t])
                pb = proj(832, 64)
                P.act(adsb[:], pb[0:64, :], AF.Copy, [pb.t], [adsb.t])
                if full:
                    pb = proj(896, 128)
                    P.act(sgd0[:], pb[:, :], AF.Sigmoid, [pb.t], [sgd0.t])
                    pb = proj(1024, 32)
                    P.act(sgd1[:], pb[0:32, :], AF.Sigmoid, [pb.t], [sgd1.t])
                cut(1)
                for g in range(2):
                    ch = slice(g * 128, (g + 1) * 128)
                    if full:
                        pb = proj(0 + g * 128, 128)
                        P.act(r_sb[:], pb[:, :], AF.Copy, [pb.t], [r_sb.t])
                    pb = proj(256 + g * 128, 128)
                    P.act(k_sb[:], pb[:, :], AF.Copy, [pb.t], [k_sb.t])
                    pb = proj(512 + g * 128, 128)
                    P.act(v_sb[:], pb[:, :], AF.Copy, [pb.t], [v_sb.t])
                    P.cp(vb[:], v_sb[:], [v_sb.t], [vb.t])
                    pb = nb()
                    P.mm(pb[:, :], w2b[:, ch], tanhwd[:], [w2b.t, tanhwd.t], [pb.t])
                    P.act(sg[:], pb[:, :], AF.Sigmoid, [pb.t, colp.t], [sg.t], bias=cpar(g, 0))
                    pb = nb()
                    P.mm(pb[:, :], a2b[:, ch], adsb[:], [a2b.t, adsb.t], [pb.t])
                    P.act(a_sb[:], pb[:, :], AF.Sigmoid, [pb.t, colp.t], [a_sb.t], bias=cpar(g, 1))
                    if full:
                        pb = nb()
                        P.mm(pb[:, :], g2b0[:, ch], sgd0[:], [g2b0.t, sgd0.t], [pb.t], start=True, stop=False)
                        P.mm(pb[:, :], g2b1[:, ch], sgd1[:], [g2b1.t, sgd1.t], [pb.t], start=False, stop=True, acc=True)
                        P.act(gg[:], pb[:, :], AF.Copy, [pb.t], [gg.t])
                    P.act(sq[:], k_sb[:], AF.Square, [k_sb.t, colp.t], [sq.t], scale=cpar(g, 2))
                    pb = nb()
                    P.mm(pb[:, :], blk1[:], sq[:], [blk1.t, sq.t], [pb.t])
                    P.act(rn[:], pb[:, :], AF.Ln, [pb.t], [rn.t], bias=1e-24)
                    P.act(rn[:], rn[:], AF.Exp, [rn.t], [rn.t], scale=-0.5)
                    P.stt(kk[:], k_sb[:], cpar(g, 2), rn[:], ALU.mult, ALU.mult, [k_sb.t, colp.t, rn.t], [kk.t])
                    P.ts(tmpa[:], a_sb[:], cpar(g, 3), colp[:, 16 + g:17 + g], ALU.mult, ALU.add, [a_sb.t, colp.t], [tmpa.t])
                    P.tt(k2[:], k_sb[:], tmpa[:], ALU.mult, [k_sb.t, tmpa.t], [k2.t])
                    P.tt(bb[:], kk[:], a_sb[:], ALU.mult, [kk.t, a_sb.t], [bb.t])
                    P.op("dve", lambda e: e.tensor_tensor_scan(out=Lsg[:], data0=smask[:], data1=sg[:], initial=0.0,
                                                               op0=ALU.mult, op1=ALU.add), [smask.t, sg.t], [Lsg.t])
                    P.tt(Lx[:], Lsg[:], sg[:], ALU.subtract, [Lsg.t, sg.t], [Lx.t])
                    P.act(Eneg[:], Lsg[:], AF.Exp, [Lsg.t], [Eneg.t], scale=CDEC)
                    P.act(Epos[:], Lsg[:], AF.Exp, [Lsg.t], [Epos.t], scale=-CDEC)
                    P.act(Eprev[:], Lx[:], AF.Exp, [Lx.t], [Eprev.t], scale=-CDEC)
                    gamC = v3(Epos)[:, :, C - 1:C]
                    P.tt(v3(Egc), v3(Eneg), gamC.to_broadcast([128, NCH, C]), ALU.mult, [Eneg.t, Epos.t], [Egc.t])
                    P.tt(AR[:, :, 0, :], v3(kk), v3(Eprev), ALU.mult, [kk.t, Eprev.t], [AR.t])
                    if full:
                        P.tt(AR[:, :, 1, :], v3(r_sb), v3(Epos), ALU.mult, [r_sb.t, Epos.t], [AR.t])
                    P.tt(Bt[:], bb[:], Eneg[:], ALU.mult, [bb.t, Eneg.t], [Bt.t])
                    P.tt(Kt[:], k2[:], Eneg[:], ALU.mult, [k2.t, Eneg.t], [Kt.t])
                    P.tt(BG[:], bb[:], Egc[:], ALU.mult, [bb.t, Egc.t], [BG.t])
                    P.tt(KG[:], k2[:], Egc[:], ALU.mult, [k2.t, Egc.t], [KG.t])
                    D0 = (hh == 0 and blk == 0 and g == 0)
                    for nm_, b_, dt_ in (("r", r_sb, F32), ("k", k_sb, F32), ("v", v_sb, F32), ("sg", sg, F32),
                                         ("kk", kk, F32), ("k2", k2, F32), ("bb", bb, F32), ("Lsg", Lsg, F32),
                                         ("Eneg", Eneg, F32), ("Epos", Epos, F32), ("Eprev", Eprev, F32), ("Egc", Egc, F32),
                                         ("gg", gg, F32), ("Bt", Bt, BF16), ("Kt", Kt, BF16), ("BG", BG, BF16), ("KG", KG, BF16)):
                        dump(nm_, b_, b_[:], [128, TB], dt_, D0)
                    dump("AR", AR, AR[:].rearrange("p c s t -> p (c s t)"), [128, NCH * 2 * C], BF16, D0)
                    cut(2)
                    for src_ap, srct, dst in ((lambda c: BG[:, c * C:(c + 1) * C], BG.t, BGt),
                                              (lambda c: KG[:, c * C:(c + 1) * C], KG.t, KGt),
                                              (lambda c: vb[:, c * C:(c + 1) * C], vb.t, Vt),
                                              (lambda c: AR[:, c, 0, :], AR.t, Att)):
                        pb = nb()
                        pv = pb[:].bitcast(BF16)
                        for c in range(NCH):
                            P.tr(pv[0:64, c * 128:(c + 1) * 128], src_ap(c), ident[:], [srct, ident.t], [pb.t], acc=(c > 0))
                        P.cp(dst[:].rearrange("p c k -> p (c k)"), pv[0:64, 0:NCH * 128], [pb.t], [dst.t])
                    for nm_, b_ in (("BGt", BGt), ("KGt", KGt), ("Vt", Vt), ("Att", Att)):
                        dump(nm_, b_, b_[:].rearrange("p c k -> p (c k)"), [64, NCH * 128], BF16, D0)
                    cut(3)
                    if g == 0:
                        prefetch_next()
                    HP = [slice(0, 64), slice(64, 128)]

                    def mat(lhs_fn, rhs_fn, Rr):
                        pb = nb()
                        for c in range(NCH):
                            P.mm(pb[0:64, c * C:(c + 1) * C], lhs_fn(c), rhs_fn(c), Rr, [pb.t],
                                 start=(c == 0), stop=True, acc=(c > 0))
                        return pb
                    Btc = lambda h: (lambda c: Bt[HP[h], c * C:(c + 1) * C])
                    Ktc = lambda h: (lambda c: Kt[HP[h], c * C:(c + 1) * C])
                    Atc = lambda h: (lambda c: AR[HP[h], c, 0, :])
                    Rtc = lambda h: (lambda c: AR[HP[h], c, 1, :])
                    f3 = lambda b_: b_[:].rearrange("p c t -> p (c t)")
                    for h in range(2):
                        pb = mat(Btc(h), Atc(h), [Bt.t, AR.t])
                        P.stt(Pm[h][0][:], pv3(pb), -1.0, mSU[:], ALU.mult, ALU.mult, [pb.t, mSU.t], [Pm[h][0].t])
                        pb = mat(Atc(h), Btc(h), [Bt.t, AR.t])
                        P.stt(PmT[h][0][:], pv3(pb), -1.0, mSL[:], ALU.mult, ALU.mult, [pb.t, mSL.t], [PmT[h][0].t])
                    for h in range(2):
                        P.tt(Rm[h][0][:], Pm[h][0][:], idf[:], ALU.add, [Pm[h][0].t, idf.t], [Rm[h][0].t])
                    for h in range(2):
                        pb = mat(Ktc(h), Atc(h), [Kt.t, AR.t])
                        P.tt(MakT[h][:], pv3(pb), mSU[:], ALU.mult, [pb.t, mSU.t], [MakT[h].t])
                    cur = 0
                    rcur = 0
                    for lvl in range(1, 6):
                        pbs = []
                        for h in range(2):
                            Pc, PcT = Pm[h][cur], PmT[h][cur]
                            pb = mat(lambda c, PcT=PcT: PcT[:, c, :], lambda c, Pc=Pc: Pc[:, c, :], [Pc.t, PcT.t])
                            pb2 = mat(lambda c, Pc=Pc: Pc[:, c, :], lambda c, PcT=PcT: PcT[:, c, :], [Pc.t, PcT.t])
                            pbs.append((pb, pb2))
                        for h in range(2):
                            Pn, PnT = Pm[h][1 - cur], PmT[h][1 - cur]
                            P.cp(Pn[:], pv3(pbs[h][0]), [pbs[h][0].t], [Pn.t])
                            P.act(PnT[:], pv3(pbs[h][1]), AF.Copy, [pbs[h][1].t], [PnT.t])
                        pb3s = []
                        for h in range(2):
                            PnT = PmT[h][1 - cur]
                            Rc = Rm[h][rcur]
                            pb3s.append(mat(lambda c, PnT=PnT: PnT[:, c, :], lambda c, Rc=Rc: Rc[:, c, :], [PnT.t, Rc.t]))
                        for h in range(2):
                            Rc, Rn = Rm[h][rcur], Rm[h][1 - rcur]
                            P.tt(Rn[:], pv3(pb3s[h]), Rc[:], ALU.add, [pb3s[h].t, Rc.t], [Rn.t])
                        if lvl == 1 and full:
                            for h in range(2):
                                pb = mat(Btc(h), Rtc(h), [Bt.t, AR.t])
                                P.tt(MrbT[h][:], pv3(pb), mIU[:], ALU.mult, [pb.t, mIU.t], [MrbT[h].t])
                        if lvl == 2 and full:
                            for h in range(2):
                                pb = mat(Ktc(h), Rtc(h), [Kt.t, AR.t])
                                P.tt(MrkT[h][:], pv3(pb), mIU[:], ALU.mult, [pb.t, mIU.t], [MrkT[h].t])
                        if lvl == 3:
                            for h in range(2):
                                pb = mat(lambda c, h=h: MakT[h][:, c, :], lambda c, h=h: Vt[:, c, HP[h]], [MakT[h].t, Vt.t])
                                P.cp(Xs[:, :, HP[h]], pv3(pb), [pb.t], [Xs.t])
                        cur = 1 - cur
                        rcur = 1 - rcur
                    for h in range(2):
                        P.cp(TTb[h][:], Rm[h][rcur][:], [Rm[h][rcur].t], [TTb[h].t])
                    for h in range(2):
                        pb = mat(lambda c, h=h: TTb[h][:, c, :], lambda c, h=h: Xs[:, c, HP[h]], [TTb[h].t, Xs.t])
                        P.cp(Ut[:, :, HP[h]], pv3(pb), [pb.t], [Ut.t])
                        pb = nb()
                        for c in range(NCH):
                            P.mm(pb[HP[h], c * C:(c + 1) * C], Att[:, c, HP[h]], TTb[h][:, c, :], [Att.t, TTb[h].t], [pb.t],
                                 start=(c == 0), stop=True, acc=(c > 0))
                        P.cp(WtT[HP[h], :, :], pb[HP[h], :].rearrange("p (c t) -> p c t", t=C), [pb.t], [WtT.t])
                    dump("TT", TTb[0], f3(TTb[0]), [64, NCH * C], BF16, D0)
                    dump("MrbT", MrbT[0], f3(MrbT[0]), [64, NCH * C], BF16, D0)
                    dump("MakT", MakT[0], f3(MakT[0]), [64, NCH * C], BF16, D0)
                    dump("MrkT", MrkT[0], f3(MrkT[0]), [64, NCH * C], BF16, D0)
                    dump("Xs", Xs, Xs[:].rearrange("p c k -> p (c k)"), [64, NCH * 128], BF16, D0)
                    dump("Ut", Ut, Ut[:].rearrange("p c k -> p (c k)"), [64, NCH * 128], F32, D0)
                    dump("WtT", WtT, WtT[:].rearrange("p c t -> p (c t)"), [128, NCH * C], BF16, D0)
                    cut(6)
                    py = banks[1 + g]
                    for c in range(NCH):
                        U = Ub[c % 2]
                        pu = nb()
                        for h in range(2):
                            hp = slice(h * 64, (h + 1) * 64)
                            P.mm(pu[0:64, hp], WtT[hp, c, :], STb[g][hp, :], [WtT.t, STb[g].t], [pu.t],
                                 start=True, stop=True, acc=(h > 0))
                        P.stt(U[:], pu[0:64, 0:128], -1.0, Ut[:, c, :], ALU.mult, ALU.subtract, [pu.t, Ut.t], [U.t])
                        dump("U0", U, U[:], [64, 128], BF16, D0 and c == 0)
                        dump("U1", U, U[:], [64, 128], BF16, D0 and c == 1)
                        for h in (range(2) if full else ()):
                            hp = slice(h * 64, (h + 1) * 64)
                            oy = py[hp, c * C:(c + 1) * C]
                            P.mm(oy, Vt[:, c, hp], MrkT[h][:, c, :], [Vt.t, MrkT[h].t], [py.t], start=True, stop=False,
                                 acc=(c > 0 or h > 0))
                            P.mm(oy, STb[g][hp, :], AR[hp, c, 1, :], [STb[g].t, AR.t], [py.t], start=False, stop=False, acc=True)
                            P.mm(oy, U[:, hp], MrbT[h][:, c, :], [U.t, MrbT[h].t], [py.t], start=False, stop=True, acc=True)
                        psn = nb()
                        for h in range(2):
                            hp = slice(h * 64, (h + 1) * 64)
                            P.mm(psn[hp, 0:C], KGt[:, c, hp], Vt[:, c, hp], [KGt.t, Vt.t], [psn.t], start=True, stop=False,
                                 acc=(h > 0))
                            P.mm(psn[hp, 0:C], BGt[:, c, hp], U[:, hp], [BGt.t, U.t], [psn.t], start=False, stop=True, acc=True)
                        gcol = v3(Epos)[:, c, C - 1:C]
                        P.stt(STb[g][:], ST32[g][:], gcol, psn[:, 0:C], ALU.mult, ALU.add, [ST32[g].t, Epos.t, psn.t], [STb[g].t])
                        P.stt(ST32[g][:], ST32[g][:], gcol, psn[:, 0:C], ALU.mult, ALU.add, [ST32[g].t, Epos.t, psn.t], [ST32[g].t])
                    if blk == OWN0 - 1:
                        P.ts(STb[g][:], STb[g][:], fl[:, 1:2], None, ALU.mult, ALU.bypass, [STb[g].t, fl.t], [STb[g].t])
                        P.ts(ST32[g][:], ST32[g][:], fl[:, 1:2], None, ALU.mult, ALU.bypass, [ST32[g].t, fl.t], [ST32[g].t])
                    P.mute = not full
                    cut(7)
                    P.act(ysb[:], py[:, :], AF.Copy, [py.t], [ysb.t])
                    if dbg and os.environ.get("KDUMPS") and hh == 0 and blk == 0 and g == 0:
                        P.dma(dbgY_d, ysb[:], [ysb.t], [dbgT], eng="sp", sem_tile=ysb.t)
                    P.cp(yb[:], ysb[:], [ysb.t], [yb.t])
                    P.act(ysqb[:], ysb[:], AF.Square, [ysb.t], [ysqb.t])
                    cut(71)
                    pmean = nb()
                    P.mm(pmean[:, :], blk64[:], yb[:], [blk64.t, yb.t], [pmean.t])
                    pmsq = nb()
                    P.mm(pmsq[:, :], blk64[:], ysqb[:], [blk64.t, ysqb.t], [pmsq.t])
                    cut(72)
                    P.stt(dd[:], pmean[:, :], -1.0, ysb[:], ALU.mult, ALU.add, [ysb.t, pmean.t], [dd.t])
                    P.act(m2[:], pmean[:, :], AF.Square, [pmean.t], [m2.t])
                    P.tt(var[:], pmsq[:, :], m2[:], ALU.subtract, [pmsq.t, m2.t], [var.t])
                    cut(73)
                    P.act(var[:], var[:], AF.Ln, [var.t], [var.t], bias=64e-5)
                    P.act(var[:], var[:], AF.Exp, [var.t], [var.t], scale=-0.5)
                    cut(74)
                    P.tt(dd[:], dd[:], var[:], ALU.mult, [dd.t, var.t], [dd.t])
                    P.ts(dd[:], dd[:], cpar(g, 5), cpar(g, 6), ALU.mult, ALU.add, [dd.t, colp.t], [dd.t])
                    cut(8)
                    P.stt(rkb[:], r_sb[:], cpar(g, 4), k2[:], ALU.mult, ALU.mult, [r_sb.t, colp.t, k2.t], [rkb.t])
                    pbon = nb()
                    P.mm(pbon[:, :], blk1[:], rkb[:], [blk1.t, rkb.t], [pbon.t])
                    P.tt(bon[:], pbon[:, :], v_sb[:], ALU.mult, [pbon.t, v_sb.t], [bon.t])
                    P.tt(dd[:], dd[:], bon[:], ALU.add, [dd.t, bon.t], [dd.t])
                    yo_ = yout[g]
                    P.tt(yo_[:], dd[:], gg[:], ALU.mult, [dd.t, gg.t], [yo_.t])
                    cut(9)
                    P.dma(ysc_d[0, hh * 256 + g * 128:hh * 256 + (g + 1) * 128, t0:t0 + TB], yo_[:], [yo_.t], [yT],
                          eng="sp", sem_tile=yo_.t)
                    P.mute = False

        P.mute = False
        rot["l"] = banks[3:]
        P.barrier()
      if "A" in phases:
        with ExitStack() as es:
            NSB = 768
            Wb = sbuf(es, "WbA", [128, 8, NSB], BF16)
            colp = sbuf(es, "colpA", [128, 24], F32)
            w1v = w1_d[hh].rearrange("(kc p) n -> p kc n", p=128)
            for kc in range(8):
                P.dma(Wb[:, kc, :], w1v[:, kc, NLERP:NLERP + NSB], [], [Wb.t], eng="pool")
            P.dma(colp[:, 0:16], colp_d[hh], [], [colp.t])
            P.ts(colp[:, 18:19], colp[:, 14:15], 0.125, None, ALU.mult, ALU.bypass, [colp.t], [colp.t])
            Tm = sbuf(es, "Tm", [128, 128], BF16)
            P.ms(Tm[:], -1.0, [Tm.t])
            P.op("pool", lambda e: e.affine_select(out=Tm[:], in_=Tm[:], pattern=[[-1, 128]], compare_op=ALU.is_ge,
                                                   fill=0.0, base=0, channel_multiplier=1), [Tm.t], [Tm.t])
            NO = sbuf(es, "NO", [128, 128], BF16)
            P.ms(NO[:], -1.0, [NO.t])
            cmask = sbuf(es, "cmask", [128, 128], BF16)
            P.ms(cmask[:], 1.0, [cmask.t])
            P.op("pool", lambda e: e.affine_select(out=cmask[:], in_=cmask[:], pattern=[[1, 128]], compare_op=ALU.is_gt,
                                                   fill=0.0, base=0, channel_multiplier=-1), [cmask.t], [cmask.t])
            KT = [sbuf(es, f"KT{g}", [128, S], BF16) for g in range(2)]
            Vres = sbuf(es, "Vres", [128, NKB, 256], BF16)
            KT_t = [[T(f"kt{g}_{b}") for b in range(NB)] for g in range(2)]
            V_t = [T(f"v{b}") for b in range(NB)]
            hTsA = [sbuf(es, f"hTA{i}", [128, 8, TB], BF16) for i in range(2)]
            hcurA = [hTsA[0]]
            xts = [sbuf(es, f"xtA{i}", [128, D], F32) for i in range(2)]
            nbufs = (sbuf(es, "nstA", [128, 4], F32), sbuf(es, "xnA", [128, D], BF16), sbuf(es, "junkA", [128, D], BF16))
            rn = sbuf(es, "rnA", [128, TB], F32)
            sq = sbuf(es, "sqA", [128, TB], BF16)
            vb = sbuf(es, "vbA", [128, TB], BF16)
            QT = [sbuf(es, f"QT{g}", [128, TB], BF16) for g in range(2)]
            E2 = [sbuf(es, f"E2_{i}", [128, 2, TB], F32) for i in range(2)]
            L2 = [sbuf(es, f"L2_{i}", [128, 2, TB], BF16) for i in range(2)]
            Ls2 = sbuf(es, "Ls2", [128, 2, TB], BF16)
            A2 = [sbuf(es, f"A2_{i}", [128, 2, TB], BF16) for i in range(2)]
            qfA = sbuf(es, "qfA", [128, TB], F32)
            NSET = 3
            p1S = [[banks[1 + 2 * k + h] for h in range(2)] for k in range(NSET)]
            p1P = [psall[:, (1 + 2 * k) * 512:(3 + 2 * k) * 512].rearrange("p (s q) -> p s q", s=2) for k in range(NSET)]
            rot["l"] = [banks[7]]
            yBo = [sbuf(es, f"yBo{i}", [128, TB], BF16) for i in range(2)]

            def projA(c0):
                pb = nb()
                for kc in range(8):
                    hT = hcurA[0]
                    P.mm(pb[:, :], Wb[:, kc, c0:c0 + 128], hT[:, kc, :], [Wb.t, hT.t], [pb.t],
                         start=(kc == 0), stop=(kc == 7), acc=(kc > 0))
                return pb

            for blk in range(NB):
                t0 = blk * TB
                full = blk >= SH // TB - 1
                hcurA[0] = hTsA[blk % 2]
                if blk == 0:
                    load_x_block(0, xts, nbufs, hTsA[0], 0)
                for g in range(2):
                    for which, c0 in (("q", 0), ("k", 256)):
                        if which == "q" and not full:
                            continue
                        pb = projA(c0 + g * 128)
                        P.act(qfA[:], pb[:, :], AF.Copy, [pb.t], [qfA.t])
                        P.act(sq[:], qfA[:], AF.Square, [qfA.t], [sq.t])
                        pn = nb()
                        P.mm(pn[:, :], blk64[:], sq[:], [blk64.t, sq.t], [pn.t])
                        P.act(rn[:], pn[:, :], AF.Ln, [pn.t], [rn.t], bias=1e-6)
                        P.act(rn[:], rn[:], AF.Exp, [rn.t], [rn.t], scale=-0.5)
                        if which == "q":
                            P.stt(QT[g][:], qfA[:], colp[:, 18:19], rn[:], ALU.mult, ALU.mult, [qfA.t, colp.t, rn.t], [QT[g].t])
                        else:
                            P.stt(KT[g][:, t0:t0 + TB], qfA[:], colp[:, 15:16], rn[:], ALU.mult, ALU.mult,
                                  [qfA.t, colp.t, rn.t], [KT_t[g][blk]])
                    pb = projA(512 + g * 128)
                    P.act(vb[:], pb[:, :], AF.Copy, [pb.t], [vb.t])
                    pb = nb()
                    pv = pb[:].bitcast(BF16)
                    for j in range(4):
                        P.tr(pv[:, j * 128:(j + 1) * 128], vb[:, j * 128:(j + 1) * 128], ident[:], [vb.t, ident.t], [pb.t], acc=(j > 0))
                    if blk < SH // TB:
                        P.ts(Vres[:, blk * 4:blk * 4 + 4, g * 128:(g + 1) * 128], pv[:, 0:512].rearrange("p (j c) -> p j c", c=128),
                             fl[:, 1:2], None, ALU.mult, ALU.bypass, [pb.t, fl.t], [V_t[blk]])
                    else:
                        P.cp(Vres[:, blk * 4:blk * 4 + 4, g * 128:(g + 1) * 128], pv[:, 0:512].rearrange("p (j c) -> p j c", c=128),
                             [pb.t], [V_t[blk]])
                if blk + 1 < NB:
                    load_x_block(blk + 1, xts, nbufs, hTsA[(blk + 1) % 2], 0)
                P.mute = not full
                nkb_ = (blk + 1) * 4
                kbs = list(range(nkb_ - 1, -1, -1))
                nst_ = len(kbs)
                cm2 = cmask[:].unsqueeze(1).to_broadcast([128, 2, 128])
                po = banks[0]

                def geom(i):
                    kb = kbs[i]
                    o = kb - blk * 4
                    q0 = max(o, 0) * 128
                    return kb, o, q0, slice(q0, TB), kb // 4

                for g in range(2):
                    P.ms(Ls2[:], 0.0, [Ls2.t], eng="dve")

                    def qk(i, g=g):
                        kb, o, q0, qs, kblk = geom(i)
                        for h in range(2):
                            hp = slice(h * 64, (h + 1) * 64)
                            p1 = p1S[i % NSET][h]
                            P.mm(p1[:, qs], KT[g][hp, kb * 128:(kb + 1) * 128], QT[g][hp, qs], [KT_t[g][kblk], QT[g].t], [p1.t])

                    def front(i):
                        kb, o, q0, qs, kblk = geom(i)
                        k_ = i % NSET
                        pt = [p1S[k_][0].t, p1S[k_][1].t]
                        E_, L_ = E2[i % 2], L2[i % 2]
                        P.act(E_[:, :, qs], p1P[k_][:, :, qs], AF.Exp, pt, [E_.t])
                        P.act(L_[:, :, qs], E_[:, :, qs], AF.Ln, [E_.t], [L_.t], bias=1.0)
                        if o >= 0:
                            P.tt(L_[:, :, q0:q0 + 128], L_[:, :, q0:q0 + 128], cm2, ALU.mult, [L_.t, cmask.t], [L_.t])

                    qk(0)
                    if nst_ > 1:
                        qk(1)
                    front(0)
                    for i in range(nst_):
                        kb, o, q0, qs, kblk = geom(i)
                        k_ = i % NSET
                        L_, A_ = L2[i % 2], A2[i % 2]
                        for h in range(2):
                            p1 = p1S[k_][h]
                            P.mm(p1[:, qs], Tm[:], L_[:, h, qs], [Tm.t, L_.t], [p1.t], start=False, stop=False, acc=True)
                            P.mm(p1[:, qs], NO[:], Ls2[:, h, qs], [NO.t, Ls2.t], [p1.t], start=False, stop=True, acc=True)
                        if kb > 0:
                            P.tt(Ls2[:, :, qs], Ls2[:, :, qs], L_[:, :, qs], ALU.add, [Ls2.t, L_.t], [Ls2.t])
                        if i + 2 < nst_:
                            qk(i + 2)
                        if i + 1 < nst_:
                            front(i + 1)
                        pt = [p1S[k_][0].t, p1S[k_][1].t]
                        P.act(A_[:, :, qs], p1P[k_][:, :, qs], AF.Exp, pt, [A_.t])
                        if o >= 0:
                            P.tt(A_[:, :, q0:q0 + 128], A_[:, :, q0:q0 + 128], cm2, ALU.mult, [A_.t, cmask.t], [A_.t])
                        for h in range(2):
                            hp = slice(h * 64, (h + 1) * 64)
                            P.mm(po[hp, qs], Vres[:, kb, (2 * g + h) * 64:(2 * g + h + 1) * 64], A_[:, h, qs], [V_t[kblk], A_.t], [po.t],
                                 start=(i == 0), stop=(kb == 0), acc=True)
                    yb_ = yBo[g]
                    P.act(yb_[:], po[:, :], AF.Copy, [po.t], [yb_.t])
                    P.dma(ysc_d[1, hh * 256 + g * 128:hh * 256 + (g + 1) * 128, t0:t0 + TB], yb_[:], [yb_.t], [yT],
                          eng="sp", sem_tile=yb_.t)
                P.mute = False

            rot["l"] = banks[3:]
        P.barrier()
    x1T = T("x1sc")
    with ExitStack() as es:
      if "2a" in phases:
          Wg = sbuf(es, "Wg", [128, 8, 2 * D], BF16)
          Woa = sbuf(es, "Woa", [128, 4, D], BF16)
          Wob = sbuf(es, "Wob", [128, 4, D], BF16)
          Wo = sbuf(es, "Wo", [128, 8, D], BF16)
          bgT = sbuf(es, "bgT", [128, 16], F32)
          for kc in range(8):
              P.dma(Wg[:, kc, :], wg_d.rearrange("(kc p) n -> p kc n", p=128)[:, kc, :], [], [Wg.t], eng="pool")
          P.dma(Woa[:], woa_d.rearrange("(kc p) n -> p kc n", p=128), [], [Woa.t], eng="pool")
          P.dma(Wob[:], wob_d.rearrange("(kc p) n -> p kc n", p=128), [], [Wob.t], eng="pool")
          for kc in range(8):
              P.dma(Wo[:, kc, :], wout_d.rearrange("(kc p) n -> p kc n", p=128)[:, kc, :], [], [Wo.t], eng="pool")
          P.dma(bgT[:], bgT_d, [], [bgT.t])
          hT2s = [sbuf(es, f"hT2_{i}", [128, 8, TB], BF16) for i in range(2)]
          xts2 = [[sbuf(es, f"xq{k}_{i}", [128, D], F32) for i in range(4)] for k in range(2)]
          nst = sbuf(es, "nst2", [128, 4], F32)
          xn = sbuf(es, "xn2", [128, D], BF16)
          junk = sbuf(es, "junk2", [128, D], BF16)
          gT = sbuf(es, "gT", [128, 16, TB], BF16)
          yA = sbuf(es, "yA", [128, 4, TB], BF16)
          yBb = sbuf(es, "yBb", [128, 4, TB], BF16)
          yA2 = sbuf(es, "yA2", [128, 4, TB], BF16)
          yB2 = sbuf(es, "yB2", [128, 4, TB], BF16)
          mT = sbuf(es, "mT", [128, 8, TB], BF16)
          tA = sbuf(es, "tA", [128, TB], F32)
          tB = sbuf(es, "tB", [128, TB], F32)
          x1s = [sbuf(es, f"x1s{i}", [128, D], F32) for i in range(2)]
          yv = [ysc_d[br].rearrange("(kc p) t -> p kc t", p=128) for br in range(2)]
          x1hT = T("x1halo")

          ysets = [(yA, yBb), (yA2, yB2)]

          def pre2a(k, nt, xsrc, ytok):
              NT = nt * 128
              for br in range(2):
                  yb_ = ysets[k][br]
                  P.dma(yb_[:, :, 0:NT], yv[br][:, :, ytok:ytok + NT], [yT], [yb_.t])
              for ti in range(nt):
                  xt = xts2[k][ti]
                  P.dma(xt[:], xsrc(ti), [], [xt.t])
                  norm_T((nst, xn, junk), xt, hT2s[k], ti * 128, G1[:, :], SH1)

          def blk2a(k, nt, x1dst, dstT, nxt):
              NT = nt * 128
              hT = hT2s[k]
              xts = xts2[k]
              yA_, yB_ = ysets[k]
              for mt in range(16):
                  pb = nb()
                  for kc in range(8):
                      P.mm(pb[:, 0:NT], Wg[:, kc, mt * 128:(mt + 1) * 128], hT[:, kc, 0:NT], [Wg.t, hT.t], [pb.t],
                           start=(kc == 0), stop=(kc == 7), acc=(kc > 0))
                  P.act(gT[:, mt, 0:NT], pb[:, 0:NT], AF.Sigmoid, [pb.t, bgT.t], [gT.t], bias=bgT[:, mt:mt + 1])
              if nxt is not None:
                  nxt()
              for mt in range(8):
                  pa = nb()
                  for kc in range(4):
                      P.mm(pa[:, 0:NT], Woa[:, kc, mt * 128:(mt + 1) * 128], yA_[:, kc, 0:NT], [Woa.t, yA_.t], [pa.t],
                           start=(kc == 0), stop=(kc == 3), acc=(kc > 0))
                  pbb = nb()
                  for kc in range(4):
                      P.mm(pbb[:, 0:NT], Wob[:, kc, mt * 128:(mt + 1) * 128], yB_[:, kc, 0:NT], [Wob.t, yB_.t], [pbb.t],
                           start=(kc == 0), stop=(kc == 3), acc=(kc > 0))
                  P.tt(tA[:, 0:NT], pa[:, 0:NT], gT[:, mt, 0:NT], ALU.mult, [pa.t, gT.t], [tA.t])
                  P.tt(tB[:, 0:NT], pbb[:, 0:NT], gT[:, 8 + mt, 0:NT], ALU.mult, [pbb.t, gT.t], [tB.t])
                  P.tt(mT[:, mt, 0:NT], tA[:, 0:NT], tB[:, 0:NT], ALU.add, [tA.t, tB.t], [mT.t])
              for ti in range(nt):
                  x1 = x1s[ti % 2]
                  for half in range(2):
                      pb = nb()
                      for kc in range(8):
                          P.mm(pb[:, :], mT[:, kc, ti * 128:(ti + 1) * 128], Wo[:, kc, half * 512:(half + 1) * 512],
                               [mT.t, Wo.t], [pb.t], start=(kc == 0), stop=(kc == 7), acc=(kc > 0))
                      cs = slice(half * 512, (half + 1) * 512)
                      P.tt(x1[:, cs], pb[:, :], grow[:, cs], ALU.mult, [pb.t, grow.t], [x1.t])
                      P.tt(x1[:, cs], x1[:, cs], xts[ti][:, cs], ALU.add, [x1.t, xts[ti].t], [x1.t])
                  P.dma(x1dst(ti), x1[:], [x1.t], [dstT], eng="sp", sem_tile=x1.t)

          items = [(1, (lambda ti: xhalo_d), SH - 128, (lambda ti: x1h_d), x1hT)]
          for blk in range(SH // TB):
              t0 = blk * TB
              items.append((4, (lambda ti, t0=t0: xown_d[t0 + ti * 128:t0 + (ti + 1) * 128, :]), SH + t0,
                            (lambda ti, t0=t0: x1_d[t0 + ti * 128:t0 + (ti + 1) * 128, :]), x1T))
          pre2a(0, items[0][0], items[0][1], items[0][2])
          for i_, (nt_, xsrc_, ytok_, x1dst_, dstT_) in enumerate(items):
              nxt_ = None
              if i_ + 1 < len(items):
                  n_ = items[i_ + 1]
                  nxt_ = (lambda k=(i_ + 1) % 2, n_=n_: pre2a(k, n_[0], n_[1], n_[2]))
              blk2a(i_ % 2, nt_, x1dst_, dstT_, nxt_)

    P.barrier()
    TB2 = 256
    NB2 = SH // TB2
    outT = T("out")
    finals = []
    with ExitStack() as es:
      if "2b" in phases:
          Wup = sbuf(es, "Wup", [128, 8, 2 * DFF], BF16)
          Wdn = sbuf(es, "Wdn", [128, 21, D], BF16)
          convT = sbuf(es, "convTs", [128, 4, 42], F32)
          for kc in range(8):
              for hf in range(2):
                  P.dma(Wup[:, kc, hf * DFF:(hf + 1) * DFF],
                        wup_d.rearrange("(kc p) n -> p kc n", p=128)[:, kc, hf * DFF:(hf + 1) * DFF], [], [Wup.t], eng="pool")
          for kc in range(21):
              P.dma(Wdn[:, kc, :], wdn_d.rearrange("(kc p) n -> p kc n", p=128)[:, kc, :], [], [Wdn.t], eng="pool")
          P.dma(convT[:], convT_d, [], [convT.t])
          h2Ts = [sbuf(es, f"h2T{i}", [128, 8, TB2], BF16) for i in range(2)]
          x1ts = [[sbuf(es, f"x1t{k}_{i}", [128, D], F32) for i in range(2)] for k in range(2)]
          h2T = h2Ts[1]
          x1t = x1ts[1]
          nst = sbuf(es, "nst3", [128, 4], F32)
          xn = sbuf(es, "xn3", [128, D], BF16)
          junk = sbuf(es, "junk3", [128, D], BF16)
          halo = sbuf(es, "halo", [128, 42, 2], F32)
          halo_t = [T(f"halo{m}") for m in range(42)]
          NRU = 4
          ups = [sbuf(es, f"ups{i}", [128, TB2 + 2], F32) for i in range(NRU)]
          c1 = [sbuf(es, f"c1_{i}", [128, TB2], F32) for i in range(NRU)]
          vals = sbuf(es, "vals", [128, 21, TB2], BF16)
          vals_t = [T(f"vals{m}") for m in range(21)]
          actT = sbuf(es, "actT", [128, 21, TB2], BF16)
          sgt = [sbuf(es, f"sgt{i}", [128, TB2], F32) for i in range(2)]
          ost = [sbuf(es, f"ost{i}", [128, D], F32) for i in range(2)]
          P.dma(x1t[0][:], x1h_d, [x1hT], [x1t[0].t])
          norm_T((nst, xn, junk), x1t[0], h2T, 0, G2[:, :], SH2)
          ph = nb()
          for mt in range(42):
              for kc in range(8):
                  P.mm(ph[:, 2 * mt:2 * mt + 2], Wup[:, kc, mt * 128:(mt + 1) * 128], h2T[:, kc, 126:128], [Wup.t, h2T.t], [ph.t],
                       start=(kc == 0), stop=(kc == 7), acc=(mt > 0 or kc > 0))
          P.ts(halo[:].rearrange("p m c -> p (m c)"), ph[:, 0:84], fl[:, 1:2], None, ALU.mult, ALU.bypass,
               [ph.t, fl.t], halo_t)
          def pre2b(blk):
              for ti in range(2):
                  xt = x1ts[blk % 2][ti]
                  P.dma(xt[:], x1_d[blk * TB2 + ti * 128:blk * TB2 + (ti + 1) * 128, :], [x1T], [xt.t])
                  norm_T((nst, xn, junk), xt, h2Ts[blk % 2], ti * 128, G2[:, :], SH2)

          pre2b(0)
          for blk in range(NB2):
              t0 = blk * TB2
              h2T = h2Ts[blk % 2]
              x1t = x1ts[blk % 2]
              for mt in range(42):
                  if mt == 21 and blk + 1 < NB2:
                      pre2b(blk + 1)
                  pb = nb()
                  for kc in range(8):
                      P.mm(pb[:, 0:TB2], Wup[:, kc, mt * 128:(mt + 1) * 128], h2T[:, kc, :], [Wup.t, h2T.t], [pb.t],
                           start=(kc == 0), stop=(kc == 7), acc=(kc > 0))
                  u = ups[mt % NRU]
                  cc = c1[mt % NRU]
                  P.act(u[:, 2:TB2 + 2], pb[:, 0:TB2], AF.Copy, [pb.t], [u.t])
                  P.act(cc[:], pb[:, 0:TB2], AF.Identity, [pb.t, convT.t], [cc.t],
                        scale=convT[:, 2, mt:mt + 1], bias=convT[:, 3, mt:mt + 1])
                  P.cp(u[:, 0:2], halo[:, mt, :], [halo_t[mt]], [u.t], eng="pool")
                  P.cp(halo[:, mt, :], u[:, TB2:TB2 + 2], [u.t], [halo_t[mt]], eng="pool")
                  P.stt(cc[:], u[:, 1:TB2 + 1], convT[:, 1, mt:mt + 1], cc[:], ALU.mult, ALU.add, [u.t, convT.t, cc.t], [cc.t])
                  if mt < 21:
                      P.stt(vals[:, mt, :], u[:, 0:TB2], convT[:, 0, mt:mt + 1], cc[:], ALU.mult, ALU.add,
                            [u.t, convT.t, cc.t], [vals_t[mt]])
                  else:
                      sg_ = sgt[mt % 2]
                      P.stt(cc[:], u[:, 0:TB2], convT[:, 0, mt:mt + 1], cc[:], ALU.mult, ALU.add, [u.t, convT.t, cc.t], [cc.t])
                      P.act(sg_[:], cc[:], AF.Silu, [cc.t], [sg_.t])
                      P.tt(actT[:, mt - 21, :], sg_[:], vals[:, mt - 21, :], ALU.mult, [sg_.t, vals_t[mt - 21]], [actT.t])
              for ti in range(2):
                  o_ = ost[ti]
                  for half in range(2):
                      pb = nb()
                      for kc in range(21):
                          P.mm(pb[:, :], actT[:, kc, ti * 128:(ti + 1) * 128], Wdn[:, kc, half * 512:(half + 1) * 512],
                               [actT.t, Wdn.t], [pb.t], start=(kc == 0), stop=(kc == 20), acc=(kc > 0))
                      cs = slice(half * 512, (half + 1) * 512)
                      P.tt(o_[:, cs], pb[:, :], grow[:, D + half * 512:D + (half + 1) * 512], ALU.mult, [pb.t, grow.t], [o_.t])
                      P.tt(o_[:, cs], o_[:, cs], x1t[ti][:, cs], ALU.add, [o_.t, x1t[ti].t], [o_.t])
                  P.dma(out_d[t0 + ti * 128:t0 + (ti + 1) * 128, :], o_[:], [o_.t], [outT], eng="sp", sem_tile=o_.t)
                  finals.append(o_.t)
    P.emit("sp", list({id(t): t for t in P.dma_tiles}.values()))
    top.close()
    return nc


def _col(v):
    v = np.asarray(v, np.float32)
    return np.ascontiguousarray(v.reshape(-1, 128).T)


def prep_inputs(S, x_b, c_b, p, own=0):
    w_in = p["w_in"]
    RW = 512
    o_sb = 1824
    w1 = []
    mu = []
    colp = []
    w2h, a2h, g2h = [], [], []
    for hh in range(2):
        hs = slice(hh * 256, (hh + 1) * 256)
        cols = np.concatenate([np.arange(0, 512)[hs], 512 + np.arange(512)[hs], 1024 + np.arange(512)[hs],
                               np.arange(1536, 1824),
                               o_sb + np.arange(512)[hs], o_sb + 512 + np.arange(512)[hs], o_sb + 1024 + np.arange(512)[hs]])
        w1.append(w_in[:, cols])
        mu.append(p["rwkv_mu"][cols[:NLERP]])
        cp_ = np.zeros((128, 16), np.float32)
        for g in range(2):
            cs = slice(hh * 256 + g * 128, hh * 256 + (g + 1) * 128)
            for j, nm in enumerate(("rwkv_w0", "rwkv_a0", "rwkv_k_k", "rwkv_k_a", "rwkv_r_k", "rwkv_lnx_g", "rwkv_lnx_b")):
                cp_[:, 7 * g + j] = np.asarray(p[nm]).reshape(-1)[cs]
        cp_[:, 14] = np.tile(p["sb_q_g"], 2)
        cp_[:, 15] = np.tile(p["sb_k_g"], 2)
        colp.append(cp_)
        w2h.append(p["rwkv_w2"][:, hs])
        a2h.append(p["rwkv_a2"][:, hs])
        g2h.append(p["rwkv_g2"][:, hs])
    convT = np.stack([_col(p["conv_w"][0]), _col(p["conv_w"][1]), _col(p["conv_w"][2]), _col(p["conv_b"])], axis=1)
    f = lambda a: np.ascontiguousarray(np.asarray(a, np.float32))
    SHh = S // 2
    x_seq = x_b if own == 1 else np.concatenate([x_b[:SHh], x_b[:SHh]], axis=0)
    return {
        "x": f(x_seq), "x_own": f(x_seq[SHh:]), "x_halo": f(x_seq[SHh - 128:SHh]),
        "flags": f(np.tile(np.array([[1.0 - own, float(own)]], np.float32), (128, 1))), "cT": _col(c_b), "w_ada": f(p["w_ada"]), "badaT": _col(p["b_ada"]),
        "bada_g": f(np.stack([p["b_ada"][2 * D:3 * D], p["b_ada"][5 * D:6 * D]])),
        "n1g": _col(p["norm1_g"]), "n2g": _col(p["norm2_g"]),
        "w1": f(np.stack(w1)), "mu": f(np.stack(mu)), "colp": f(np.stack(colp)),
        "w2h": f(np.stack(w2h)), "a2h": f(np.stack(a2h)), "g2h": f(np.stack(g2h)),
        "wg": f(w_in[:, 3360:]), "bgT": _col(p["b_gate"]), "woa": f(p["w_o_rwkv"]), "wob": f(p["w_o_sb"]),
        "wout": f(p["w_out"]), "wup": f(p["w_up"]), "convT": f(convT), "wdn": f(p["w_down"]),
    }


_NC_CACHE = {}


def kernel(**inputs):
    x = np.asarray(inputs["x"], np.float32)
    Bn, S, _ = x.shape
    p = {k: np.asarray(v)[0] for k, v in inputs.items() if k not in ("x", "c")}
    c = np.asarray(inputs["c"], np.float32)
    if S not in _NC_CACHE:
        _NC_CACHE[S] = build(S)
    nc = _NC_CACHE[S]
    ncores = 8
    maps = []
    for core in range(ncores):
        b = (core // 2) % Bn
        maps.append(prep_inputs(S, x[b], c[b], p, core % 2))
    res = run_bass_kernel_spmd(nc, maps, core_ids=list(range(ncores)))
    out = np.stack([np.concatenate([res.results[2 * b]["out"], res.results[2 * b + 1]["out"]], axis=0)
                    for b in range(Bn)], axis=0)
    return out.astype(np.float32)
```

```python
import math
import os
from contextlib import ExitStack
import numpy as np
import concourse.bass as bass
import concourse.mybir as mybir
from concourse.bass_utils import run_bass_kernel_spmd

F32 = mybir.dt.float32
BF16 = mybir.dt.bfloat16
AF = mybir.ActivationFunctionType
ALU = mybir.AluOpType

ENGS = ("pe", "act", "dve", "pool", "sp")
D = 1024
NIN1 = 1824
NLERP = 1056
DFF = 2688
CDEC = math.exp(-0.5)
C = 64
TB = 512


class T:
    __slots__ = ("name", "w", "r", "dsem", "dcnt", "psum")

    def __init__(self, name=""):
        self.name = name
        self.psum = False
        self.w = None
        self.r = []
        self.dsem = None
        self.dcnt = 0


class Op:
    __slots__ = ("eng", "idx", "fn", "deps", "dma", "sem_tile", "dval", "sig", "cnt", "rt")

    def __init__(self, eng, idx, fn):
        self.eng = eng
        self.idx = idx
        self.fn = fn
        self.deps = []
        self.dma = False
        self.sem_tile = None
        self.dval = 0
        self.sig = False
        self.cnt = 0
        self.rt = None


class Prog:
    def __init__(self, nc):
        self.nc = nc
        self.ops = {e: [] for e in ENGS}
        self.dma_tiles = []
        self.last_dma = {}
        self.mute = False

    def op(self, eng, fn, reads=(), writes=(), dma=False, sem_tile=None, acc=False, extra=(), rt=None):
        if self.mute:
            return None
        o = Op(eng, len(self.ops[eng]), fn)
        o.rt = rt
        deps = {}
        force = set()
        if eng == "pe":
            for t in writes:
                if t.w is not None and t.w.eng == "pe" and t.w.rt != rt:
                    deps[id(t.w)] = (t.w, True)
                    force.add(id(t.w))
        for d in extra:
            deps[id(d)] = (d, True)
        for t in reads:
            if t.w is not None:
                deps[id(t.w)] = (t.w, True)
            if t.psum:
                for r in t.r:
                    if r.eng != eng:
                        deps[id(r)] = (r, True)
        for t in writes:
            w = t.w
            if w is not None:
                if id(w) in deps:
                    pass
                elif acc and w.eng == "pe" and eng == "pe":
                    pass
                elif dma and w.dma and w.sem_tile is (sem_tile or t) and w.eng == eng:
                    pass
                elif id(w) not in deps:
                    deps[id(w)] = (w, False)
            for r in t.r:
                if id(r) not in deps:
                    deps[id(r)] = (r, False)
        dl = []
        for d, israw in deps.values():
            if d is o:
                continue
            if d.eng == eng and not d.dma and id(d) not in force:
                if eng == "pe" or not israw:
                    continue
            dl.append(d)
        o.deps = dl
        if dma:
            o.dma = True
            st = sem_tile if sem_tile is not None else writes[0]
            o.sem_tile = st
            st.dcnt += 16
            o.dval = st.dcnt
            if st not in self.dma_tiles:
                self.dma_tiles.append(st)
        for d in dl:
            d.sig = True
        for t in reads:
            t.r.append(o)
        for t in writes:
            t.w = o
            t.r = []
        self.ops[eng].append(o)
        if dma:
            self.last_dma[id(o.sem_tile)] = o
        return o

    def barrier(self):
        lasts = [self.ops[e][-1] for e in ENGS if self.ops[e]]
        lasts = [o for o in lasts if o.fn is not None and not o.dma]
        dmas = list(self.last_dma.values())
        self.last_dma = {}
        for e in ENGS:
            self.op(e, None, extra=[o for o in lasts if o.eng != e] + dmas)

    def act(self, out, in_, func, R, W, **kw):
        return self.op("act", lambda e: e.activation(out=out, in_=in_, func=func, **kw), R, W)

    def mm(self, out, lhsT, rhs, R, W, start=True, stop=True, acc=False):
        rt = lhsT.base_partition() if lhsT.partition_size() <= 64 else None
        return self.op("pe", lambda e: e.matmul(out, lhsT=lhsT, rhs=rhs, start=start, stop=stop,
                                                skip_group_check=True), R, W, acc=acc, rt=rt)

    def tr(self, out, in_, ident, R, W, acc=False):
        return self.op("pe", lambda e: e.transpose(out=out, in_=in_, identity=ident), R, W, acc=acc)

    def tt(self, out, in0, in1, op, R, W, eng="dve"):
        return self.op(eng, lambda e: e.tensor_tensor(out=out, in0=in0, in1=in1, op=op), R, W)

    def ts(self, out, in0, s1, s2, op0, op1, R, W):
        return self.op("dve", lambda e: e.tensor_scalar(out=out, in0=in0, scalar1=s1, scalar2=s2,
                                                         op0=op0, op1=op1), R, W)

    def stt(self, out, in0, scalar, in1, op0, op1, R, W):
        return self.op("dve", lambda e: e.scalar_tensor_tensor(out=out, in0=in0, scalar=scalar, in1=in1,
                                                                op0=op0, op1=op1), R, W)

    def cp(self, out, in_, R, W, eng="dve"):
        return self.op(eng, lambda e: e.tensor_copy(out=out, in_=in_), R, W)

    def ms(self, ap, val, W, eng="pool"):
        return self.op(eng, lambda e: e.memset(ap, val), (), W)

    def dma(self, out, in_, R, W, eng="sp", sem_tile=None):
        return self.op(eng, lambda e: e.dma_start(out=out, in_=in_), R, W, dma=True, sem_tile=sem_tile)

    def emit(self, final_eng, final_tiles, sem_cap=30000):
        nc = self.nc
        if os.environ.get("KVERBOSE"):
            print("DMA semaphores:", len(self.dma_tiles), " ops:", {e: len(v) for e, v in self.ops.items()})
        self.op(final_eng, None, reads=final_tiles)
        with ExitStack() as es:
            esems = {}
            for e in ENGS:
                n = sum(1 for o in self.ops[e] if o.sig and not o.dma)
                k = max(1, (n + sem_cap - 1) // sem_cap)
                esems[e] = [es.enter_context(nc.semaphore(f"s_{e}{i}")) for i in range(k)]
                c = 0
                for o in self.ops[e]:
                    if o.sig and not o.dma:
                        c += 1
                        o.cnt = c
            for i, t in enumerate(self.dma_tiles):
                t.dsem = es.enter_context(nc.semaphore(f"d{i}"))

            def sigof(d):
                if d.dma:
                    return d.sem_tile.dsem, d.dval
                k = (d.cnt - 1) // sem_cap
                return esems[d.eng][k], d.cnt - k * sem_cap

            block = es.enter_context(nc.Block())

            def run(e, engobj):
                seen = {}
                for o in self.ops[e]:
                    for d in o.deps:
                        s, v = sigof(d)
                        if seen.get(id(s), 0) >= v:
                            continue
                        seen[id(s)] = v
                        engobj.wait_ge(s, v)
                    if o.fn is None:
                        continue
                    ins = o.fn(engobj)
                    if o.dma:
                        ins.then_inc(o.sem_tile.dsem, 16)
                    elif o.sig:
                        s, _ = sigof(o)
                        ins.then_inc(s, 1)

            @block.tensor
            def _(eng):
                run("pe", eng)

            @block.scalar
            def _(eng):
                run("act", eng)

            @block.vector
            def _(eng):
                run("dve", eng)

            @block.gpsimd
            def _(eng):
                run("pool", eng)

            @block.sync
            def _(eng):
                run("sp", eng)


class B:
    def __init__(self, h, name):
        self.h = h
        self.t = T(name)

    def __getitem__(self, k):
        return self.h[k]


def build(S, dbg=False, phases=("R", "A", "2a", "2b")):
    nc = bass.Bass("TRN2", target_bir_lowering=False)
    NB = S // TB
    NKB = S // 128
    NCH = TB // C

    def din(name, shape, dt=F32):
        return nc.dram_tensor(name, list(shape), dt, kind="ExternalInput").ap()

    x_d = din("x", [S, D])
    cT_d = din("cT", [128, 8])
    wada_d = din("w_ada", [D, 6 * D])
    badaT_d = din("badaT", [128, 48])
    bada_g_d = din("bada_g", [2, D])
    n1g_d = din("n1g", [128, 8])
    n2g_d = din("n2g", [128, 8])
    w1_d = din("w1", [2, D, NIN1])
    mu_d = din("mu", [2, NLERP])
    colp_d = din("colp", [2, 128, 16])
    w2_d = din("w2h", [2, 64, 256])
    a2_d = din("a2h", [2, 64, 256])
    g2_d = din("g2h", [2, 160, 256])
    wg_d = din("wg", [D, 2 * D])
    bgT_d = din("bgT", [128, 16])
    woa_d = din("woa", [512, D])
    wob_d = din("wob", [512, D])
    wout_d = din("wout", [D, D])
    wup_d = din("wup", [D, 2 * DFF])
    convT_d = din("convT", [128, 4, 42])
    wdn_d = din("wdn", [DFF, D])
    SH = S // 2
    xown_d = din("x_own", [SH, D])
    xhalo_d = din("x_halo", [128, D])
    flags_d = din("flags", [128, 2])
    out_d = nc.dram_tensor("out", [SH, D], F32, kind="ExternalOutput").ap()
    ykind = "ExternalOutput" if dbg else "Internal"
    ysc_d = nc.dram_tensor("ysc", [2, 512, S], BF16, kind=ykind).ap()
    x1_d = nc.dram_tensor("x1sc", [SH, D], F32, kind=ykind).ap()
    x1h_d = nc.dram_tensor("x1halo", [128, D], F32, kind="Internal").ap()

    dbgY_d = nc.dram_tensor("dbgY", [128, TB], F32, kind="ExternalOutput").ap() if dbg else None
    dbgT = T("dbgY")
    P = Prog(nc)
    top = ExitStack()
    RC = int(os.environ.get("RCUT", "99"))
    dumped = set()

    def dump(name, b, ap, shape, dt, cond=True):
        if not dbg or not cond or name in dumped or not os.environ.get("KDUMPS"):
            return
        dumped.add(name)
        d = nc.dram_tensor("dbg_" + name, list(shape), dt, kind="ExternalOutput").ap()
        P.dma(d, ap, [b.t], [T("dd" + name)], eng="sp", sem_tile=T("ds" + name))

    def cut(k):
        if RC == k:
            P.mute = True

    uniq = {"n": 0}

    def sbuf(es, name, shape, dt):
        uniq["n"] += 1
        nm = f"{name}_{uniq['n']}"
        return B(es.enter_context(nc.sbuf_tensor(nm, list(shape), dt)), nm)

    psall = top.enter_context(nc.psum_tensor("psall", [128, 8 * 512], F32))
    banks = [B(psall[:, i * 512:(i + 1) * 512], f"bank{i}") for i in range(8)]
    for b_ in banks:
        b_.t.psum = True
    rot = {"i": 0, "l": banks[3:]}

    def nb():
        b = rot["l"][rot["i"] % len(rot["l"])]
        rot["i"] += 1
        return b

    ident = sbuf(top, "ident", [128, 128], BF16)
    P.ms(ident[:], 1.0, [ident.t])
    P.op("pool", lambda e: e.affine_select(out=ident[:], in_=ident[:], pattern=[[1, 128]],
                                           compare_op=ALU.is_equal, fill=0.0, base=0, channel_multiplier=-1),
         [ident.t], [ident.t])
    blk1 = sbuf(top, "blk1", [128, 128], BF16)
    blk64 = sbuf(top, "blk64", [128, 128], BF16)
    for bt, val in ((blk1, 1.0), (blk64, 1.0 / 64)):
        P.ms(bt[:], 0.0, [bt.t])
        P.ms(bt[0:64, 0:64], val, [bt.t])
        P.ms(bt[64:128, 64:128], val, [bt.t])
    modT = sbuf(top, "modT", [128, 48], F32)
    grow = sbuf(top, "grow", [128, 2 * D], F32)
    G1 = sbuf(top, "G1", [128, 8], F32)
    fl = sbuf(top, "fl", [128, 2], F32)
    P.dma(fl[:], flags_d, [], [fl.t])
    G2 = sbuf(top, "G2", [128, 8], F32)

    with ExitStack() as es:
        cT = sbuf(es, "cTs", [128, 8], F32)
        sc = sbuf(es, "sc", [128, 8], F32)
        screp = sbuf(es, "screp", [128, 8, 128], F32)
        badaT = sbuf(es, "badaTs", [128, 48], F32)
        ng = sbuf(es, "ng", [128, 16], F32)
        wab = [sbuf(es, f"wab{i}", [128, 8, 512], F32) for i in range(2)]
        P.dma(cT[:], cT_d, [], [cT.t])
        P.dma(badaT[:], badaT_d, [], [badaT.t])
        P.dma(ng[:, 0:8], n1g_d, [], [ng.t])
        P.dma(ng[:, 8:16], n2g_d, [], [ng.t])
        P.dma(grow[:, 0:D], bada_g_d[0:1, :].partition_broadcast(128), [], [grow.t])
        P.dma(grow[:, D:2 * D], bada_g_d[1:2, :].partition_broadcast(128), [], [grow.t])
        P.act(sc[:], cT[:], AF.Silu, [cT.t], [sc.t])
        P.cp(screp[:], sc[:].unsqueeze(2).to_broadcast([128, 8, 128]), [sc.t], [screp.t])
        pm = banks[0]
        wv = wada_d.rearrange("(kc p) n -> p kc n", p=128)
        for blk in range(12):
            wb = wab[blk % 2]
            P.dma(wb[:], wv[:, :, blk * 512:(blk + 1) * 512], [], [wb.t])
            for j in range(4):
                oc = blk * 4 + j
                for kc in range(8):
                    P.mm(pm[:, oc:oc + 1], wb[:, kc, j * 128:(j + 1) * 128], sc[:, kc:kc + 1],
                         [wb.t, sc.t], [pm.t], start=(kc == 0), stop=(kc == 7), acc=(oc > 0 or kc > 0))
            gi = {4: 0, 5: 1, 10: 2, 11: 3}.get(blk)
            if gi is not None:
                pr = nb()
                for kc in range(8):
                    P.mm(pr[:, :], screp[:, kc, :], wb[:, kc, :], [wb.t, screp.t], [pr.t],
                         start=(kc == 0), stop=(kc == 7), acc=(kc > 0))
                P.tt(grow[:, gi * 512:(gi + 1) * 512], pr[:, :], grow[:, gi * 512:(gi + 1) * 512], ALU.add,
                     [pr.t, grow.t], [grow.t])
        P.tt(modT[:], pm[:, 0:48], badaT[:], ALU.add, [pm.t, badaT.t], [modT.t])
        P.stt(G1[:], modT[:, 8:16], 1.0, ng[:, 0:8], ALU.add, ALU.mult, [modT.t, ng.t], [G1.t])
        P.stt(G2[:], modT[:, 32:40], 1.0, ng[:, 8:16], ALU.add, ALU.mult, [modT.t, ng.t], [G2.t])
    SH1 = modT[:, 0:8]
    SH2 = modT[:, 24:32]
    P.barrier()

    def norm_T(es_bufs, xt, hT, col0, Gm, SHm):
        st, xn, junk = es_bufs
        P.act(junk[:], xt[:], AF.Square, [xt.t], [junk.t, st.t], accum_out=st[:, 0:1])
        P.act(st[:, 1:2], st[:, 0:1], AF.Ln, [st.t], [st.t], scale=1.0 / D, bias=1e-6)
        P.act(st[:, 2:3], st[:, 1:2], AF.Exp, [st.t], [st.t], scale=-0.5)
        P.act(xn[:], xt[:], AF.Copy, [xt.t, st.t], [xn.t], scale=st[:, 2:3])
        pb = nb()
        pv = pb[:].bitcast(BF16)
        for kc in range(8):
            P.tr(pv[:, kc * 128:(kc + 1) * 128], xn[:, kc * 128:(kc + 1) * 128], ident[:],
                 [xn.t, ident.t], [pb.t], acc=(kc > 0))
        dst = hT[:, :, col0:col0 + 128]
        P.tt(dst, pv.rearrange("p (k t) -> p k t", t=128), Gm.unsqueeze(2).to_broadcast([128, 8, 128]),
             ALU.mult, [pb.t, G1.t, G2.t, modT.t], [hT.t])
        P.tt(dst, dst, SHm.unsqueeze(2).to_broadcast([128, 8, 128]), ALU.add, [hT.t, modT.t], [hT.t])

    yT = T("ysc")

    def load_x_block(blk, xts, nbufs, hT, col_off):
        t0 = blk * TB
        for ti in range(4):
            xt = xts[ti % 2]
            P.dma(xt[:], x_d[t0 + ti * 128:t0 + (ti + 1) * 128, :], [], [xt.t])
            norm_T(nbufs, xt, hT, col_off + ti * 128, G1[:, :], SH1)

    for hh in range(2):
      if "R" in phases:
        with ExitStack() as es:
            Wb = sbuf(es, "Wb", [128, 8, NLERP], BF16)
            Wmu = sbuf(es, "Wmu", [128, 8, NLERP], BF16)
            colp = sbuf(es, "colp", [128, 24], F32)
            w2b = sbuf(es, "w2b", [64, 256], BF16)
            a2b = sbuf(es, "a2b", [64, 256], BF16)
            g2b0 = sbuf(es, "g2b0", [128, 256], BF16)
            g2b1 = sbuf(es, "g2b1", [32, 256], BF16)
            w1v = w1_d[hh].rearrange("(kc p) n -> p kc n", p=128)
            for kc in range(8):
                P.dma(Wb[:, kc, :], w1v[:, kc, 0:NLERP], [], [Wb.t], eng="pool")
            P.dma(colp[:, 0:16], colp_d[hh], [], [colp.t])
            P.dma(w2b[:], w2_d[hh], [], [w2b.t], eng="pool")
            P.dma(a2b[:], a2_d[hh], [], [a2b.t], eng="pool")
            P.dma(g2b0[:], g2_d[hh, 0:128, :], [], [g2b0.t], eng="pool")
            P.dma(g2b1[:], g2_d[hh, 128:160, :], [], [g2b1.t], eng="pool")
            MU = sbuf(es, "MU", [128, NLERP], F32)
            P.dma(MU[:], mu_d[hh:hh + 1, :].partition_broadcast(128), [], [MU.t])
            for kc in range(8):
                P.tt(Wmu[:, kc, :], Wb[:, kc, :], MU[:], ALU.mult, [Wb.t, MU.t], [Wmu.t])
                P.tt(Wb[:, kc, :], Wb[:, kc, :], Wmu[:, kc, :], ALU.subtract, [Wb.t, Wmu.t], [Wb.t])
            for g in range(2):
                P.ts(colp[:, 16 + g:17 + g], colp[:, 7 * g + 3:7 * g + 4], -1.0, 1.0, ALU.mult, ALU.add,
                     [colp.t], [colp.t])

            def cpar(g, j):
                return colp[:, 7 * g + j:7 * g + j + 1]

            msk = {}
            for nm_, cmp_, cm_, st_ in (("SU", ALU.is_gt, -1, 1), ("IU", ALU.is_ge, -1, 1),
                                        ("SL", ALU.is_gt, 1, -1), ("ID", ALU.is_equal, -1, 1)):
                mt_ = sbuf(es, "m" + nm_, [64, NCH, C], F32)
                P.ms(mt_[:], 1.0, [mt_.t])
                P.op("pool", (lambda mt_=mt_, cmp_=cmp_, cm_=cm_, st_=st_:
                              lambda e: e.affine_select(out=mt_[:], in_=mt_[:], pattern=[[0, NCH], [st_, C]],
                                                        compare_op=cmp_, fill=0.0, base=0,
                                                        channel_multiplier=cm_))(),
                     [mt_.t], [mt_.t])
                msk[nm_] = mt_
            mSU, mIU, mSL, idf = msk["SU"], msk["IU"], msk["SL"], msk["ID"]
            smask = sbuf(es, "smask", [128, TB], F32)
            P.ms(smask[:], 1.0, [smask.t])
            P.ms(smask[:].rearrange("p (c t) -> p c t", t=C)[:, :, 0:1], 0.0, [smask.t])

            hTs = [sbuf(es, f"hT{i}", [128, 8, TB + 1], BF16) for i in range(2)]
            hcur = [hTs[0]]
            P.ms(hTs[0][:, :, 0:1], 0.0, [hTs[0].t])
            xts = [sbuf(es, f"xt{i}", [128, D], F32) for i in range(2)]
            nbufs = (sbuf(es, "nst", [128, 4], F32), sbuf(es, "xn", [128, D], BF16), sbuf(es, "junk", [128, D], BF16))
            ST32 = [sbuf(es, f"ST32_{g}", [128, C], F32) for g in range(2)]
            STb = [sbuf(es, f"STb_{g}", [128, C], BF16) for g in range(2)]
            for g in range(2):
                P.ms(ST32[g][:], 0.0, [ST32[g].t])
                P.ms(STb[g][:], 0.0, [STb[g].t])
            tanhwd = sbuf(es, "tanhwd", [64, TB], BF16)
            adsb = sbuf(es, "adsb", [64, TB], BF16)
            sgd0 = sbuf(es, "sgd0", [128, TB], BF16)
            sgd1 = sbuf(es, "sgd1", [32, TB], BF16)

            def f32b(name):
                return sbuf(es, name, [128, TB], F32)

            def b16b(name):
                return sbuf(es, name, [128, TB], BF16)

            r_sb, k_sb, v_sb, sg, a_sb, gg, rn, kk, k2, bb, Lsg, Eneg, Epos, tmpa = [
                f32b(n) for n in ("r_sb", "k_sb", "v_sb", "sg", "a_sb", "gg", "rn", "kk", "k2", "bb", "Lsg",
                                  "Eneg", "Epos", "tmpa")]
            Lx, Eprev, Egc = tmpa, rn, a_sb
            ysb, dd, m2, var, bon = Lsg, tmpa, Eneg, rn, a_sb
            sq, Bt, Kt, BG, KG, vb = [b16b(n) for n in ("sq", "Bt", "Kt", "BG", "KG", "vb")]
            yb, ysqb, rkb = Bt, Kt, BG
            AR = sbuf(es, "AR", [128, NCH, 2, C], BF16)
            BGt, KGt, Vt, Att = [sbuf(es, n, [64, NCH, 128], BF16) for n in ("BGt", "KGt", "Vt", "Att")]
            Pm = [[sbuf(es, f"Pm{h}_{i}", [64, NCH, C], F32) for i in range(2)] for h in range(2)]
            PmT = [[sbuf(es, f"PmT{h}_{i}", [64, NCH, C], F32) for i in range(2)] for h in range(2)]
            Rm = [[sbuf(es, f"Rm{h}_{i}", [64, NCH, C], F32) for i in range(2)] for h in range(2)]
            rot["l"] = [banks[0]] + banks[3:]
            TTb = [sbuf(es, f"TTb{h}", [64, NCH, C], BF16) for h in range(2)]
            MakT = [sbuf(es, f"MakT{h}", [64, NCH, C], BF16) for h in range(2)]
            MrbT = [sbuf(es, f"MrbT{h}", [64, NCH, C], BF16) for h in range(2)]
            MrkT = [sbuf(es, f"MrkT{h}", [64, NCH, C], BF16) for h in range(2)]
            Xs = sbuf(es, "Xs", [64, NCH, 128], BF16)
            Ut = sbuf(es, "Ut", [64, NCH, 128], F32)
            WtT = sbuf(es, "WtT", [128, NCH, C], BF16)
            UtN = sbuf(es, "UtN", [64, NCH, 128], BF16)
            Wtm = sbuf(es, "Wtm", [64, NCH, 128], BF16)
            NZT = sbuf(es, "NZT", [128, NCH, C], BF16)
            Uall = sbuf(es, "Uall", [64, NCH, 128], BF16)
            Sall = [sbuf(es, f"Sall{g}", [128, NCH + 1, C], BF16) for g in range(2)]
            Sall_t = [[T(f"S{g}_{c}") for c in range(NCH + 1)] for g in range(2)]
            for g in range(2):
                P.ms(Sall[g][:, 0, :], 0.0, [Sall_t[g][0]])
            yout = [b16b(f"yout{i}") for i in range(2)]

            def proj(c0, M):
                pb = nb()
                n = 0
                for kc in range(8):
                    hT = hcur[0]
                    P.mm(pb[0:M, :], Wb[:, kc, c0:c0 + M], hT[:, kc, 1:TB + 1], [Wb.t, hT.t], [pb.t],
                         start=(n == 0), stop=False, acc=(n > 0))
                    n += 1
                    P.mm(pb[0:M, :], Wmu[:, kc, c0:c0 + M], hT[:, kc, 0:TB], [Wmu.t, hT.t], [pb.t],
                         start=False, stop=(n == 15), acc=True)
                    n += 1
                return pb

            v3 = lambda bf: bf[:].rearrange("p (c t) -> p c t", t=C)
            pv3 = lambda pb: pb[0:64, :].rearrange("p (c t) -> p c t", t=C)

            OWN0 = SH // TB
            HB = OWN0 - 1
            for blk in range(NB):
                t0 = blk * TB
                full = blk >= HB
                hcur[0] = hTs[blk % 2]

                def prefetch_next(blk=blk):
                    nb_ = blk + 1
                    if nb_ >= NB:
                        return
                    hc, hn = hTs[blk % 2], hTs[nb_ % 2]
                    if nb_ == OWN0:
                        P.ts(hn[:, :, 0:1], hc[:, :, TB:TB + 1], fl[:, 1:2], None, ALU.mult, ALU.bypass, [hc.t, fl.t], [hn.t])
                    else:
                        P.cp(hn[:, :, 0:1], hc[:, :, TB:TB + 1], [hc.t], [hn.t])
                    load_x_block(nb_, xts, nbufs, hn, 1)
                if blk == 0:
                    load_x_block(0, xts, nbufs, hTs[0], 1)
                pb = proj(768, 64)
                P.act(tanhwd[:], pb[0:64, :], AF.Tanh, [pb.t], [tanhwd.t])
                pb = proj(832, 64)
                P.act(adsb[:], pb[0:64, :], AF.Copy, [pb.t], [adsb.t])
                if full:
                    pb = proj(896, 128)
                    P.act(sgd0[:], pb[:, :], AF.Sigmoid, [pb.t], [sgd0.t])
                    pb = proj(1024, 32)
                    P.act(sgd1[:], pb[0:32, :], AF.Sigmoid, [pb.t], [sgd1.t])
                cut(1)
                for g in range(2):
                    ch = slice(g * 128, (g + 1) * 128)
                    if full:
                        pb = proj(0 + g * 128, 128)
                        P.act(r_sb[:], pb[:, :], AF.Copy, [pb.t], [r_sb.t])
                    pb = proj(256 + g * 128, 128)
                    P.act(k_sb[:], pb[:, :], AF.Copy, [pb.t], [k_sb.t])
                    pb = proj(512 + g * 128, 128)
                    P.act(v_sb[:], pb[:, :], AF.Copy, [pb.t], [v_sb.t])
                    P.cp(vb[:], v_sb[:], [v_sb.t], [vb.t])
                    pb = nb()
                    P.mm(pb[:, :], w2b[:, ch], tanhwd[:], [w2b.t, tanhwd.t], [pb.t])
                    P.act(sg[:], pb[:, :], AF.Sigmoid, [pb.t, colp.t], [sg.t], bias=cpar(g, 0))
                    pb = nb()
                    P.mm(pb[:, :], a2b[:, ch], adsb[:], [a2b.t, adsb.t], [pb.t])
                    P.act(a_sb[:], pb[:, :], AF.Sigmoid, [pb.t, colp.t], [a_sb.t], bias=cpar(g, 1))
                    if full:
                        pb = nb()
                        P.mm(pb[:, :], g2b0[:, ch], sgd0[:], [g2b0.t, sgd0.t], [pb.t], start=True, stop=False)
                        P.mm(pb[:, :], g2b1[:, ch], sgd1[:], [g2b1.t, sgd1.t], [pb.t], start=False, stop=True, acc=True)
                        P.act(gg[:], pb[:, :], AF.Copy, [pb.t], [gg.t])
                    P.act(sq[:], k_sb[:], AF.Square, [k_sb.t, colp.t], [sq.t], scale=cpar(g, 2))
                    pb = nb()
                    P.mm(pb[:, :], blk1[:], sq[:], [blk1.t, sq.t], [pb.t])
                    P.act(rn[:], pb[:, :], AF.Ln, [pb.t], [rn.t], bias=1e-24)
                    P.act(rn[:], rn[:], AF.Exp, [rn.t], [rn.t], scale=-0.5)
                    P.stt(kk[:], k_sb[:], cpar(g, 2), rn[:], ALU.mult, ALU.mult, [k_sb.t, colp.t, rn.t], [kk.t])
                    P.ts(tmpa[:], a_sb[:], cpar(g, 3), colp[:, 16 + g:17 + g], ALU.mult, ALU.add, [a_sb.t, colp.t], [tmpa.t])
                    P.tt(k2[:], k_sb[:], tmpa[:], ALU.mult, [k_sb.t, tmpa.t], [k2.t])
                    P.tt(bb[:], kk[:], a_sb[:], ALU.mult, [kk.t, a_sb.t], [bb.t])
                    P.op("dve", lambda e: e.tensor_tensor_scan(out=Lsg[:], data0=smask[:], data1=sg[:], initial=0.0,
                                                               op0=ALU.mult, op1=ALU.add), [smask.t, sg.t], [Lsg.t])
                    P.tt(Lx[:], Lsg[:], sg[:], ALU.subtract, [Lsg.t, sg.t], [Lx.t])
                    P.act(Eneg[:], Lsg[:], AF.Exp, [Lsg.t], [Eneg.t], scale=CDEC)
                    P.act(Epos[:], Lsg[:], AF.Exp, [Lsg.t], [Epos.t], scale=-CDEC)
                    P.act(Eprev[:], Lx[:], AF.Exp, [Lx.t], [Eprev.t], scale=-CDEC)
                    gamC = v3(Epos)[:, :, C - 1:C]
                    P.tt(v3(Egc), v3(Eneg), gamC.to_broadcast([128, NCH, C]), ALU.mult, [Eneg.t, Epos.t], [Egc.t])
                    P.tt(AR[:, :, 0, :], v3(kk), v3(Eprev), ALU.mult, [kk.t, Eprev.t], [AR.t])
                    if full:
                        P.tt(AR[:, :, 1, :], v3(r_sb), v3(Epos), ALU.mult, [r_sb.t, Epos.t], [AR.t])
                    P.tt(Bt[:], bb[:], Eneg[:], ALU.mult, [bb.t, Eneg.t], [Bt.t])
                    P.tt(Kt[:], k2[:], Eneg[:], ALU.mult, [k2.t, Eneg.t], [Kt.t])
                    P.tt(BG[:], bb[:], Egc[:], ALU.mult, [bb.t, Egc.t], [BG.t])
                    P.tt(KG[:], k2[:], Egc[:], ALU.mult, [k2.t, Egc.t], [KG.t])
                    D0 = (hh == 0 and blk == 0 and g == 0)
                    for nm_, b_, dt_ in (("r", r_sb, F32), ("k", k_sb, F32), ("v", v_sb, F32), ("sg", sg, F32),
                                         ("kk", kk, F32), ("k2", k2, F32), ("bb", bb, F32), ("Lsg", Lsg, F32),
                                         ("Eneg", Eneg, F32), ("Epos", Epos, F32), ("Eprev", Eprev, F32), ("Egc", Egc, F32),
                                         ("gg", gg, F32), ("Bt", Bt, BF16), ("Kt", Kt, BF16), ("BG", BG, BF16), ("KG", KG, BF16)):
                        dump(nm_, b_, b_[:], [128, TB], dt_, D0)
                    dump("AR", AR, AR[:].rearrange("p c s t -> p (c s t)"), [128, NCH * 2 * C], BF16, D0)
                    cut(2)
                    for src_ap, srct, dst in ((lambda c: BG[:, c * C:(c + 1) * C], BG.t, BGt),
                                              (lambda c: KG[:, c * C:(c + 1) * C], KG.t, KGt),
                                              (lambda c: vb[:, c * C:(c + 1) * C], vb.t, Vt),
                                              (lambda c: AR[:, c, 0, :], AR.t, Att)):
                        pb = nb()
                        pv = pb[:].bitcast(BF16)
                        for c in range(NCH):
                            P.tr(pv[0:64, c * 128:(c + 1) * 128], src_ap(c), ident[:], [srct, ident.t], [pb.t], acc=(c > 0))
                        P.cp(dst[:].rearrange("p c k -> p (c k)"), pv[0:64, 0:NCH * 128], [pb.t], [dst.t])
                    for nm_, b_ in (("BGt", BGt), ("KGt", KGt), ("Vt", Vt), ("Att", Att)):
                        dump(nm_, b_, b_[:].rearrange("p c k -> p (c k)"), [64, NCH * 128], BF16, D0)
                    cut(3)
                    if g == 0:
                        prefetch_next()
                    HP = [slice(0, 64), slice(64, 128)]

                    def mat(lhs_fn, rhs_fn, Rr):
                        pb = nb()
                        for c in range(NCH):
                            P.mm(pb[0:64, c * C:(c + 1) * C], lhs_fn(c), rhs_fn(c), Rr, [pb.t],
                                 start=(c == 0), stop=True, acc=(c > 0))
                        return pb
                    Btc = lambda h: (lambda c: Bt[HP[h], c * C:(c + 1) * C])
                    Ktc = lambda h: (lambda c: Kt[HP[h], c * C:(c + 1) * C])
                    Atc = lambda h: (lambda c: AR[HP[h], c, 0, :])
                    Rtc = lambda h: (lambda c: AR[HP[h], c, 1, :])
                    f3 = lambda b_: b_[:].rearrange("p c t -> p (c t)")
                    for h in range(2):
                        pb = mat(Btc(h), Atc(h), [Bt.t, AR.t])
                        P.stt(Pm[h][0][:], pv3(pb), -1.0, mSU[:], ALU.mult, ALU.mult, [pb.t, mSU.t], [Pm[h][0].t])
                        pb = mat(Atc(h), Btc(h), [Bt.t, AR.t])
                        P.stt(PmT[h][0][:], pv3(pb), -1.0, mSL[:], ALU.mult, ALU.mult, [pb.t, mSL.t], [PmT[h][0].t])
                    for h in range(2):
                        P.tt(Rm[h][0][:], Pm[h][0][:], idf[:], ALU.add, [Pm[h][0].t, idf.t], [Rm[h][0].t])
                    for h in range(2):
                        pb = mat(Ktc(h), Atc(h), [Kt.t, AR.t])
                        P.tt(MakT[h][:], pv3(pb), mSU[:], ALU.mult, [pb.t, mSU.t], [MakT[h].t])
                    cur = 0
                    rcur = 0
                    for lvl in range(1, 6):
                        pbs = []
                        for h in range(2):
                            Pc, PcT = Pm[h][cur], PmT[h][cur]
                            pb = mat(lambda c, PcT=PcT: PcT[:, c, :], lambda c, Pc=Pc: Pc[:, c, :], [Pc.t, PcT.t])
                            pb2 = mat(lambda c, Pc=Pc: Pc[:, c, :], lambda c, PcT=PcT: PcT[:, c, :], [Pc.t, PcT.t])
                            pbs.append((pb, pb2))
                        for h in range(2):
                            Pn, PnT = Pm[h][1 - cur], PmT[h][1 - cur]
                            P.cp(Pn[:], pv3(pbs[h][0]), [pbs[h][0].t], [Pn.t])
                            P.act(PnT[:], pv3(pbs[h][1]), AF.Copy, [pbs[h][1].t], [PnT.t])
                        pb3s = []
                        for h in range(2):
                            PnT = PmT[h][1 - cur]
                            Rc = Rm[h][rcur]
                            pb3s.append(mat(lambda c, PnT=PnT: PnT[:, c, :], lambda c, Rc=Rc: Rc[:, c, :], [PnT.t, Rc.t]))
                        for h in range(2):
                            Rc, Rn = Rm[h][rcur], Rm[h][1 - rcur]
                            P.tt(Rn[:], pv3(pb3s[h]), Rc[:], ALU.add, [pb3s[h].t, Rc.t], [Rn.t])
                        if lvl == 1 and full:
                            for h in range(2):
                                pb = mat(Btc(h), Rtc(h), [Bt.t, AR.t])
                                P.tt(MrbT[h][:], pv3(pb), mIU[:], ALU.mult, [pb.t, mIU.t], [MrbT[h].t])
                        if lvl == 2 and full:
                            for h in range(2):
                                pb = mat(Ktc(h), Rtc(h), [Kt.t, AR.t])
                                P.tt(MrkT[h][:], pv3(pb), mIU[:], ALU.mult, [pb.t, mIU.t], [MrkT[h].t])
                        if lvl == 3:
                            for h in range(2):
                                pb = mat(lambda c, h=h: MakT[h][:, c, :], lambda c, h=h: Vt[:, c, HP[h]], [MakT[h].t, Vt.t])
                                P.cp(Xs[:, :, HP[h]], pv3(pb), [pb.t], [Xs.t])
                        cur = 1 - cur
                        rcur = 1 - rcur
                    for h in range(2):
                        P.cp(TTb[h][:], Rm[h][rcur][:], [Rm[h][rcur].t], [TTb[h].t])
                    for h in range(2):
                        pb = mat(lambda c, h=h: TTb[h][:, c, :], lambda c, h=h: Xs[:, c, HP[h]], [TTb[h].t, Xs.t])
                        P.cp(Ut[:, :, HP[h]], pv3(pb), [pb.t], [Ut.t])
                        P.act(UtN[:, :, HP[h]], pv3(pb), AF.Copy, [pb.t], [UtN.t], scale=-1.0)
                        pb = mat(lambda c, h=h: TTb[h][:, c, :], lambda c, h=h: Att[:, c, HP[h]], [TTb[h].t, Att.t])
                        P.cp(Wtm[:, :, HP[h]], pv3(pb), [pb.t], [Wtm.t])
                    for h in range(2):
                        pb = nb()
                        for c in range(NCH):
                            P.mm(pb[HP[h], c * C:(c + 1) * C], Wtm[:, c, HP[h]], BGt[:, c, HP[h]], [Wtm.t, BGt.t], [pb.t],
                                 start=(c == 0), stop=True, acc=(c > 0))
                        P.act(NZT[HP[h], :, :], pb[HP[h], :].rearrange("p (c t) -> p c t", t=C), AF.Copy, [pb.t], [NZT.t], scale=-1.0)
                        if full:
                            pb = nb()
                            for c in range(NCH):
                                P.mm(pb[HP[h], c * C:(c + 1) * C], Att[:, c, HP[h]], TTb[h][:, c, :], [Att.t, TTb[h].t], [pb.t],
                                     start=(c == 0), stop=True, acc=(c > 0))
                            P.cp(WtT[HP[h], :, :], pb[HP[h], :].rearrange("p (c t) -> p c t", t=C), [pb.t], [WtT.t])
                    Sg, Sg_t = Sall[g], Sall_t[g]
                    for c in range(NCH):
                        psn = nb()
                        for h in range(2):
                            hp = HP[h]
                            P.mm(psn[hp, 0:C], KGt[:, c, hp], Vt[:, c, hp], [KGt.t, Vt.t], [psn.t], start=True, stop=False, acc=(h > 0))
                            P.mm(psn[hp, 0:C], BGt[:, c, hp], UtN[:, c, hp], [BGt.t, UtN.t], [psn.t], start=False, stop=False, acc=True)
                        for h in range(2):
                            hp = HP[h]
                            P.mm(psn[hp, 0:C], NZT[hp, c, :], Sg[hp, c, :], [NZT.t, Sg_t[c]], [psn.t], start=False, stop=True, acc=True)
                        gcol = v3(Epos)[:, c, C - 1:C]
                        P.stt(Sg[:, c + 1, :], ST32[g][:], gcol, psn[:, 0:C], ALU.mult, ALU.add, [ST32[g].t, Epos.t, psn.t], [Sg_t[c + 1]])
                        P.stt(ST32[g][:], ST32[g][:], gcol, psn[:, 0:C], ALU.mult, ALU.add, [ST32[g].t, Epos.t, psn.t], [ST32[g].t])
                    py = banks[1 + g]
                    if full:
                        for hf in range(2):
                            pbU = nb()
                            first_ = True
                            for h in range(2):
                                for cc in range(4):
                                    c = hf * 4 + cc
                                    P.mm(pbU[0:64, cc * 128 + h * 64:cc * 128 + (h + 1) * 64], WtT[HP[h], c, :], Sg[HP[h], c, :],
                                         [WtT.t, Sg_t[c]], [pbU.t], start=first_, stop=True, acc=(not first_))
                                    first_ = False
                            P.stt(Uall[:, hf * 4:(hf + 1) * 4, :], pbU[0:64, :].rearrange("p (c k) -> p c k", k=128), -1.0,
                                  Ut[:, hf * 4:(hf + 1) * 4, :], ALU.mult, ALU.subtract, [pbU.t, Ut.t], [Uall.t])
                        for h in range(2):
                            hp = HP[h]
                            for c in range(NCH):
                                P.mm(py[hp, c * C:(c + 1) * C], Sg[hp, c, :], AR[hp, c, 1, :], [Sg_t[c], AR.t], [py.t],
                                     start=(c == 0), stop=False, acc=(c > 0 or h > 0))
                            for c in range(NCH):
                                oy = py[hp, c * C:(c + 1) * C]
                                P.mm(oy, Vt[:, c, hp], MrkT[h][:, c, :], [Vt.t, MrkT[h].t], [py.t], start=False, stop=False, acc=True)
                                P.mm(oy, Uall[:, c, hp], MrbT[h][:, c, :], [Uall.t, MrbT[h].t], [py.t], start=False, stop=True, acc=True)
                    if blk == OWN0 - 1:
                        P.ts(Sg[:, NCH, :], Sg[:, NCH, :], fl[:, 1:2], None, ALU.mult, ALU.bypass, [Sg_t[NCH], fl.t], [Sg_t[NCH]])
                        P.ts(ST32[g][:], ST32[g][:], fl[:, 1:2], None, ALU.mult, ALU.bypass, [ST32[g].t, fl.t], [ST32[g].t])
                    P.cp(Sg[:, 0, :], Sg[:, NCH, :], [Sg_t[NCH]], [Sg_t[0]])
                    P.mute = not full
                    cut(7)
                    P.act(ysb[:], py[:, :], AF.Copy, [py.t], [ysb.t])
                    if dbg and os.environ.get("KDUMPS") and hh == 0 and blk == 0 and g == 0:
                        P.dma(dbgY_d, ysb[:], [ysb.t], [dbgT], eng="sp", sem_tile=ysb.t)
                    P.cp(yb[:], ysb[:], [ysb.t], [yb.t])
                    P.act(ysqb[:], ysb[:], AF.Square, [ysb.t], [ysqb.t])
                    cut(71)
                    pmean = nb()
                    P.mm(pmean[:, :], blk64[:], yb[:], [blk64.t, yb.t], [pmean.t])
                    pmsq = nb()
                    P.mm(pmsq[:, :], blk64[:], ysqb[:], [blk64.t, ysqb.t], [pmsq.t])
                    cut(72)
                    P.stt(dd[:], pmean[:, :], -1.0, ysb[:], ALU.mult, ALU.add, [ysb.t, pmean.t], [dd.t])
                    P.act(m2[:], pmean[:, :], AF.Square, [pmean.t], [m2.t])
                    P.tt(var[:], pmsq[:, :], m2[:], ALU.subtract, [pmsq.t, m2.t], [var.t])
                    cut(73)
                    P.act(var[:], var[:], AF.Ln, [var.t], [var.t], bias=64e-5)
                    P.act(var[:], var[:], AF.Exp, [var.t], [var.t], scale=-0.5)
                    cut(74)
                    P.tt(dd[:], dd[:], var[:], ALU.mult, [dd.t, var.t], [dd.t])
                    P.ts(dd[:], dd[:], cpar(g, 5), cpar(g, 6), ALU.mult, ALU.add, [dd.t, colp.t], [dd.t])
                    cut(8)
                    P.stt(rkb[:], r_sb[:], cpar(g, 4), k2[:], ALU.mult, ALU.mult, [r_sb.t, colp.t, k2.t], [rkb.t])
                    pbon = nb()
                    P.mm(pbon[:, :], blk1[:], rkb[:], [blk1.t, rkb.t], [pbon.t])
                    P.tt(bon[:], pbon[:, :], v_sb[:], ALU.mult, [pbon.t, v_sb.t], [bon.t])
                    P.tt(dd[:], dd[:], bon[:], ALU.add, [dd.t, bon.t], [dd.t])
                    yo_ = yout[g]
                    P.tt(yo_[:], dd[:], gg[:], ALU.mult, [dd.t, gg.t], [yo_.t])
                    cut(9)
                    P.dma(ysc_d[0, hh * 256 + g * 128:hh * 256 + (g + 1) * 128, t0:t0 + TB], yo_[:], [yo_.t], [yT],
                          eng="sp", sem_tile=yo_.t)
                    P.mute = False

        P.mute = False
        rot["l"] = banks[3:]
        P.barrier()
      if "A" in phases:
        with ExitStack() as es:
            NSB = 768
            Wb = sbuf(es, "WbA", [128, 8, NSB], BF16)
            colp = sbuf(es, "colpA", [128, 24], F32)
            w1v = w1_d[hh].rearrange("(kc p) n -> p kc n", p=128)
            for kc in range(8):
                P.dma(Wb[:, kc, :], w1v[:, kc, NLERP:NLERP + NSB], [], [Wb.t], eng="pool")
            P.dma(colp[:, 0:16], colp_d[hh], [], [colp.t])
            P.ts(colp[:, 18:19], colp[:, 14:15], 0.125, None, ALU.mult, ALU.bypass, [colp.t], [colp.t])
            Tm = sbuf(es, "Tm", [128, 128], BF16)
            P.ms(Tm[:], -1.0, [Tm.t])
            P.op("pool", lambda e: e.affine_select(out=Tm[:], in_=Tm[:], pattern=[[-1, 128]], compare_op=ALU.is_ge,
                                                   fill=0.0, base=0, channel_multiplier=1), [Tm.t], [Tm.t])
            NO = sbuf(es, "NO", [128, 128], BF16)
            P.ms(NO[:], -1.0, [NO.t])
            cmask = sbuf(es, "cmask", [128, 128], BF16)
            P.ms(cmask[:], 1.0, [cmask.t])
            P.op("pool", lambda e: e.affine_select(out=cmask[:], in_=cmask[:], pattern=[[1, 128]], compare_op=ALU.is_gt,
                                                   fill=0.0, base=0, channel_multiplier=-1), [cmask.t], [cmask.t])
            KT = [sbuf(es, f"KT{g}", [128, S], BF16) for g in range(2)]
            Vres = sbuf(es, "Vres", [128, NKB, 256], BF16)
            KT_t = [[T(f"kt{g}_{b}") for b in range(NB)] for g in range(2)]
            V_t = [T(f"v{b}") for b in range(NB)]
            hTsA = [sbuf(es, f"hTA{i}", [128, 8, TB], BF16) for i in range(2)]
            hcurA = [hTsA[0]]
            xts = [sbuf(es, f"xtA{i}", [128, D], F32) for i in range(2)]
            nbufs = (sbuf(es, "nstA", [128, 4], F32), sbuf(es, "xnA", [128, D], BF16), sbuf(es, "junkA", [128, D], BF16))
            rn = sbuf(es, "rnA", [128, TB], F32)
            sq = sbuf(es, "sqA", [128, TB], BF16)
            vb = sbuf(es, "vbA", [128, TB], BF16)
            QT = [sbuf(es, f"QT{g}", [128, TB], BF16) for g in range(2)]
            E2 = [sbuf(es, f"E2_{i}", [128, 2, TB], F32) for i in range(2)]
            L2 = [sbuf(es, f"L2_{i}", [128, 2, TB], BF16) for i in range(2)]
            Ls2 = sbuf(es, "Ls2", [128, 2, TB], BF16)
            A2 = [sbuf(es, f"A2_{i}", [128, 2, TB], BF16) for i in range(2)]
            qfA = sbuf(es, "qfA", [128, TB], F32)
            NSET = 3
            p1S = [[banks[1 + 2 * k + h] for h in range(2)] for k in range(NSET)]
            p1P = [psall[:, (1 + 2 * k) * 512:(3 + 2 * k) * 512].rearrange("p (s q) -> p s q", s=2) for k in range(NSET)]
            rot["l"] = [banks[7]]
            yBo = [sbuf(es, f"yBo{i}", [128, TB], BF16) for i in range(2)]

            def projA(c0):
                pb = nb()
                for kc in range(8):
                    hT = hcurA[0]
                    P.mm(pb[:, :], Wb[:, kc, c0:c0 + 128], hT[:, kc, :], [Wb.t, hT.t], [pb.t],
                         start=(kc == 0), stop=(kc == 7), acc=(kc > 0))
                return pb

            for blk in range(NB):
                t0 = blk * TB
                full = blk >= SH // TB - 1
                hcurA[0] = hTsA[blk % 2]
                if blk == 0:
                    load_x_block(0, xts, nbufs, hTsA[0], 0)
                for g in range(2):
                    for which, c0 in (("q", 0), ("k", 256)):
                        if which == "q" and not full:
                            continue
                        pb = projA(c0 + g * 128)
                        P.act(qfA[:], pb[:, :], AF.Copy, [pb.t], [qfA.t])
                        P.act(sq[:], qfA[:], AF.Square, [qfA.t], [sq.t])
                        pn = nb()
                        P.mm(pn[:, :], blk64[:], sq[:], [blk64.t, sq.t], [pn.t])
                        P.act(rn[:], pn[:, :], AF.Ln, [pn.t], [rn.t], bias=1e-6)
                        P.act(rn[:], rn[:], AF.Exp, [rn.t], [rn.t], scale=-0.5)
                        if which == "q":
                            P.stt(QT[g][:], qfA[:], colp[:, 18:19], rn[:], ALU.mult, ALU.mult, [qfA.t, colp.t, rn.t], [QT[g].t])
                        else:
                            P.stt(KT[g][:, t0:t0 + TB], qfA[:], colp[:, 15:16], rn[:], ALU.mult, ALU.mult,
                                  [qfA.t, colp.t, rn.t], [KT_t[g][blk]])
                    pb = projA(512 + g * 128)
                    P.act(vb[:], pb[:, :], AF.Copy, [pb.t], [vb.t])
                    pb = nb()
                    pv = pb[:].bitcast(BF16)
                    for j in range(4):
                        P.tr(pv[:, j * 128:(j + 1) * 128], vb[:, j * 128:(j + 1) * 128], ident[:], [vb.t, ident.t], [pb.t], acc=(j > 0))
                    if blk < SH // TB:
                        P.ts(Vres[:, blk * 4:blk * 4 + 4, g * 128:(g + 1) * 128], pv[:, 0:512].rearrange("p (j c) -> p j c", c=128),
                             fl[:, 1:2], None, ALU.mult, ALU.bypass, [pb.t, fl.t], [V_t[blk]])
                    else:
                        P.cp(Vres[:, blk * 4:blk * 4 + 4, g * 128:(g + 1) * 128], pv[:, 0:512].rearrange("p (j c) -> p j c", c=128),
                             [pb.t], [V_t[blk]])
                if blk + 1 < NB:
                    load_x_block(blk + 1, xts, nbufs, hTsA[(blk + 1) % 2], 0)
                P.mute = not full
                nkb_ = (blk + 1) * 4
                kbs = list(range(nkb_ - 1, -1, -1))
                nst_ = len(kbs)
                cm2 = cmask[:].unsqueeze(1).to_broadcast([128, 2, 128])
                po = banks[0]

                def geom(i):
                    kb = kbs[i]
                    o = kb - blk * 4
                    q0 = max(o, 0) * 128
                    return kb, o, q0, slice(q0, TB), kb // 4

                for g in range(2):
                    P.ms(Ls2[:], 0.0, [Ls2.t], eng="dve")

                    def qk(i, g=g):
                        kb, o, q0, qs, kblk = geom(i)
                        for h in range(2):
                            hp = slice(h * 64, (h + 1) * 64)
                            p1 = p1S[i % NSET][h]
                            P.mm(p1[:, qs], KT[g][hp, kb * 128:(kb + 1) * 128], QT[g][hp, qs], [KT_t[g][kblk], QT[g].t], [p1.t])

                    def front(i):
                        kb, o, q0, qs, kblk = geom(i)
                        k_ = i % NSET
                        pt = [p1S[k_][0].t, p1S[k_][1].t]
                        E_, L_ = E2[i % 2], L2[i % 2]
                        P.act(E_[:, :, qs], p1P[k_][:, :, qs], AF.Exp, pt, [E_.t])
                        P.act(L_[:, :, qs], E_[:, :, qs], AF.Ln, [E_.t], [L_.t], bias=1.0)
                        if o >= 0:
                            P.tt(L_[:, :, q0:q0 + 128], L_[:, :, q0:q0 + 128], cm2, ALU.mult, [L_.t, cmask.t], [L_.t])

                    qk(0)
                    if nst_ > 1:
                        qk(1)
                    front(0)
                    for i in range(nst_):
                        kb, o, q0, qs, kblk = geom(i)
                        k_ = i % NSET
                        L_, A_ = L2[i % 2], A2[i % 2]
                        for h in range(2):
                            p1 = p1S[k_][h]
                            P.mm(p1[:, qs], Tm[:], L_[:, h, qs], [Tm.t, L_.t], [p1.t], start=False, stop=False, acc=True)
                            P.mm(p1[:, qs], NO[:], Ls2[:, h, qs], [NO.t, Ls2.t], [p1.t], start=False, stop=True, acc=True)
                        if kb > 0:
                            P.tt(Ls2[:, :, qs], Ls2[:, :, qs], L_[:, :, qs], ALU.add, [Ls2.t, L_.t], [Ls2.t])
                        if i + 2 < nst_:
                            qk(i + 2)
                        if i + 1 < nst_:
                            front(i + 1)
                        pt = [p1S[k_][0].t, p1S[k_][1].t]
                        P.act(A_[:, :, qs], p1P[k_][:, :, qs], AF.Exp, pt, [A_.t])
                        if o >= 0:
                            P.tt(A_[:, :, q0:q0 + 128], A_[:, :, q0:q0 + 128], cm2, ALU.mult, [A_.t, cmask.t], [A_.t])
                        for h in range(2):
                            hp = slice(h * 64, (h + 1) * 64)
                            P.mm(po[hp, qs], Vres[:, kb, (2 * g + h) * 64:(2 * g + h + 1) * 64], A_[:, h, qs], [V_t[kblk], A_.t], [po.t],
                                 start=(i == 0), stop=(kb == 0), acc=True)
                    yb_ = yBo[g]
                    P.act(yb_[:], po[:, :], AF.Copy, [po.t], [yb_.t])
                    P.dma(ysc_d[1, hh * 256 + g * 128:hh * 256 + (g + 1) * 128, t0:t0 + TB], yb_[:], [yb_.t], [yT],
                          eng="sp", sem_tile=yb_.t)
                P.mute = False

            rot["l"] = banks[3:]
        P.barrier()
    x1T = T("x1sc")
    with ExitStack() as es:
      if "2a" in phases:
          Wg = sbuf(es, "Wg", [128, 8, 2 * D], BF16)
          Woa = sbuf(es, "Woa", [128, 4, D], BF16)
          Wob = sbuf(es, "Wob", [128, 4, D], BF16)
          Wo = sbuf(es, "Wo", [128, 8, D], BF16)
          bgT = sbuf(es, "bgT", [128, 16], F32)
          for kc in range(8):
              P.dma(Wg[:, kc, :], wg_d.rearrange("(kc p) n -> p kc n", p=128)[:, kc, :], [], [Wg.t], eng="pool")
          P.dma(Woa[:], woa_d.rearrange("(kc p) n -> p kc n", p=128), [], [Woa.t], eng="pool")
          P.dma(Wob[:], wob_d.rearrange("(kc p) n -> p kc n", p=128), [], [Wob.t], eng="pool")
          for kc in range(8):
              P.dma(Wo[:, kc, :], wout_d.rearrange("(kc p) n -> p kc n", p=128)[:, kc, :], [], [Wo.t], eng="pool")
          P.dma(bgT[:], bgT_d, [], [bgT.t])
          hT2s = [sbuf(es, f"hT2_{i}", [128, 8, TB], BF16) for i in range(2)]
          xts2 = [[sbuf(es, f"xq{k}_{i}", [128, D], F32) for i in range(4)] for k in range(2)]
          nst = sbuf(es, "nst2", [128, 4], F32)
          xn = sbuf(es, "xn2", [128, D], BF16)
          junk = sbuf(es, "junk2", [128, D], BF16)
          gT = sbuf(es, "gT", [128, 16, TB], BF16)
          yA = sbuf(es, "yA", [128, 4, TB], BF16)
          yBb = sbuf(es, "yBb", [128, 4, TB], BF16)
          yA2 = sbuf(es, "yA2", [128, 4, TB], BF16)
          yB2 = sbuf(es, "yB2", [128, 4, TB], BF16)
          mT = sbuf(es, "mT", [128, 8, TB], BF16)
          tA = sbuf(es, "tA", [128, TB], F32)
          tB = sbuf(es, "tB", [128, TB], F32)
          x1s = [sbuf(es, f"x1s{i}", [128, D], F32) for i in range(2)]
          yv = [ysc_d[br].rearrange("(kc p) t -> p kc t", p=128) for br in range(2)]
          x1hT = T("x1halo")

          ysets = [(yA, yBb), (yA2, yB2)]

          def pre2a(k, nt, xsrc, ytok):
              NT = nt * 128
              for br in range(2):
                  yb_ = ysets[k][br]
                  P.dma(yb_[:, :, 0:NT], yv[br][:, :, ytok:ytok + NT], [yT], [yb_.t])
              for ti in range(nt):
                  xt = xts2[k][ti]
                  P.dma(xt[:], xsrc(ti), [], [xt.t])
                  norm_T((nst, xn, junk), xt, hT2s[k], ti * 128, G1[:, :], SH1)

          def blk2a(k, nt, x1dst, dstT, nxt):
              NT = nt * 128
              hT = hT2s[k]
              xts = xts2[k]
              yA_, yB_ = ysets[k]
              for mt in range(16):
                  pb = nb()
                  for kc in range(8):
                      P.mm(pb[:, 0:NT], Wg[:, kc, mt * 128:(mt + 1) * 128], hT[:, kc, 0:NT], [Wg.t, hT.t], [pb.t],
                           start=(kc == 0), stop=(kc == 7), acc=(kc > 0))
                  P.act(gT[:, mt, 0:NT], pb[:, 0:NT], AF.Sigmoid, [pb.t, bgT.t], [gT.t], bias=bgT[:, mt:mt + 1])
              if nxt is not None:
                  nxt()
              for mt in range(8):
                  pa = nb()
                  for kc in range(4):
                      P.mm(pa[:, 0:NT], Woa[:, kc, mt * 128:(mt + 1) * 128], yA_[:, kc, 0:NT], [Woa.t, yA_.t], [pa.t],
                           start=(kc == 0), stop=(kc == 3), acc=(kc > 0))
                  pbb = nb()
                  for kc in range(4):
                      P.mm(pbb[:, 0:NT], Wob[:, kc, mt * 128:(mt + 1) * 128], yB_[:, kc, 0:NT], [Wob.t, yB_.t], [pbb.t],
                           start=(kc == 0), stop=(kc == 3), acc=(kc > 0))
                  P.tt(tA[:, 0:NT], pa[:, 0:NT], gT[:, mt, 0:NT], ALU.mult, [pa.t, gT.t], [tA.t])
                  P.tt(tB[:, 0:NT], pbb[:, 0:NT], gT[:, 8 + mt, 0:NT], ALU.mult, [pbb.t, gT.t], [tB.t])
                  P.tt(mT[:, mt, 0:NT], tA[:, 0:NT], tB[:, 0:NT], ALU.add, [tA.t, tB.t], [mT.t])
              for ti in range(nt):
                  x1 = x1s[ti % 2]
                  for half in range(2):
                      pb = nb()
                      for kc in range(8):
                          P.mm(pb[:, :], mT[:, kc, ti * 128:(ti + 1) * 128], Wo[:, kc, half * 512:(half + 1) * 512],
                               [mT.t, Wo.t], [pb.t], start=(kc == 0), stop=(kc == 7), acc=(kc > 0))
                      cs = slice(half * 512, (half + 1) * 512)
                      P.tt(x1[:, cs], pb[:, :], grow[:, cs], ALU.mult, [pb.t, grow.t], [x1.t])
                      P.tt(x1[:, cs], x1[:, cs], xts[ti][:, cs], ALU.add, [x1.t, xts[ti].t], [x1.t])
                  P.dma(x1dst(ti), x1[:], [x1.t], [dstT], eng="sp", sem_tile=x1.t)

          items = [(1, (lambda ti: xhalo_d), SH - 128, (lambda ti: x1h_d), x1hT)]
          for blk in range(SH // TB):
              t0 = blk * TB
              items.append((4, (lambda ti, t0=t0: xown_d[t0 + ti * 128:t0 + (ti + 1) * 128, :]), SH + t0,
                            (lambda ti, t0=t0: x1_d[t0 + ti * 128:t0 + (ti + 1) * 128, :]), x1T))
          pre2a(0, items[0][0], items[0][1], items[0][2])
          for i_, (nt_, xsrc_, ytok_, x1dst_, dstT_) in enumerate(items):
              nxt_ = None
              if i_ + 1 < len(items):
                  n_ = items[i_ + 1]
                  nxt_ = (lambda k=(i_ + 1) % 2, n_=n_: pre2a(k, n_[0], n_[1], n_[2]))
              blk2a(i_ % 2, nt_, x1dst_, dstT_, nxt_)

    P.barrier()
    TB2 = 256
    NB2 = SH // TB2
    outT = T("out")
    finals = []
    with ExitStack() as es:
      if "2b" in phases:
          Wup = sbuf(es, "Wup", [128, 8, 2 * DFF], BF16)
          Wdn = sbuf(es, "Wdn", [128, 21, D], BF16)
          convT = sbuf(es, "convTs", [128, 4, 42], F32)
          for kc in range(8):
              for hf in range(2):
                  P.dma(Wup[:, kc, hf * DFF:(hf + 1) * DFF],
                        wup_d.rearrange("(kc p) n -> p kc n", p=128)[:, kc, hf * DFF:(hf + 1) * DFF], [], [Wup.t], eng="pool")
          for kc in range(21):
              P.dma(Wdn[:, kc, :], wdn_d.rearrange("(kc p) n -> p kc n", p=128)[:, kc, :], [], [Wdn.t], eng="pool")
          P.dma(convT[:], convT_d, [], [convT.t])
          h2Ts = [sbuf(es, f"h2T{i}", [128, 8, TB2], BF16) for i in range(2)]
          x1ts = [[sbuf(es, f"x1t{k}_{i}", [128, D], F32) for i in range(2)] for k in range(2)]
          h2T = h2Ts[1]
          x1t = x1ts[1]
          nst = sbuf(es, "nst3", [128, 4], F32)
          xn = sbuf(es, "xn3", [128, D], BF16)
          junk = sbuf(es, "junk3", [128, D], BF16)
          halo = sbuf(es, "halo", [128, 42, 2], F32)
          halo_t = [T(f"halo{m}") for m in range(42)]
          NRU = 4
          ups = [sbuf(es, f"ups{i}", [128, TB2 + 2], F32) for i in range(NRU)]
          c1 = [sbuf(es, f"c1_{i}", [128, TB2], F32) for i in range(NRU)]
          vals = sbuf(es, "vals", [128, 21, TB2], BF16)
          vals_t = [T(f"vals{m}") for m in range(21)]
          actT = sbuf(es, "actT", [128, 21, TB2], BF16)
          sgt = [sbuf(es, f"sgt{i}", [128, TB2], F32) for i in range(2)]
          ost = [sbuf(es, f"ost{i}", [128, D], F32) for i in range(2)]
          P.dma(x1t[0][:], x1h_d, [x1hT], [x1t[0].t])
          norm_T((nst, xn, junk), x1t[0], h2T, 0, G2[:, :], SH2)
          ph = nb()
          for mt in range(42):
              for kc in range(8):
                  P.mm(ph[:, 2 * mt:2 * mt + 2], Wup[:, kc, mt * 128:(mt + 1) * 128], h2T[:, kc, 126:128], [Wup.t, h2T.t], [ph.t],
                       start=(kc == 0), stop=(kc == 7), acc=(mt > 0 or kc > 0))
          P.ts(halo[:].rearrange("p m c -> p (m c)"), ph[:, 0:84], fl[:, 1:2], None, ALU.mult, ALU.bypass,
               [ph.t, fl.t], halo_t)
          def pre2b(blk):
              for ti in range(2):
                  xt = x1ts[blk % 2][ti]
                  P.dma(xt[:], x1_d[blk * TB2 + ti * 128:blk * TB2 + (ti + 1) * 128, :], [x1T], [xt.t])
                  norm_T((nst, xn, junk), xt, h2Ts[blk % 2], ti * 128, G2[:, :], SH2)

          pre2b(0)
          for blk in range(NB2):
              t0 = blk * TB2
              h2T = h2Ts[blk % 2]
              x1t = x1ts[blk % 2]
              for mt in range(42):
                  if mt == 21 and blk + 1 < NB2:
                      pre2b(blk + 1)
                  pb = nb()
                  for kc in range(8):
                      P.mm(pb[:, 0:TB2], Wup[:, kc, mt * 128:(mt + 1) * 128], h2T[:, kc, :], [Wup.t, h2T.t], [pb.t],
                           start=(kc == 0), stop=(kc == 7), acc=(kc > 0))
                  u = ups[mt % NRU]
                  cc = c1[mt % NRU]
                  P.act(u[:, 2:TB2 + 2], pb[:, 0:TB2], AF.Copy, [pb.t], [u.t])
                  P.act(cc[:], pb[:, 0:TB2], AF.Identity, [pb.t, convT.t], [cc.t],
                        scale=convT[:, 2, mt:mt + 1], bias=convT[:, 3, mt:mt + 1])
                  P.cp(u[:, 0:2], halo[:, mt, :], [halo_t[mt]], [u.t], eng="pool")
                  P.cp(halo[:, mt, :], u[:, TB2:TB2 + 2], [u.t], [halo_t[mt]], eng="pool")
                  P.stt(cc[:], u[:, 1:TB2 + 1], convT[:, 1, mt:mt + 1], cc[:], ALU.mult, ALU.add, [u.t, convT.t, cc.t], [cc.t])
                  if mt < 21:
                      P.stt(vals[:, mt, :], u[:, 0:TB2], convT[:, 0, mt:mt + 1], cc[:], ALU.mult, ALU.add,
                            [u.t, convT.t, cc.t], [vals_t[mt]])
                  else:
                      sg_ = sgt[mt % 2]
                      P.stt(cc[:], u[:, 0:TB2], convT[:, 0, mt:mt + 1], cc[:], ALU.mult, ALU.add, [u.t, convT.t, cc.t], [cc.t])
                      P.act(sg_[:], cc[:], AF.Silu, [cc.t], [sg_.t])
                      P.tt(actT[:, mt - 21, :], sg_[:], vals[:, mt - 21, :], ALU.mult, [sg_.t, vals_t[mt - 21]], [actT.t])
              for ti in range(2):
                  o_ = ost[ti]
                  for half in range(2):
                      pb = nb()
                      for kc in range(21):
                          P.mm(pb[:, :], actT[:, kc, ti * 128:(ti + 1) * 128], Wdn[:, kc, half * 512:(half + 1) * 512],
                               [actT.t, Wdn.t], [pb.t], start=(kc == 0), stop=(kc == 20), acc=(kc > 0))
                      cs = slice(half * 512, (half + 1) * 512)
                      P.tt(o_[:, cs], pb[:, :], grow[:, D + half * 512:D + (half + 1) * 512], ALU.mult, [pb.t, grow.t], [o_.t])
                      P.tt(o_[:, cs], o_[:, cs], x1t[ti][:, cs], ALU.add, [o_.t, x1t[ti].t], [o_.t])
                  P.dma(out_d[t0 + ti * 128:t0 + (ti + 1) * 128, :], o_[:], [o_.t], [outT], eng="sp", sem_tile=o_.t)
                  finals.append(o_.t)
    P.emit("sp", list({id(t): t for t in P.dma_tiles}.values()))
    top.close()
    return nc


def _col(v):
    v = np.asarray(v, np.float32)
    return np.ascontiguousarray(v.reshape(-1, 128).T)


def prep_inputs(S, x_b, c_b, p, own=0):
    w_in = p["w_in"]
    RW = 512
    o_sb = 1824
    w1 = []
    mu = []
    colp = []
    w2h, a2h, g2h = [], [], []
    for hh in range(2):
        hs = slice(hh * 256, (hh + 1) * 256)
        cols = np.concatenate([np.arange(0, 512)[hs], 512 + np.arange(512)[hs], 1024 + np.arange(512)[hs],
                               np.arange(1536, 1824),
                               o_sb + np.arange(512)[hs], o_sb + 512 + np.arange(512)[hs], o_sb + 1024 + np.arange(512)[hs]])
        w1.append(w_in[:, cols])
        mu.append(p["rwkv_mu"][cols[:NLERP]])
        cp_ = np.zeros((128, 16), np.float32)
        for g in range(2):
            cs = slice(hh * 256 + g * 128, hh * 256 + (g + 1) * 128)
            for j, nm in enumerate(("rwkv_w0", "rwkv_a0", "rwkv_k_k", "rwkv_k_a", "rwkv_r_k", "rwkv_lnx_g", "rwkv_lnx_b")):
                cp_[:, 7 * g + j] = np.asarray(p[nm]).reshape(-1)[cs]
        cp_[:, 14] = np.tile(p["sb_q_g"], 2)
        cp_[:, 15] = np.tile(p["sb_k_g"], 2)
        colp.append(cp_)
        w2h.append(p["rwkv_w2"][:, hs])
        a2h.append(p["rwkv_a2"][:, hs])
        g2h.append(p["rwkv_g2"][:, hs])
    convT = np.stack([_col(p["conv_w"][0]), _col(p["conv_w"][1]), _col(p["conv_w"][2]), _col(p["conv_b"])], axis=1)
    f = lambda a: np.ascontiguousarray(np.asarray(a, np.float32))
    SHh = S // 2
    x_seq = x_b if own == 1 else np.concatenate([x_b[:SHh], x_b[:SHh]], axis=0)
    return {
        "x": f(x_seq), "x_own": f(x_seq[SHh:]), "x_halo": f(x_seq[SHh - 128:SHh]),
        "flags": f(np.tile(np.array([[1.0 - own, float(own)]], np.float32), (128, 1))), "cT": _col(c_b), "w_ada": f(p["w_ada"]), "badaT": _col(p["b_ada"]),
        "bada_g": f(np.stack([p["b_ada"][2 * D:3 * D], p["b_ada"][5 * D:6 * D]])),
        "n1g": _col(p["norm1_g"]), "n2g": _col(p["norm2_g"]),
        "w1": f(np.stack(w1)), "mu": f(np.stack(mu)), "colp": f(np.stack(colp)),
        "w2h": f(np.stack(w2h)), "a2h": f(np.stack(a2h)), "g2h": f(np.stack(g2h)),
        "wg": f(w_in[:, 3360:]), "bgT": _col(p["b_gate"]), "woa": f(p["w_o_rwkv"]), "wob": f(p["w_o_sb"]),
        "wout": f(p["w_out"]), "wup": f(p["w_up"]), "convT": f(convT), "wdn": f(p["w_down"]),
    }


_NC_CACHE = {}


def kernel(**inputs):
    x = np.asarray(inputs["x"], np.float32)
    Bn, S, _ = x.shape
    p = {k: np.asarray(v)[0] for k, v in inputs.items() if k not in ("x", "c")}
    c = np.asarray(inputs["c"], np.float32)
    if S not in _NC_CACHE:
        _NC_CACHE[S] = build(S)
    nc = _NC_CACHE[S]
    ncores = 8
    maps = []
    for core in range(ncores):
        b = (core // 2) % Bn
        maps.append(prep_inputs(S, x[b], c[b], p, core % 2))
    res = run_bass_kernel_spmd(nc, maps, core_ids=list(range(ncores)))
    out = np.stack([np.concatenate([res.results[2 * b]["out"], res.results[2 * b + 1]["out"]], axis=0)
                    for b in range(Bn)], axis=0)
    return out.astype(np.float32)
```
